# Optimizing a Trainium2 kernel written in Bass

```python
import jax, jax.numpy as jnp
from jax import lax
import numpy as np

D_MODEL = 1024
BATCH = 4
SEQ = 8192
DEPTH = 1
DEC_BATCH = 128
DEC_SEQ = 4
PAST_LEN = 16384
PAGE_SIZE = 128

MIX_WIDTH = D_MODEL
LRU_WIDTH = MIX_WIDTH // 2
LRU_BLOCKS = 8
LRU_BLOCK_W = LRU_WIDTH // LRU_BLOCKS
LRU_C = 8.0
CONV_W = 4
HEAD_DIM = 64
N_HEADS = (MIX_WIDTH - LRU_WIDTH) // HEAD_DIM
N_KV_HEADS = 2
GROUP = N_HEADS // N_KV_HEADS
ATTN_WIDTH = N_HEADS * HEAD_DIM
KV_WIDTH = N_KV_HEADS * HEAD_DIM
WINDOW = 128
ATTN_BLOCK = WINDOW
ROPE_THETA = 10000.0
D_FF = 2816
NORM_EPS = 1e-6
IN_WIDTH = 2 * LRU_WIDTH + ATTN_WIDTH + 2 * KV_WIDTH
SPLITS = [LRU_WIDTH, 2 * LRU_WIDTH, 2 * LRU_WIDTH + ATTN_WIDTH, 2 * LRU_WIDTH + ATTN_WIDTH + KV_WIDTH]

kernel_name = 'hymba_rglru_swa_sink_macaron_step'


def rms_norm(x, g):
    xf = x.astype(jnp.float32)
    y = xf * lax.rsqrt(jnp.mean(xf * xf, axis=-1, keepdims=True) + NORM_EPS)
    return (y * g.astype(jnp.float32)).astype(x.dtype)


def swiglu_ffn(x, w_gate, w_up, w_down):
    return (jax.nn.silu(x @ w_gate) * (x @ w_up)) @ w_down


def rope(x, pos):
    half = HEAD_DIM // 2
    inv_freq = ROPE_THETA ** (-jnp.arange(half, dtype=jnp.float32) / half)
    ang = pos.astype(jnp.float32)[:, None] * inv_freq[None, :]
    cos = jnp.cos(ang)[:, None, :]
    sin = jnp.sin(ang)[:, None, :]
    xf = x.astype(jnp.float32)
    x1, x2 = xf[..., :half], xf[..., half:]
    return jnp.concatenate([x1 * cos - x2 * sin, x2 * cos + x1 * sin], axis=-1).astype(x.dtype)


def causal_conv(x, buf, w, b):
    T = x.shape[1]
    xc = jnp.concatenate([buf.astype(x.dtype), x], axis=1)
    y = b
    for j in range(CONV_W):
        y = y + xc[:, j:j + T] * w[j]
    return y, xc[:, xc.shape[1] - (CONV_W - 1):]


def _lin_combine(left, right):
    a1, b1 = left
    a2, b2 = right
    return a1 * a2, a2 * b1 + b2


def rg_lru(x, h0, wa, ba, wx, bx, lam):
    B, T, C = x.shape
    xb = x.reshape(B, T, LRU_BLOCKS, LRU_BLOCK_W)
    r = jax.nn.sigmoid(jnp.einsum('btnc,ncd->btnd', xb, wa).reshape(B, T, C) + ba).astype(jnp.float32)
    i = jax.nn.sigmoid(jnp.einsum('btnc,ncd->btnd', xb, wx).reshape(B, T, C) + bx).astype(jnp.float32)
    log_a = -LRU_C * jax.nn.softplus(-lam.astype(jnp.float32)) * r
    a = jnp.exp(log_a)
    u = jnp.sqrt(-jnp.expm1(2.0 * log_a)) * (i * x.astype(jnp.float32))
    u = u.at[:, 0].add(a[:, 0] * h0.astype(jnp.float32))
    _, h = lax.associative_scan(_lin_combine, (a, u), axis=1)
    return h, h[:, -1]


def sink_attention(q, k, v, mask, sinks):
    s = jnp.einsum('...qkgd,...jkd->...kgqj', q, k, preferred_element_type=jnp.float32) * (HEAD_DIM ** -0.5)
    s = jnp.where(mask[..., None, None, :, :], s, -jnp.inf)
    sk = sinks.astype(jnp.float32).reshape(N_KV_HEADS, GROUP, 1, 1)
    mx = jnp.maximum(jnp.max(s, axis=-1, keepdims=True), sk)
    p = jnp.exp(s - mx)
    p = p / (jnp.sum(p, axis=-1, keepdims=True) + jnp.exp(sk - mx))
    o = jnp.einsum('...kgqj,...jkd->...qkgd', p, v.astype(jnp.float32))
    return o.astype(q.dtype)


def swa_banded(q, k, v, sinks):
    B, S = q.shape[0], q.shape[1]
    nb = S // ATTN_BLOCK
    qb = q.reshape(B, nb, ATTN_BLOCK, N_KV_HEADS, GROUP, HEAD_DIM)

    def band(t):
        cur = t.reshape(B, nb, ATTN_BLOCK, N_KV_HEADS, HEAD_DIM)
        prev = jnp.concatenate([jnp.zeros_like(cur[:, :1]), cur[:, :-1]], axis=1)
        return jnp.concatenate([prev, cur], axis=2)

    kb, vb = band(k), band(v)
    qi = jnp.arange(ATTN_BLOCK)[:, None]
    kj = jnp.arange(2 * ATTN_BLOCK)[None, :]
    d = qi + ATTN_BLOCK - kj
    blk = jnp.arange(nb)[:, None, None]
    mask = (d >= 0) & (d < WINDOW) & ((blk > 0) | (kj >= ATTN_BLOCK))
    o = sink_attention(qb, kb, vb, mask, sinks)
    return o.reshape(B, S, N_HEADS, HEAD_DIM)


def swa_cached(q, k, v, k_cache, v_cache, sinks):
    B, T = q.shape[0], q.shape[1]
    W = k_cache.shape[1]
    kk = jnp.concatenate([k_cache.astype(k.dtype), k], axis=1)
    vv = jnp.concatenate([v_cache.astype(v.dtype), v], axis=1)
    d = (jnp.arange(T)[:, None] + W) - jnp.arange(W + T)[None, :]
    mask = (d >= 0) & (d < WINDOW)
    qg = q.reshape(B, T, N_KV_HEADS, GROUP, HEAD_DIM)
    o = sink_attention(qg, kk, vv, mask, sinks)
    return o.reshape(B, T, N_HEADS, HEAD_DIM), kk[:, T:], vv[:, T:]


def token_mixer(hn, pos, h0, conv_buf, k_cache, v_cache, win, mp):
    (w_in, conv_w, conv_b, wa, ba, wx, bx, lam, qn, kn, sinks, lru_on, attn_on, w_out) = mp
    B, T, _ = hn.shape
    proj = hn @ w_in
    x_lru, gate, q, k, v = jnp.split(proj, SPLITS, axis=-1)
    xc, new_conv = causal_conv(x_lru, conv_buf, conv_w, conv_b)
    h, h_last = rg_lru(xc, h0, wa, ba, wx, bx, lam)
    lru_out = rms_norm((h * jax.nn.gelu(gate.astype(jnp.float32))).astype(hn.dtype), lru_on)
    q = rope(rms_norm(q.reshape(B, T, N_HEADS, HEAD_DIM), qn), pos)
    k = rope(rms_norm(k.reshape(B, T, N_KV_HEADS, HEAD_DIM), kn), pos)
    v = v.reshape(B, T, N_KV_HEADS, HEAD_DIM)
    if k_cache is None:
        o = swa_banded(q, k, v, sinks)
        new_k, new_v = k[:, T - win:], v[:, T - win:]
    else:
        o, new_k, new_v = swa_cached(q, k, v, k_cache, v_cache, sinks)
    attn_out = rms_norm(o.reshape(B, T, ATTN_WIDTH), attn_on)
    y = jnp.concatenate([lru_out, attn_out], axis=-1) @ w_out
    return y, h_last.astype(hn.dtype), new_conv, new_k, new_v


def decoder_layer(x, pos, h0, conv_buf, k_cache, v_cache, win, ffn1, mix, ffn2):
    x = x + 0.5 * swiglu_ffn(rms_norm(x, ffn1[0]), ffn1[1], ffn1[2], ffn1[3])
    y, h_last, new_conv, new_k, new_v = token_mixer(rms_norm(x, mix[0]), pos, h0, conv_buf,
                                                    k_cache, v_cache, win, mix[1:])
    x = x + y
    x = x + 0.5 * swiglu_ffn(rms_norm(x, ffn2[0]), ffn2[1], ffn2[2], ffn2[3])
    return x, h_last, new_conv, new_k, new_v


def setup_inputs(seed: int = 0) -> dict:
    key = jax.random.key(seed)
    ks = jax.random.split(key, 32)
    f32 = jnp.float32
    L = DEPTH
    win = min(WINDOW, PAST_LEN)

    def nrm(k, shape, s):
        return jax.random.normal(k, shape, f32) * s

    def gain(k, shape):
        return 1.0 + 0.01 * jax.random.normal(k, shape, f32)

    u = jax.random.uniform(ks[18], (L, LRU_WIDTH), f32, 0.9, 0.999)
    sg = u ** (1.0 / LRU_C)
    lam = jnp.log(sg) - jnp.log1p(-sg)
    return {
        'x_prompt': nrm(ks[0], (BATCH, SEQ, D_MODEL), 1.0),
        'x_sample': nrm(ks[1], (DEC_BATCH, DEC_SEQ, D_MODEL), 1.0),
        'state_lru_h': nrm(ks[2], (L, DEC_BATCH, LRU_WIDTH), 0.5),
        'state_conv': nrm(ks[3], (L, DEC_BATCH, CONV_W - 1, LRU_WIDTH), 1.0),
        'cache_k': nrm(ks[4], (L, DEC_BATCH, win, N_KV_HEADS, HEAD_DIM), 1.0),
        'cache_v': nrm(ks[5], (L, DEC_BATCH, win, N_KV_HEADS, HEAD_DIM), 1.0),
        'ffn1_norm': gain(ks[6], (L, D_MODEL)),
        'ffn1_w_gate': nrm(ks[7], (L, D_MODEL, D_FF), D_MODEL ** -0.5),
        'ffn1_w_up': nrm(ks[8], (L, D_MODEL, D_FF), D_MODEL ** -0.5),
        'ffn1_w_down': nrm(ks[9], (L, D_FF, D_MODEL), D_FF ** -0.5),
        'mix_norm': gain(ks[10], (L, D_MODEL)),
        'w_in': nrm(ks[11], (L, D_MODEL, IN_WIDTH), D_MODEL ** -0.5),
        'conv_w': nrm(ks[12], (L, CONV_W, LRU_WIDTH), CONV_W ** -0.5),
        'conv_b': nrm(ks[13], (L, LRU_WIDTH), 0.01),
        'lru_wa': nrm(ks[14], (L, LRU_BLOCKS, LRU_BLOCK_W, LRU_BLOCK_W), LRU_BLOCK_W ** -0.5),
        'lru_ba': nrm(ks[15], (L, LRU_WIDTH), 0.01),
        'lru_wx': nrm(ks[16], (L, LRU_BLOCKS, LRU_BLOCK_W, LRU_BLOCK_W), LRU_BLOCK_W ** -0.5),
        'lru_bx': nrm(ks[17], (L, LRU_WIDTH), 0.01),
        'lru_lambda': lam,
        'q_norm': gain(ks[19], (L, HEAD_DIM)),
        'k_norm': gain(ks[20], (L, HEAD_DIM)),
        'attn_sinks': nrm(ks[21], (L, N_HEADS), 0.5),
        'lru_out_norm': gain(ks[22], (L, LRU_WIDTH)),
        'attn_out_norm': gain(ks[23], (L, ATTN_WIDTH)),
        'w_out': nrm(ks[24], (L, MIX_WIDTH, D_MODEL), MIX_WIDTH ** -0.5),
        'ffn2_norm': gain(ks[25], (L, D_MODEL)),
        'ffn2_w_gate': nrm(ks[26], (L, D_MODEL, D_FF), D_MODEL ** -0.5),
        'ffn2_w_up': nrm(ks[27], (L, D_MODEL, D_FF), D_MODEL ** -0.5),
        'ffn2_w_down': nrm(ks[28], (L, D_FF, D_MODEL), D_FF ** -0.5),
    }


def reference(x_prompt, x_sample, state_lru_h, state_conv, cache_k, cache_v,
              ffn1_norm, ffn1_w_gate, ffn1_w_up, ffn1_w_down,
              mix_norm, w_in, conv_w, conv_b, lru_wa, lru_ba, lru_wx, lru_bx, lru_lambda,
              q_norm, k_norm, attn_sinks, lru_out_norm, attn_out_norm, w_out,
              ffn2_norm, ffn2_w_gate, ffn2_w_up, ffn2_w_down):
    win = min(WINDOW, PAST_LEN)
    pos_p = jnp.arange(SEQ, dtype=jnp.int32)
    pos_s = PAST_LEN + jnp.arange(DEC_SEQ, dtype=jnp.int32)
    yp, ys = x_prompt, x_sample
    p_h, p_c, p_k, p_v = [], [], [], []
    s_h, s_c, s_k, s_v = [], [], [], []
    for l in range(DEPTH):
        ffn1 = (ffn1_norm[l], ffn1_w_gate[l], ffn1_w_up[l], ffn1_w_down[l])
        mix = (mix_norm[l], w_in[l], conv_w[l], conv_b[l], lru_wa[l], lru_ba[l], lru_wx[l], lru_bx[l],
               lru_lambda[l], q_norm[l], k_norm[l], attn_sinks[l], lru_out_norm[l], attn_out_norm[l], w_out[l])
        ffn2 = (ffn2_norm[l], ffn2_w_gate[l], ffn2_w_up[l], ffn2_w_down[l])
        h0 = jnp.zeros((BATCH, LRU_WIDTH), x_prompt.dtype)
        c0 = jnp.zeros((BATCH, CONV_W - 1, LRU_WIDTH), x_prompt.dtype)
        yp, h, c, k, v = decoder_layer(yp, pos_p, h0, c0, None, None, win, ffn1, mix, ffn2)
        p_h.append(h); p_c.append(c); p_k.append(k); p_v.append(v)
        ys, h, c, k, v = decoder_layer(ys, pos_s, state_lru_h[l], state_conv[l], cache_k[l], cache_v[l],
                                       win, ffn1, mix, ffn2)
        s_h.append(h); s_c.append(c); s_k.append(k); s_v.append(v)
    prompt_lru_h, prompt_conv = jnp.stack(p_h), jnp.stack(p_c)
    prompt_k, prompt_v = jnp.stack(p_k), jnp.stack(p_v)
    sample_lru_h, sample_conv = jnp.stack(s_h), jnp.stack(s_c)
    sample_k, sample_v = jnp.stack(s_k), jnp.stack(s_v)
    return (yp, ys, prompt_lru_h, prompt_conv, prompt_k, prompt_v, sample_lru_h, sample_conv, sample_k, sample_v)
```

```python
import numpy as np
from contextlib import ExitStack
import concourse.bass as bass
import concourse.mybir as mybir
from concourse.bass_utils import run_bass_kernel_spmd

F32 = mybir.dt.float32
BF16 = mybir.dt.bfloat16
AF = mybir.ActivationFunctionType
ALU = mybir.AluOpType

D = 1024
DFF = 2816
NJ = DFF // 128
LRU_W = 512
HD = 64
NH = 8
WIN = 128
EPS = 1e-6
PAST_LEN = 16384
THETA = 10000.0
NCORES = 8
NSEQ = 16
TS = 4


class T:
    __slots__ = ("t", "name", "w", "r", "dsem", "dcnt")

    def __init__(self, t=None, name=""):
        self.t = t
        self.name = name
        self.w = None
        self.r = {}
        self.dsem = None
        self.dcnt = 0

    def __getitem__(self, k):
        return self.t[k]


class Rec:
    def __init__(self):
        self.calls = []

    def __getattr__(self, name):
        def f(*a, **k):
            self.calls.append((name, a, k))
            return self
        return f


def _record(fn):
    r = Rec()
    fn(r)
    assert r.calls
    return r.calls


class Sched:
    ENG = ("pe", "act", "dve", "pool", "sp")

    def __init__(self, nc, es):
        self.nc = nc
        self.es = es
        self.lists = {e: [] for e in self.ENG}
        self.sem = {}
        self.cnt = {e: 0 for e in self.ENG}
        self.seen = {e: {} for e in self.ENG}
        for e in ("pe", "act", "dve", "pool"):
            self.sem[e] = es.enter_context(nc.semaphore("s_" + e))
        import os
        self.same_eng_sync = os.environ.get("KSYNC", "1") == "1"
        self.nsem = 4
        self.sb_bytes = 0

    def sb(self, name, shape, dt):
        t = self.es.enter_context(self.nc.sbuf_tensor("sb_" + name, list(shape), dt))
        n = 1
        for s in shape[1:]:
            n *= s
        self.sb_bytes += n * (2 if dt == BF16 else 4)
        return T(t, name)

    def ps(self, name):
        t = self.es.enter_context(self.nc.psum_tensor("ps_" + name, [128, 512], F32))
        return T(t, name)

    def dram(self, name, shape, dt):
        t = self.nc.dram_tensor(name, list(shape), dt)
        return T(t.ap(), name)

    def _dsem(self, b):
        if b.dsem is None:
            b.dsem = self.es.enter_context(self.nc.semaphore("d%d" % self.nsem))
            self.nsem += 1
        return b.dsem

    def _deps(self, reads, writes):
        deps = {}

        def add(tk):
            if tk is None:
                return
            k = id(tk[0])
            if k not in deps or deps[k][1] < tk[1]:
                deps[k] = tk
        for b in reads:
            add(b.w)
        for b in writes:
            add(b.w)
            for tk in b.r.values():
                add(tk)
        return deps

    def _commit(self, reads, writes, tk):
        k = id(tk[0])
        ws = set(id(b) for b in writes)
        for b in reads:
            if id(b) in ws:
                continue
            b.r[k] = tk
        for b in writes:
            b.w = tk
            b.r = {}

    def op(self, eng, reads, writes, fn):
        deps = self._deps(reads, writes)
        waits = []
        own = id(self.sem[eng])
        for k, (s, v) in deps.items():
            if k == own and (eng == "pe" or not self.same_eng_sync):
                continue
            if self.seen[eng].get(k, 0) >= v:
                continue
            self.seen[eng][k] = v
            waits.append((s, v))
        self.cnt[eng] += 1
        tk = (self.sem[eng], self.cnt[eng])
        self.lists[eng].append((waits, _record(fn), tk))
        self._commit(reads, writes, tk)
        return tk

    def dma(self, reads, writes, fns, semb, q="sp"):
        sem = self._dsem(semb)
        deps = self._deps(reads, writes)
        if semb.dcnt > 0:
            k = id(sem)
            if k not in deps or deps[k][1] < semb.dcnt:
                deps[k] = (sem, semb.dcnt)
        waits = []
        for k, (s, v) in deps.items():
            if self.seen[q].get(k, 0) >= v:
                continue
            self.seen[q][k] = v
            waits.append((s, v))
        semb.dcnt += 16 * len(fns)
        tk = (sem, semb.dcnt)
        self.lists[q].append((waits, [_record(f) for f in fns], ("dma", sem)))
        self._commit(reads, writes, tk)
        return tk

    def collective(self, reads, writes, fn):
        eng = "pool"
        sem = self.es.enter_context(self.nc.semaphore("ccsem%d" % self.nsem))
        self.nsem += 1
        deps = self._deps(reads, writes)
        waits = []
        for k, (s, v) in deps.items():
            if self.seen[eng].get(k, 0) >= v:
                continue
            self.seen[eng][k] = v
            waits.append((s, v))
        tk = (sem, 1)
        self.lists[eng].append((waits, _record(fn), ("cc", sem)))
        self._commit(reads, writes, tk)
        return tk

    def final_wait(self, q, bufs):
        deps = self._deps(bufs, [])
        self.lists[q].append((list(deps.values()), None, None))

    def emit(self):
        nc = self.nc
        lists = self.lists

        def run(e, items):
            for waits, fn, tk in items:
                for s, v in waits:
                    e.wait_ge(s, v)
                if fn is None:
                    continue
                if tk[0] == "cc":
                    for (nm, a, k) in fn:
                        ins = getattr(e, nm)(*a, **k)
                    ins.then_inc(tk[1])
                elif tk[0] == "dma":
                    for calls in fn:
                        for (nm, a, k) in calls:
                            ins = getattr(e, nm)(*a, **k)
                        ins.then_inc(tk[1], 16)
                else:
                    for (nm, a, k) in fn:
                        ins = getattr(e, nm)(*a, **k)
                    ins.then_inc(tk[0], 1)

        with nc.Block() as block:
            @block.tensor
            def _(e):
                run(e, lists["pe"])

            @block.scalar
            def _(e):
                run(e, lists["act"])

            @block.vector
            def _(e):
                run(e, lists["dve"])

            @block.gpsimd
            def _(e):
                run(e, lists["pool"])

            @block.sync
            def _(e):
                run(e, lists["sp"])


class Ring:
    def __init__(self, items):
        self.items = items
        self.i = 0

    def next(self):
        x = self.items[self.i % len(self.items)]
        self.i += 1
        return x


class Builder:
    def __init__(self, NT, wseq=None):
        self.NT = NT
        self.wseq_in = wseq
        self.nc = bass.Bass("TRN2", target_bir_lowering=False)

    def din(self, name, shape, dt=F32):
        return T(self.nc.dram_tensor(name, list(shape), dt, kind="ExternalInput").ap(), name)

    def dout(self, name, shape, dt=F32):
        t = T(self.nc.dram_tensor(name, list(shape), dt, kind="ExternalOutput").ap(), name)
        self.outs.append(t)
        return t

    def build(self):
        nc = self.nc
        NT = self.NT
        NTOK = NT * 512
        self.outs = []
        with ExitStack() as es:
            S = self.S = Sched(nc, es)
            I = self.I = {}
            I["xp"] = self.din("xp", [NTOK, D])
            I["xh"] = self.din("xh", [128, D])
            I["flag"] = self.din("flag", [1])
            I["rope_h"] = self.din("rope_h", [128, 2, 128])
            I["xs"] = self.din("xs", [NSEQ * TS, D])
            I["st_h"] = self.din("st_h", [NSEQ, LRU_W])
            I["st_conv"] = self.din("st_conv", [NSEQ * 3, LRU_W])
            I["ck"] = self.din("ck", [NSEQ, WIN, 128])
            I["cv"] = self.din("cv", [NSEQ, WIN, 128])
            for f in (1, 2):
                I["f%d_g" % f] = self.din("f%d_g" % f, [D, DFF])
                I["f%d_u" % f] = self.din("f%d_u" % f, [D, DFF])
                I["f%d_d" % f] = self.din("f%d_d" % f, [DFF, D])
            I["w_in"] = self.din("w_in", [D, 1792])
            I["w_out"] = self.din("w_out", [D, D])
            I["gains"] = self.din("gains", [3, D])
            I["chv"] = self.din("chv", [10, LRU_W])
            I["qkn"] = self.din("qkn", [2, 128])
            I["sinks"] = self.din("sinks", [NH])
            I["wa"] = self.din("wa", [8, 64, 64])
            I["wx"] = self.din("wx", [8, 64, 64])
            I["cst"] = self.din("cst", [128, 4 * 128])
            I["mask_p"] = self.din("mask_p", [128, 2, 128])
            I["mask_s"] = self.din("mask_s", [128, TS + 64])
            I["rope_p"] = self.din("rope_p", [128, 2, NTOK])
            I["rope_s"] = self.din("rope_s", [128, 2, 64])
            O = self.O = {}
            O["yp"] = self.dout("yp", [NTOK, D])
            O["ys"] = self.dout("ys", [NSEQ * TS, D])
            O["p_h"] = self.dout("p_h", [4, 128])
            O["p_conv"] = self.dout("p_conv", [3, LRU_W])
            O["p_k"] = self.dout("p_k", [WIN, 128])
            O["p_v"] = self.dout("p_v", [WIN, 128])
            O["s_h"] = self.dout("s_h", [NSEQ, LRU_W])
            O["s_conv"] = self.dout("s_conv", [NSEQ * 3, LRU_W])
            O["s_k"] = self.dout("s_k", [NSEQ, WIN, 128])
            O["s_v"] = self.dout("s_v", [NSEQ, WIN, 128])
            self.dbg_out = {}
            self.alloc()
            import os
            self.stop = float(os.environ.get("KSTOP", "99"))
            try:
                self.prologue()
                self.ck(0)
                self.prompt_all()
            except StopIteration:
                pass
            S.final_wait("sp", self.outs)
            S.emit()
        return nc

    def ck(self, n):
        if self.stop <= n:
            raise StopIteration

    def alloc(self):
        S = self.S
        self.cst = S.sb("cst", [128, 4, 128], F32)
        self.identb = S.sb("identb", [128, 128], BF16)
        self.maskp = S.sb("maskp", [128, 2, 4, 128], BF16)
        self.maskc = S.sb("maskc", [128, TS], BF16)
        self.maskn = S.sb("maskn", [64, 8, 64], BF16)
        self.gT = S.sb("gT", [128, 3, 8], F32)
        self.chv = S.sb("chv", [128, 4, 16], F32)
        self.gaT = S.sb("gaT", [128, 4], F32)
        self.qkn = S.sb("qkn", [128, 2], F32)
        self.esink = S.sb("esink", [128, NH], F32)
        self.wab = S.sb("wab", [128, 4, 2, 128], BF16)
        self.rope = [S.sb("rope%d" % i, [128, 2, 512], F32) for i in range(1)]
        self.rope_s = S.sb("rope_s", [128, 2, 64], F32)
        self.NWB = 4
        self.cstage = Ring([S.sb("cstage%d" % i, [128, 2048], F32) for i in range(2)])
        self.ceng = Ring(["act"])
        self.spill_sem = Ring([T(None, "spill%d" % i) for i in range(4)])
        self.wbuf = [S.sb("wbuf%d" % i, [128, 4096], BF16) for i in range(self.NWB)]
        self.xt_t = [S.sb("xt%d" % i, [128, 4, D], F32) for i in range(2)]
        self.XT = [[T(self.xt_t[i].t, "xt%d_%d" % (i, b)) for b in range(4)] for i in range(2)]
        self.xnb = Ring([S.sb("xnb%d" % i, [128, D], BF16) for i in range(3)])
        self.junk = S.sb("junk", [128, D], BF16)
        self.xnT_t = S.sb("xnT", [128, 8, 512], BF16)
        self.XNT = [T(self.xnT_t.t, "xnT%d" % b) for b in range(4)]
        self.aT_t = S.sb("aT", [128, NJ, 512], BF16)
        self.AT = [T(self.aT_t.t, "aT%d" % j) for j in range(NJ)]
        self.sg = Ring([S.sb("sg%d" % i, [128, 512], F32) for i in range(2)])
        self.small = Ring([S.sb("small%d" % i, [128, 16], F32) for i in range(24)])
        self.small_f = Ring([S.sb("smallf%d" % i, [128, 16], F32) for i in range(12)])
        self.zerob = S.sb("zerob", [128, 512], BF16)
        self.flagB = S.sb("flagB", [128, 1], F32)
        self.maskp0 = S.sb("maskp0", [128, 4, 128], BF16)
        self.hmid = S.sb("hmid", [128, 4], F32)
        self.ac = S.sb("ac", [128, 4], F32)
        self.AC = [T(self.ac.t, "ac%d" % c) for c in range(4)]
        self.tmp = Ring([S.sb("tmp%d" % i, [128, 512], F32) for i in range(10)])
        self.tmpb = Ring([S.sb("tmpb%d" % i, [128, 512], BF16) for i in range(2)])
        self.xl = Ring([S.sb("xl%d" % i, [128, 515], F32) for i in range(2)])
        self.carry = [S.sb("carry%d" % c, [128, 3], F32) for c in range(4)]
        self.hc = S.sb("hc", [128, 4], F32)
        self.HC = [T(self.hc.t, "hc%d" % c) for c in range(4)]
        self.hg = [S.sb("hg%d" % c, [128, 512], F32) for c in range(4)]
        self.qT4 = S.sb("qT4", [128, 4, 4, 128], BF16)
        self.qsA = S.sb("qsA", [128, NSEQ, 4, TS], BF16)
        self.qsB = S.sb("qsB", [128, 4, NSEQ * TS], BF16)
        self.QT = [T(self.qT4.t, "qT%d" % c) for c in range(4)]
        self.kT = S.sb("kT", [128, 640], BF16)
        self.kr = S.sb("kr", [128, 512], F32)
        self.vext = S.sb("vext", [128, 5, 2, 65], BF16)
        self.vf = S.sb("vf", [128, 128], F32)
        self.ep = Ring([S.sb("ep%d" % i, [128, 512], BF16) for i in range(4)])
        self.o = Ring([S.sb("o%d" % i, [128, 512], F32) for i in range(1)])
        self.on = Ring([S.sb("on%d" % i, [128, 512], BF16) for i in range(1)])
        self.onT_t = S.sb("onT", [128, 4, 512], BF16)
        self.ONT = [T(self.onT_t.t, "onT%d" % b) for b in range(4)]
        self.lnT = S.sb("lnT", [128, 4, 512], BF16)
        self.kTc = S.sb("kTc", [128, NSEQ, 128], BF16)
        self.vc = S.sb("vc", [128, NSEQ, 2, 65], BF16)
        self.stcT = S.sb("stcT", [128, 4, NSEQ, 3], F32)
        self.h0T = S.sb("h0T", [128, 4, NSEQ], F32)
        self.hst = S.sb("hst", [128, 64], F32)
        self.psf = Ring([S.ps("psf%d" % i) for i in range(4)])
        self.psr = Ring([S.ps("psr%d" % i) for i in range(2)])
        self.psA = S.ps("psA")
        self.psB = S.ps("psB")
        self.cxF = dict(xnT=self.xnT_t.t, XNT=self.XNT, small=self.small_f, ps=self.psf)
        self.cxM = None
        self.wseq = []
        self.wloaded = 0
        self.wslot = {}
        print('SBUF bytes/partition', S.sb_bytes)


    def prologue(self):
        S = self.S
        I = self.I
        nc = self.nc
        cst = self.cst

        def ld(dst, src_ap, dst_ap=None, q="sp"):
            S.dma([], [dst], [lambda e: e.dma_start(out=dst_ap if dst_ap is not None else dst.t[:], in_=src_ap)], dst, q=q)

        ld(cst, I["cst"].t.rearrange("p (a b) -> p a b", a=4))
        mpf = self.tmp.next()
        msf = self.tmp.next()
        self.maskp_f = T(mpf.t, "maskp_f")
        self.masks_f = T(msf.t, "masks_f")
        mpv = mpf.t[:, 0:256].rearrange("p (a b) -> p a b", a=2)
        ld(mpf, I["mask_p"].t[:, :, :], dst_ap=mpv)
        ld(msf, I["mask_s"].t[:, :], dst_ap=msf.t[:, 0:TS + 64])
        ld(self.rope_s, I["rope_s"].t[:, :, :])
        self.ident = cst.t[:, 0, :]
        self.blk1 = cst.t[:, 1, :]
        self.rrot = cst.t[:, 2, :]
        self.ones = cst.t[:, 3, :]
        S.op("dve", [cst], [self.identb], lambda e: e.tensor_copy(out=self.identb[:, :], in_=self.ident))

        def mk(e):
            for i in range(2):
                for h in range(4):
                    ins = e.tensor_copy(out=self.maskp[:, i, h, :], in_=mpv[:, i, :])
            return ins
        S.op("dve", [mpf], [self.maskp], mk)
        S.op("dve", [msf], [self.maskc], lambda e: e.tensor_copy(out=self.maskc[:, :], in_=msf[:, 0:TS]))
        S.op("dve", [msf], [self.maskn], lambda e: e.tensor_copy(
            out=self.maskn[:, :, :], in_=msf[0:64, TS:TS + 64].unsqueeze(1).broadcast_to([64, 8, 64])))
        sk = self.small.next()
        S.dma([], [sk], [lambda e: e.dma_start(out=sk[:, 0:NH], in_=I["sinks"].t.partition_broadcast(128))], sk)
        S.op("act", [sk], [self.esink], lambda e: e.activation(out=self.esink[:, :], in_=sk[:, 0:NH], func=AF.Exp))
        stg = self.tmp.next()
        g_st = self.tmp.next()
        g_st2 = self.tmp.next()
        S.dma([], [g_st], [lambda e: e.dma_start(out=g_st[0:3, :], in_=I["gains"].t[:, 0:512])], g_st)
        S.dma([], [g_st2], [lambda e: e.dma_start(out=g_st2[0:3, :], in_=I["gains"].t[:, 512:1024])], g_st2)
        ps = self.psr.next()

        def trg(e):
            for kc in range(8):
                src = (g_st if kc < 4 else g_st2)
                ins = e.transpose(out=ps[:, kc * 4:kc * 4 + 3], in_=src[0:3, (kc % 4) * 128:(kc % 4 + 1) * 128], identity=cst.t[0:3, 0, 0:3])
            return ins
        S.op("pe", [g_st, g_st2, cst], [ps], trg)
        S.op("dve", [ps], [self.gT], lambda e: e.tensor_copy(
            out=self.gT[:, :, :], in_=ps[:, 0:32].rearrange("p (k w) -> p w k", w=4)[:, 0:3, :]))
        S.dma([], [stg], [lambda e: e.dma_start(out=stg[0:10, :], in_=I["chv"].t[:, :])], stg)
        ps2 = self.psr.next()

        def trc(e):
            for c in range(4):
                ins = e.transpose(out=ps2[:, c * 16:c * 16 + 10], in_=stg[0:10, c * 128:(c + 1) * 128], identity=cst.t[0:10, 0, 0:10])
            return ins
        S.op("pe", [stg, cst], [ps2], trc)
        chv = self.chv
        S.op("dve", [ps2], [chv], lambda e: e.tensor_copy(out=chv[:, :, 0:10], in_=ps2[:, 0:64].rearrange("p (c i) -> p c i", i=16)[:, :, 0:10]))
        S.op("act", [chv], [chv], lambda e: e.activation(out=chv[:, :, 11], in_=chv[:, :, 7], func=AF.Exp, scale=-1.0))
        S.op("act", [chv], [chv], lambda e: e.activation(out=chv[:, :, 10], in_=chv[:, :, 11], func=AF.Ln, bias=1.0))
        S.op("dve", [chv], [chv], lambda e: e.tensor_scalar(out=chv[:, :, 10], in0=chv[:, :, 10], scalar1=-8.0, scalar2=None, op0=ALU.mult))
        S.op("dve", [chv], [self.gaT], lambda e: e.tensor_copy(out=self.gaT[:, :], in_=chv[:, :, 9]))
        st3 = self.tmp.next()
        S.dma([], [st3], [lambda e: e.dma_start(out=st3[0:2, 0:128], in_=I["qkn"].t[:, :])], st3)
        ps3 = self.psr.next()
        S.op("pe", [st3, cst], [ps3], lambda e: e.transpose(out=ps3[:, 0:2], in_=st3[0:2, 0:128], identity=cst.t[0:2, 0, 0:2]))
        S.op("dve", [ps3], [self.qkn], lambda e: e.tensor_copy(out=self.qkn[:, :], in_=ps3[:, 0:2]))
        wst = self.tmp.next()
        for which, nm in ((0, "wa"), (1, "wx")):
            wst = self.tmp.next()
            S.op("pool", [], [wst], lambda e, wst=wst: e.memset(wst[:, :], 0.0))
            fns = []
            for c in range(4):
                for half in range(2):
                    n = 2 * c + half
                    fns.append(lambda e, c=c, half=half, n=n, nm=nm, wst=wst: e.dma_start(
                        out=wst[half * 64:(half + 1) * 64, c * 128 + half * 64:c * 128 + half * 64 + 64], in_=I[nm].t[n, :, :]))
            S.dma([], [wst], fns, wst)
            S.op("dve", [wst], [self.wab], lambda e, wst=wst, which=which: e.tensor_copy(
                out=self.wab[:, :, which, :], in_=wst[:, :].rearrange("p (c m) -> p c m", c=4)))
        self.pieces = {}

        def piece(name, shape, halves):
            d = S.dram("bf_" + name, shape, BF16)
            self.pieces[name] = (d, shape, halves)

        self.dgroups = [(0, 8), (8, 16), (16, 22)]
        for f in (1, 2):
            for pi in range(NJ // 2):
                j0 = pi * 2
                halves = []
                for gu, nm in ((0, "g"), (1, "u")):
                    src = I["f%d_%s" % (f, nm)].t[:, j0 * 128:(j0 + 2) * 128].rearrange("(kc p) m -> p kc m", p=128)
                    halves.append((src, gu * 2048, 2048, (8, 256)))
                piece("f%d_gu%d" % (f, pi), [128, 2, 8, 256], halves)
            for n in range(2):
                for gi, (j0, j1) in enumerate(self.dgroups):
                    jn = j1 - j0
                    hs = []
                    ja = j0
                    for cnt in (jn // 2, jn - jn // 2):
                        src = I["f%d_d" % f].t[ja * 128:(ja + cnt) * 128, n * 512:(n + 1) * 512].rearrange("(j p) n -> p j n", p=128)
                        hs.append((src, (ja - j0) * 512, cnt * 512, (cnt, 512)))
                        ja += cnt
                    piece("f%d_d%d_%d" % (f, n, gi), [128, jn, 512], hs)
        for pi, (c0, c1) in enumerate(((0, 512), (512, 1024), (1024, 1536), (1536, 1792))):
            w = c1 - c0
            hs = []
            for kc0 in (0, 4):
                src = I["w_in"].t[kc0 * 128:(kc0 + 4) * 128, c0:c1].rearrange("(kc p) n -> p kc n", p=128)
                hs.append((src, kc0 * w, 4 * w, (4, w)))
            piece("w_in%d" % pi, [128, 8, w], hs)
        for n in range(2):
            hs = []
            for kc0 in (0, 4):
                src = I["w_out"].t[kc0 * 128:(kc0 + 4) * 128, n * 512:(n + 1) * 512].rearrange("(kc p) n -> p kc n", p=128)
                hs.append((src, kc0 * 512, 4 * 512, (4, 512)))
            piece("w_out%d" % n, [128, 8, 512], hs)
        seq = []
        for f in (1, 2):
            part = ["f%d_gu%d" % (f, pi) for pi in range(NJ // 2)]
            part += ["f%d_d%d_%d" % (f, n, gi) for n in range(2) for gi in range(3)]
            if f == 1:
                part += ["w_in0", "w_in1", "w_in2", "w_in3", "w_out0", "w_out1"]
            seq += part
        self.tile_seq = seq
        self.record = self.wseq_in is None
        self.wseq = [] if self.record else list(self.wseq_in)
        self.converted = set()
        self.cpos = 0
        self.CONV_AHEAD = 6
        self.wrec = []
        self.wpos = 0
        self.wheld = set()
        self.wfree = list(range(self.NWB))
        self.wslot_of = {}
        self.wissued = 0

    def wget(self, name):
        if self.record:
            self.wrec.append(name)
            if self.wpos >= len(self.wseq):
                self.wseq.append(name)
        assert self.wseq[self.wpos] == name, (self.wseq[self.wpos], name)
        idx = self.wpos
        self.wpos += 1
        self.wheld.add(idx)
        self._wprefetch()
        assert self.wissued > idx
        slot = self.wbuf[self.wslot_of[idx]]
        d, shape, _ = self.pieces[name]
        n = 1
        for s in shape[1:]:
            n *= s
        v = slot.t[:, 0:n]
        if len(shape) == 4:
            v = v.rearrange("p (a b c) -> p a b c", a=shape[1], b=shape[2])
        else:
            v = v.rearrange("p (a b) -> p a b", a=shape[1])
        return slot, v, idx

    def wdone(self, idx):
        self.wheld.discard(idx)
        self.wfree.append(self.wslot_of.pop(idx))
        self._wprefetch()

    def _convert_upto(self, last):
        S = self.S
        while self.cpos <= min(last, len(self.wseq) - 1):
            nm = self.wseq[self.cpos]
            self.cpos += 1
            if nm in self.converted:
                continue
            self.converted.add(nm)
            d, shape, halves = self.pieces[nm]
            dflat = d.t.rearrange("p a b c -> p (a b c)") if len(shape) == 4 else d.t.rearrange("p a b -> p (a b)")
            fns = []
            for (src, off, cnt, (a_, b_)) in halves:
                if nm == "w_in2":
                    for k4 in range(4):
                        for two in range(2):
                            dv = dflat[:, off + k4 * 512:off + (k4 + 1) * 512].rearrange("p (c two d) -> p c two d", c=4, two=2)[:, :, two, :]
                            sv = src[:, k4, :].rearrange("p (two c d) -> p two c d", two=2, c=4)[:, two, :, :]
                            fns.append(lambda e, dv=dv, sv=sv: e.dma_start(out=dv, in_=sv))
                else:
                    dv = dflat[:, off:off + cnt].rearrange("p (a b) -> p a b", a=a_)
                    fns.append(lambda e, dv=dv, src=src: e.dma_start(out=dv, in_=src))
            S.dma([], [d], fns, self.spill_sem.next(), q="pool")

    def _convert_one_early(self):
        p = self.cpos
        while p < len(self.wseq) and self.wseq[p] in self.converted:
            p += 1
        if p < len(self.wseq):
            save = self.cpos
            self.cpos = p
            self._convert_upto(p)
            self.cpos = save

    def _wprefetch(self):
        S = self.S
        while self.wissued < len(self.wseq) and self.wfree:
            i = self.wissued
            nm = self.wseq[i]
            d, shape, halves = self.pieces[nm]
            si = self.wfree.pop(0)
            self.wslot_of[i] = si
            slot = self.wbuf[si]
            n = 1
            for s_ in shape[1:]:
                n *= s_
            dflat = d.t.rearrange("p a b c -> p (a b c)") if len(shape) == 4 else d.t.rearrange("p a b -> p (a b)")
            self._convert_upto(i + self.CONV_AHEAD)
            if i % 2 == 0 and not self.record:
                self._convert_one_early()
            S.dma([d], [slot], [lambda e, slot=slot, dflat=dflat, n=n: e.dma_start(out=slot[:, 0:n], in_=dflat)], slot)
            self.wissued += 1

    def rstd_tm(self, x_T, x_ap, P, n, small=None):
        S = self.S
        small = small or self.small
        ss = small.next()
        junk = self.junk
        S.op("act", [x_T], [ss], lambda e: e.activation(out=junk[0:P, 0:n], in_=x_ap, func=AF.Square, accum_out=ss[0:P, 0:1]))
        sr = small.next()
        S.op("act", [ss], [sr], lambda e: e.activation(out=sr[0:P, 0:1], in_=ss[0:P, 0:1], func=AF.Sqrt, scale=1.0 / n, bias=EPS))
        rs = small.next()
        S.op("dve", [sr], [rs], lambda e: e.reciprocal(out=rs[0:P, 0:1], in_=sr[0:P, 0:1]))
        return rs

    def norm_T(self, XB, xaps, P, which, cx=None):
        S = self.S
        cx = cx or self.cxF
        xnT_t = cx["xnT"]
        for b, (xT_, xap) in enumerate(zip(XB, xaps)):
            rs = self.rstd_tm(xT_, xap, P, D, cx["small"])
            xnb = self.xnb.next()
            S.op("act", [xT_, rs], [xnb], lambda e, xnb=xnb, xap=xap, rs=rs: e.activation(
                out=xnb[0:P, :], in_=xap, func=AF.Copy, scale=rs[0:P, 0:1]))
            ps = cx["ps"].next()
            psb = ps.t[:, :].bitcast(BF16)

            def tr(e, xnb=xnb, psb=psb):
                for k in range(8):
                    ins = e.transpose(out=psb[:, k * 128:k * 128 + P], in_=xnb[0:P, k * 128:(k + 1) * 128], identity=self.identb[0:P, 0:P])
                return ins
            S.op("pe", [xnb, self.identb], [ps], tr)
            gt = self.gT.t[:, which, :]
            S.op("dve", [ps, self.gT], [cx["XNT"][b]], lambda e, psb=psb, b=b, gt=gt: e.tensor_tensor(
                out=xnT_t[:, :, b * P:(b + 1) * P],
                in0=psb.rearrange("p (k t) -> p k t", k=8)[:, :, 0:P],
                in1=gt.unsqueeze(2).broadcast_to([128, 8, P]), op=ALU.mult))

    def ffn(self, f, XB, xaps, P, which):
        for _ in self.g_ffn(f, XB, xaps, P, which):
            pass

    def g_ffn(self, f, XB, xaps, P, which, fine=False):
        S = self.S
        cx = self.cxF
        xnT_t = cx["xnT"]
        nb = len(XB)
        NTK = nb * P
        self.norm_T(XB, xaps, P, which, cx)
        yield
        for pi in range(NJ // 2):
            slot, w, widx = self.wget("f%d_gu%d" % (f, pi))
            banks = [(self.psf.next(), self.psf.next()) for jj in range(2)]
            sgs = [None, None]
            for gu in range(2):
                for jj in range(2):
                    ps = banks[jj][gu]

                    def mm(e, w=w, jj=jj, ps=ps, gu=gu):
                        for kc in range(8):
                            ins = e.matmul(ps[:, 0:NTK], lhsT=w[:, gu, kc, jj * 128:(jj + 1) * 128], rhs=xnT_t[:, kc, 0:NTK],
                                           start=(kc == 0), stop=(kc == 7))
                        return ins
                    S.op("pe", [slot] + cx["XNT"][0:nb], [ps], mm)
                    if fine:
                        yield
                for jj in range(2):
                    j = pi * 2 + jj
                    psG, psU = banks[jj]
                    if gu == 0:
                        sgs[jj] = self.sg.next()
                        if fine:
                            S.op("act", [psG], [sgs[jj]], lambda e, sg=sgs[jj], psG=psG: e.activation(out=sg[:, 0:NTK], in_=psG[:, 0:NTK], func=AF.Tanh, scale=0.5))
                            S.op("dve", [sgs[jj], psG], [sgs[jj]], lambda e, sg=sgs[jj], psG=psG: e.scalar_tensor_tensor(
                                out=sg[:, 0:NTK], in0=sg[:, 0:NTK], scalar=1.0, in1=psG[:, 0:NTK], op0=ALU.add, op1=ALU.mult))
                        else:
                            S.op("act", [psG], [sgs[jj]], lambda e, sg=sgs[jj], psG=psG: e.activation(out=sg[:, 0:NTK], in_=psG[:, 0:NTK], func=AF.Silu))
                    elif fine:
                        S.op("dve", [sgs[jj], psU], [self.AT[j]], lambda e, sg=sgs[jj], psU=psU, j=j: e.scalar_tensor_tensor(
                            out=self.aT_t[:, j, 0:NTK], in0=sg[:, 0:NTK], scalar=0.5, in1=psU[:, 0:NTK], op0=ALU.mult, op1=ALU.mult))
                    else:
                        S.op("dve", [sgs[jj], psU], [self.AT[j]], lambda e, sg=sgs[jj], psU=psU, j=j: e.tensor_tensor(
                            out=self.aT_t[:, j, 0:NTK], in0=sg[:, 0:NTK], in1=psU[:, 0:NTK], op=ALU.mult))
            self.wdone(widx)
            if not fine:
                yield
        for n in range(2):
            pss = [self.psf.next() for b in range(nb)]
            for gi, (j0, j1) in enumerate(self.dgroups):
                slot, w, widx = self.wget("f%d_d%d_%d" % (f, n, gi))
                for b in range(nb):
                    def mm(e, w=w, b=b, ps=pss[b], j0=j0, j1=j1):
                        for j in range(j0, j1):
                            ins = e.matmul(ps[0:P, :], lhsT=self.aT_t[:, j, b * P:(b + 1) * P], rhs=w[:, j - j0, :],
                                           start=(j == 0), stop=(j == NJ - 1))
                        return ins
                    S.op("pe", [slot] + self.AT[j0:j1], [pss[b]], mm)
                    if fine:
                        yield
                self.wdone(widx)
                if not fine:
                    yield
            for b in range(nb):
                xap = xaps[b][:, n * 512:(n + 1) * 512]
                S.op("dve", [pss[b], XB[b]], [XB[b]], lambda e, ps=pss[b], xap=xap: e.scalar_tensor_tensor(
                    out=xap, in0=ps[0:P, :], scalar=0.5, in1=xap, op0=ALU.mult, op1=ALU.add))
            yield

    def qk_chunk(self, ps, N, gain_idx, rope_T, cos, sin, outs, out_f32=None):
        S = self.S
        qf = self.tmp.next()
        S.op("act", [ps], [qf], lambda e: e.activation(out=qf[:, 0:N], in_=ps[:, 0:N], func=AF.Copy))
        sq = self.tmp.next()
        S.op("act", [ps], [sq], lambda e: e.activation(out=sq[:, 0:N], in_=ps[:, 0:N], func=AF.Square))
        pss = self.psr.next()
        S.op("pe", [sq, self.cst], [pss], lambda e: e.matmul(pss[:, 0:N], lhsT=self.blk1, rhs=sq[:, 0:N], start=True, stop=True))
        sr = self.tmp.next()
        S.op("act", [pss], [sr], lambda e: e.activation(out=sr[:, 0:N], in_=pss[:, 0:N], func=AF.Sqrt, scale=1.0 / HD, bias=EPS))
        rs = self.tmp.next()
        S.op("dve", [sr], [rs], lambda e: e.reciprocal(out=rs[:, 0:N], in_=sr[:, 0:N]))
        qn = self.tmp.next()
        S.op("dve", [qf, rs, self.qkn], [qn], lambda e: e.scalar_tensor_tensor(
            out=qn[:, 0:N], in0=qf[:, 0:N], scalar=self.qkn[:, gain_idx:gain_idx + 1], in1=rs[:, 0:N], op0=ALU.mult, op1=ALU.mult))
        psr_ = self.psr.next()
        S.op("pe", [qn, self.cst], [psr_], lambda e: e.matmul(psr_[:, 0:N], lhsT=self.rrot, rhs=qn[:, 0:N], start=True, stop=True))
        t1 = self.tmp.next()
        S.op("pool", [qn, rope_T], [t1], lambda e: e.tensor_tensor(out=t1[:, 0:N], in0=qn[:, 0:N], in1=cos, op=ALU.mult))
        t2 = self.tmp.next()
        S.op("dve", [psr_, rope_T], [t2], lambda e: e.tensor_tensor(out=t2[:, 0:N], in0=psr_[:, 0:N], in1=sin, op=ALU.mult))
        if out_f32 is not None:
            fT, fap = out_f32
            S.op("dve", [t1, t2], [fT], lambda e: e.tensor_tensor(out=fap, in0=t1[:, 0:N], in1=t2[:, 0:N], op=ALU.add))
        for (oT, oap, view) in outs:
            S.op("dve", [t1, t2], [oT], lambda e, oap=oap, view=view: e.tensor_tensor(out=oap, in0=view(t1[:, 0:N]), in1=view(t2[:, 0:N]), op=ALU.add))

    def lru_gates(self, c, xc_ap, N, shape3=None):
        S = self.S
        chv = self.chv
        xcb = self.tmpb.next()
        xc_T = self._xcT
        S.op("pool", [xc_T], [xcb], lambda e: e.tensor_copy(out=xcb[:, 0:N], in_=xc_ap))
        psa = self.psr.next()
        psx = self.psr.next()
        S.op("pe", [xcb, self.wab], [psa], lambda e: e.matmul(psa[:, 0:N], lhsT=self.wab[:, c, 0, :], rhs=xcb[:, 0:N], start=True, stop=True))
        S.op("pe", [xcb, self.wab], [psx], lambda e: e.matmul(psx[:, 0:N], lhsT=self.wab[:, c, 1, :], rhs=xcb[:, 0:N], start=True, stop=True))
        r = self.tmp.next()
        S.op("act", [psa, chv], [r], lambda e: e.activation(out=r[:, 0:N], in_=psa[:, 0:N], func=AF.Sigmoid, bias=chv[:, c, 5:6]))
        ig = self.tmp.next()
        S.op("act", [psx, chv], [ig], lambda e: e.activation(out=ig[:, 0:N], in_=psx[:, 0:N], func=AF.Sigmoid, bias=chv[:, c, 6:7]))
        a = self.tmp.next()
        S.op("act", [r, chv], [a], lambda e: e.activation(out=a[:, 0:N], in_=r[:, 0:N], func=AF.Exp, scale=chv[:, c, 10:11]))
        a2 = self.tmp.next()
        S.op("act", [a], [a2], lambda e: e.activation(out=a2[:, 0:N], in_=a[:, 0:N], func=AF.Square))
        m = self.tmp.next()
        S.op("act", [a2], [m], lambda e: e.activation(out=m[:, 0:N], in_=a2[:, 0:N], func=AF.Sqrt, scale=-1.0, bias=1.0))
        t = self.tmp.next()
        S.op("dve", [ig, xc_T], [t], lambda e: e.tensor_tensor(out=t[:, 0:N], in0=ig[:, 0:N], in1=xc_ap, op=ALU.mult))
        u = self.tmp.next()
        S.op("dve", [t, m], [u], lambda e: e.tensor_tensor(out=u[:, 0:N], in0=t[:, 0:N], in1=m[:, 0:N], op=ALU.mult))
        return a, u

    def conv4(self, c, xl_T, xl_view, out_T, out_ap):
        S = self.S
        chv = self.chv
        S.op("dve", [xl_T, chv], [out_T], lambda e: e.tensor_scalar(
            out=out_ap, in0=xl_view(0), scalar1=chv[:, c, 0:1], scalar2=chv[:, c, 4:5], op0=ALU.mult, op1=ALU.add))
        for j in range(1, 4):
            S.op("dve", [xl_T, chv, out_T], [out_T], lambda e, j=j: e.scalar_tensor_tensor(
                out=out_ap, in0=xl_view(j), scalar=chv[:, c, j:j + 1], in1=out_ap, op0=ALU.mult, op1=ALU.add))

    def lru_out_norm(self, N):
        S = self.S
        psn = self.psB
        for c in range(4):
            sq = self.tmp.next()
            S.op("act", [self.hg[c]], [sq], lambda e, sq=sq, c=c: e.activation(out=sq[:, 0:N], in_=self.hg[c][:, 0:N], func=AF.Square))
            S.op("pe", [sq, self.cst], [psn], lambda e, sq=sq, c=c: e.matmul(psn[:, 0:N], lhsT=self.ones, rhs=sq[:, 0:N], start=(c == 0), stop=(c == 3)))
        sr = self.tmp.next()
        S.op("act", [psn], [sr], lambda e: e.activation(out=sr[:, 0:N], in_=psn[:, 0:N], func=AF.Sqrt, scale=1.0 / LRU_W, bias=EPS))
        rs = self.tmp.next()
        S.op("dve", [sr], [rs], lambda e: e.reciprocal(out=rs[:, 0:N], in_=sr[:, 0:N]))
        for c in range(4):
            S.op("dve", [self.hg[c], rs, self.chv], [self.lnT], lambda e, c=c: e.scalar_tensor_tensor(
                out=self.lnT[:, c, 0:N], in0=self.hg[c][:, 0:N], scalar=self.chv[:, c, 8:9], in1=rs[:, 0:N], op0=ALU.mult, op1=ALU.mult))

    def attn_finish(self, b, P, psO):
        return self.run(self.g_attn_finish(b, P, psO))

    def g_attn_finish(self, b, P, psO):
        S = self.S
        o = self.o.next()
        den = self.small.next()
        rden = self.small.next()
        for g in range(2):
            pv = psO[g][0:P, 0:260].rearrange("p (h d) -> p h d", h=4)
            S.op("dve", [psO[g], self.esink], [den], lambda e, g=g, pv=pv: e.tensor_tensor(
                out=den[0:P, g * 4:(g + 1) * 4], in0=pv[:, :, 64], in1=self.esink[0:P, g * 4:(g + 1) * 4], op=ALU.add))
        S.op("dve", [den], [rden], lambda e: e.reciprocal(out=rden[0:P, 0:8], in_=den[0:P, 0:8]))
        for g in range(2):
            pv = psO[g][0:P, 0:260].rearrange("p (h d) -> p h d", h=4)
            S.op("dve", [psO[g], rden], [o], lambda e, g=g, pv=pv: e.tensor_tensor(
                out=o[0:P, g * 256:(g + 1) * 256].rearrange("p (h d) -> p h d", h=4), in0=pv[:, :, 0:64],
                in1=rden[0:P, g * 4:(g + 1) * 4].unsqueeze(2).broadcast_to([P, 4, 64]), op=ALU.mult))
        yield
        rs = self.rstd_tm(o, o[0:P, :], P, 512)
        on = self.on.next()
        S.op("act", [o, rs], [on], lambda e: e.activation(out=on[0:P, :], in_=o[0:P, :], func=AF.Copy, scale=rs[0:P, 0:1]))
        ps = self.psr.next()
        psb = ps.t[:, :].bitcast(BF16)

        def tr(e):
            for k in range(4):
                ins = e.transpose(out=psb[:, k * 128:k * 128 + P], in_=on[0:P, k * 128:(k + 1) * 128], identity=self.identb[0:P, 0:P])
            return ins
        S.op("pe", [on, self.identb], [ps], tr)
        yield
        S.op("dve", [ps, self.gaT], [self.ONT[b]], lambda e: e.tensor_tensor(
            out=self.onT_t[:, :, b * P:(b + 1) * P], in0=psb[:, 0:512].rearrange("p (k t) -> p k t", k=4)[:, :, 0:P],
            in1=self.gaT[:, :].unsqueeze(2).broadcast_to([128, 4, P]), op=ALU.mult))

    def out_proj(self, XB, xaps, P, nb):
        for _ in self.g_out_proj(XB, xaps, P, nb):
            pass

    def handoff_back(self):
        c0, c1 = self.cstage.items
        for c, als in ((c0, self.ag), (c1, self.cxM["XNT"])):
            for al in als:
                for tk in [al.w] + list(al.r.values()):
                    if tk is None:
                        continue
                    k = id(tk[0])
                    if k not in c.r or c.r[k][1] < tk[1]:
                        c.r[k] = tk

    def handoff(self):
        c0, c1 = self.cstage.items

        def inherit(name, src):
            t_ = T(src.t, name)
            t_.w = src.w
            t_.r = dict(src.r)
            return t_
        self.ag = [inherit("ag%d" % c, c0) for c in range(4)]
        self.ag_ap = [c0.t[:, c * 512:(c + 1) * 512] for c in range(4)]
        xnTm = c1.t[:, :].bitcast(BF16).rearrange("p (k t) -> p k t", k=8)
        self.cxM = dict(xnT=xnTm, XNT=[inherit("xnTm%d" % b, c1) for b in range(4)], small=self.small, ps=self.psr)

    def run(self, g):
        try:
            while True:
                next(g)
        except StopIteration as e:
            return e.value

    def interleave(self, ga, gb, ra=1, rb=1):
        da = db = False
        na = nb = 0
        while not (da and db):
            pick_a = (not da) and (db or na * rb <= nb * ra)
            if pick_a:
                try:
                    next(ga)
                    na += 1
                except StopIteration:
                    da = True
            else:
                try:
                    next(gb)
                    nb += 1
                except StopIteration:
                    db = True
        self.unit_counts = (na, nb)

    def prompt_all(self):
        S = self.S
        NT = self.NT
        sx = S.dram("sp_x1", [NT, 128, 4, D], F32)
        sh = S.dram("sp_hg", [NT, 128, 4, 512], F32)
        sa = S.dram("sp_ag", [NT, 128, 4, 512], F32)
        so = S.dram("sp_on", [NT, 128, 4, 512], BF16)
        self.SPX = [T(sx.t[t], "spx%d" % t) for t in range(NT)]
        self.SPH = [T(sh.t[t], "sph%d" % t) for t in range(NT)]
        self.SPA = [T(sa.t[t], "spa%d" % t) for t in range(NT)]
        self.SPO = [T(so.t[t], "spo%d" % t) for t in range(NT)]
        self.ccsrc = S.dram("cc_src", [4, 128], F32)
        self.ccdst = S.dram("cc_dst", [8, 128], F32)
        self.stsem = Ring([T(None, "stsem%d" % i) for i in range(4)])
        import os
        self.r1 = tuple(int(x) for x in os.environ.get("KR1", "4,3").split(","))
        self.r2 = tuple(int(x) for x in os.environ.get("KR2", "1,1").split(","))
        self.ysems = [T(None, "ysem0"), T(None, "ysem1")]
        self.handoff()
        self.sample_stage()
        self.run(self.g_halo())
        self.ck(6)
        self.load_x(0)
        self.run(self.g_ffn1(0))
        for t in range(NT):
            if t + 1 < NT:
                self.load_x(t + 1)
                self.interleave(self.g_mix1(t), self.g_ffn1(t + 1), self.r1[0], self.r1[1])
            else:
                self.run(self.g_mix1(t))
        self.ck(7)
        self.exchange_send()
        self.sample_part2()
        self.exchange_recv()
        self.ck(8)
        self.run(self.g_l2(0))
        for t in range(NT):
            if t + 1 < NT:
                self.interleave(self.g_ffn2(t), self.g_l2(t + 1), self.r2[0], self.r2[1])
            else:
                self.run(self.g_ffn2(t))
        self.prompt_state_out()

    def load_x(self, t):
        S = self.S
        I = self.I
        s = t % 2
        xt = self.xt_t[s]
        S.dma([I["xp"]], self.XT[s], [lambda e: e.dma_start(out=xt[:, :, :], in_=I["xp"].t[t * 512:(t + 1) * 512, :].rearrange("(b p) d -> p b d", p=128))], xt)

    def g_ffn1(self, t):
        s = t % 2
        xaps = [self.xt_t[s].t[:, b, :] for b in range(4)]
        yield from self.g_ffn(1, self.XT[s], xaps, 128, 0, fine=True)

    def g_ffn2(self, t):
        S = self.S
        O = self.O
        s = t % 2
        xt = self.xt_t[s]
        xaps = [xt.t[:, b, :] for b in range(4)]
        yield from self.g_ffn(2, self.XT[s], xaps, 128, 2)
        S.dma(self.XT[s], [O["yp"]], [lambda e: e.dma_start(out=O["yp"].t[t * 512:(t + 1) * 512, :].rearrange("(b p) d -> p b d", p=128), in_=xt[:, :, :])], self.ysems[s])

    def g_halo(self):
        S = self.S
        I = self.I
        cx = self.cxF
        xnT = cx["xnT"]
        XNT = cx["XNT"]
        xt = self.xt_t[1]
        XB = [self.XT[1][0]]
        xaps = [xt.t[:, 0, :]]
        S.dma([I["xh"]], XB, [lambda e: e.dma_start(out=xt[:, 0, :], in_=I["xh"].t[:, :])], xt)
        PS_ = NSEQ * TS
        XBs = [self.XT[1][1]]
        S.op("pool", [], XBs, lambda e: e.memset(xt[64:128, 1, :], 0.0))
        S.dma([I["xs"]], XBs, [lambda e: e.dma_start(out=xt[0:PS_, 1, :], in_=I["xs"].t[:, :])], T(None, "xsld"))
        rope = self.rope[0]
        S.dma([I["rope_h"]], [rope], [lambda e: e.dma_start(out=rope[:, :, 0:128], in_=I["rope_h"].t[:, :, :])], rope)
        S.dma([I["flag"]], [self.flagB], [lambda e: e.dma_start(out=self.flagB[:, :], in_=I["flag"].t.partition_broadcast(128))], self.flagB)
        S.op("dve", [self.maskp, self.flagB], [self.maskp0], lambda e: e.tensor_scalar(
            out=self.maskp0[:, :, :], in0=self.maskp[:, 0, :, :], scalar1=self.flagB[:, 0:1], scalar2=None, op0=ALU.mult))
        S.op("pool", [], [self.zerob], lambda e: e.memset(self.zerob[:, :], 0.0))
        self.run(self.g_ffn(1, XB + XBs, xaps + [xt.t[:, 1, :]], 128, 0))
        self.sample_part1(XBs, [xt.t[0:PS_, 1, :]])
        self.norm_T(XB, xaps, 128, 1, cx)
        slot0, w0, wi0 = self.wget("w_in0")
        for c in range(4):
            ps = self.psr.next()
            S.op("pe", [slot0, XNT[0]], [ps], lambda e, ps=ps, c=c: self._mm8(e, ps[:, 0:128], lambda kc: w0[:, kc, c * 128:(c + 1) * 128], lambda kc: xnT[:, kc, 0:128]))
            S.op("act", [ps], [self.carry[c]], lambda e, ps=ps, c=c: e.activation(out=self.carry[c][:, :], in_=ps[:, 125:128], func=AF.Copy))
        self.wdone(wi0)
        slot3, w3, wi3 = self.wget("w_in3")
        ps = self.psr.next()
        S.op("pe", [slot3, XNT[0]], [ps], lambda e, ps=ps: self._mm8(e, ps[:, 0:128], lambda kc: w3[:, kc, 0:128], lambda kc: xnT[:, kc, 0:128]))
        self.qk_chunk(ps, 128, 1, rope, rope.t[:, 0, 0:128], rope.t[:, 1, 0:128], [(self.kT, self.kT[:, 0:128], lambda ap: ap)])
        psv = self.psr.next()
        S.op("pe", [slot3, XNT[0]], [psv], lambda e: self._mm8(e, psv[:, 0:128], lambda kc: xnT[:, kc, 0:128], lambda kc: w3[:, kc, 128:256]))
        S.op("act", [psv], [self.vext], lambda e: e.activation(out=self.vext[:, 0, :, 0:64], in_=psv[:, 0:128].rearrange("p (g d) -> p g d", g=2), func=AF.Copy))
        self.wdone(wi3)
        S.op("pool", [], [self.vext], lambda e: e.memset(self.vext[:, :, :, 64:65], 1.0))
        yield

    def g_mix1(self, t):
        S = self.S
        I = self.I
        NT = self.NT
        cx = self.cxM
        xnT = cx["xnT"]
        XNT = cx["XNT"]
        s = t % 2
        XB = self.XT[s]
        xt = self.xt_t[s]
        xaps = [xt.t[:, b, :] for b in range(4)]
        last = (t == NT - 1)
        rope = self.rope[0]
        S.dma([I["rope_p"]], [rope], [lambda e: e.dma_start(out=rope[:, :, :], in_=I["rope_p"].t[:, :, t * 512:(t + 1) * 512])], rope)
        cos = rope.t[:, 0, :]
        sin = rope.t[:, 1, :]
        self.norm_T(XB, xaps, 128, 1, cx)
        yield
        slot0, w0, wi0 = self.wget("w_in0")
        slot1, w1, wi1 = self.wget("w_in1")
        chv = self.chv
        ring4 = Ring([self.psr.items[0], self.psr.items[1], self.psA, self.psB])
        st = {}

        def stageA(c):
            ps = ring4.next()
            S.op("pe", [slot0] + XNT, [ps], lambda e: self._mm8(e, ps[:, :], lambda kc: w0[:, kc, c * 128:(c + 1) * 128], lambda kc: xnT[:, kc, :]))
            xl = self.xl.next()
            S.op("pool", [self.carry[c]], [xl], lambda e: e.tensor_copy(out=xl[:, 0:3], in_=self.carry[c][:, :]))
            S.op("act", [ps], [xl], lambda e: e.activation(out=xl[:, 3:515], in_=ps[:, :], func=AF.Copy))
            S.op("pool", [xl], [self.carry[c]], lambda e: e.tensor_copy(out=self.carry[c][:, :], in_=xl[:, 512:515]))
            yield
            xc = self.tmp.next()
            self.conv4(c, xl, lambda j: xl[:, j:j + 512], xc, xc[:, :])
            yield
            xcb = self.tmpb.next()
            S.op("pool", [xc], [xcb], lambda e: e.tensor_copy(out=xcb[:, :], in_=xc[:, :]))
            psa = ring4.next()
            psx = ring4.next()
            S.op("pe", [xcb, self.wab], [psa], lambda e: e.matmul(psa[:, :], lhsT=self.wab[:, c, 0, :], rhs=xcb[:, :], start=True, stop=True))
            S.op("pe", [xcb, self.wab], [psx], lambda e: e.matmul(psx[:, :], lhsT=self.wab[:, c, 1, :], rhs=xcb[:, :], start=True, stop=True))
            yield
            r = self.tmp.next()
            S.op("act", [psa, chv], [r], lambda e: e.activation(out=r[:, :], in_=psa[:, :], func=AF.Sigmoid, bias=chv[:, c, 5:6]))
            ig = self.tmp.next()
            S.op("act", [psx, chv], [ig], lambda e: e.activation(out=ig[:, :], in_=psx[:, :], func=AF.Sigmoid, bias=chv[:, c, 6:7]))
            st[c] = dict(xc=xc, r=r, ig=ig)

        def stageB(c):
            xc, r, ig = st[c]["xc"], st[c]["r"], st[c]["ig"]
            S.op("act", [r, chv], [r], lambda e: e.activation(out=r[:, :], in_=r[:, :], func=AF.Exp, scale=chv[:, c, 10:11]))
            S.op("dve", [ig, xc], [ig], lambda e: e.tensor_tensor(out=ig[:, :], in0=ig[:, :], in1=xc[:, :], op=ALU.mult))
            yield
            S.op("act", [r], [xc], lambda e: e.activation(out=xc[:, :], in_=r[:, :], func=AF.Square))
            S.op("act", [xc], [xc], lambda e: e.activation(out=xc[:, :], in_=xc[:, :], func=AF.Sqrt, scale=-1.0, bias=1.0))
            S.op("dve", [ig, xc], [ig], lambda e: e.tensor_tensor(out=ig[:, :], in0=ig[:, :], in1=xc[:, :], op=ALU.mult))
            a, u = r, ig
            yield
            h = self.tmp.next()
            init = 0.0 if t == 0 else self.hc[:, c:c + 1]
            S.op("dve", [a, u] + ([] if t == 0 else [self.HC[c]]), [h], lambda e: e.tensor_tensor_scan(
                out=h[:, :], data0=a[:, :], data1=u[:, :], initial=init, op0=ALU.mult, op1=ALU.add))
            S.op("pool", [h], [self.HC[c]], lambda e: e.tensor_copy(out=self.hc[:, c:c + 1], in_=h[:, 511:512]))
            yield
            A = self.tmp.next()
            inita = 1.0 if t == 0 else self.ac[:, c:c + 1]
            S.op("dve", [a, self.zerob] + ([] if t == 0 else [self.AC[c]]), [A], lambda e: e.tensor_tensor_scan(
                out=A[:, :], data0=a[:, :], data1=self.zerob[:, :], initial=inita, op0=ALU.mult, op1=ALU.add))
            S.op("pool", [A], [self.AC[c]], lambda e: e.tensor_copy(out=self.ac[:, c:c + 1], in_=A[:, 511:512]))
            st[c].update(h=h, A=A)

        def stageC(c):
            h, A = st[c]["h"], st[c]["A"]
            psg = ring4.next()
            S.op("pe", [slot1] + XNT, [psg], lambda e: self._mm8(e, psg[:, :], lambda kc: w1[:, kc, c * 128:(c + 1) * 128], lambda kc: xnT[:, kc, :]))
            yield
            gg = self.tmp.next()
            S.op("act", [psg], [gg], lambda e: e.activation(out=gg[:, :], in_=psg[:, :], func=AF.Gelu_apprx_tanh))
            yield
            S.op("dve", [h, gg], [self.hg[c]], lambda e: e.tensor_tensor(out=self.hg[c][:, :], in0=h[:, :], in1=gg[:, :], op=ALU.mult))
            S.op("pool", [A, gg], [self.ag[c]], lambda e: e.tensor_tensor(out=self.ag_ap[c], in0=A[:, :], in1=gg[:, :], op=ALU.mult))

        yield from stageA(0)
        yield
        for c in range(4):
            if c + 1 < 4:
                yield from stageA(c + 1)
                yield
            yield from stageB(c)
            yield
            yield from stageC(c)
            yield
        self.wdone(wi0)
        self.wdone(wi1)
        S.dma(self.hg, [self.SPH[t]], [lambda e, c=c: e.dma_start(out=self.SPH[t].t[:, c, :], in_=self.hg[c][:, :]) for c in range(4)], self.stsem.next())
        S.dma(self.ag, [self.SPA[t]], [lambda e, c=c: e.dma_start(out=self.SPA[t].t[:, c, :], in_=self.ag_ap[c]) for c in range(4)], self.stsem.next())
        slot2, w2, wi2 = self.wget("w_in2")
        slot3 = w3 = wi3 = None
        qst = {}

        def qA(c):
            nonlocal slot3, w3, wi3
            if c < 4:
                sl, lhs = slot2, (lambda kc: w2[:, kc, c * 128:(c + 1) * 128])
            else:
                slot3, w3, wi3 = self.wget("w_in3")
                w3_ = w3
                sl, lhs = slot3, (lambda kc: w3_[:, kc, 0:128])
            ps = ring4.next()
            S.op("pe", [sl] + XNT, [ps], lambda e: self._mm8(e, ps[:, :], lhs, lambda kc: xnT[:, kc, :]))
            qf = self.tmp.next()
            S.op("act", [ps], [qf], lambda e: e.activation(out=qf[:, :], in_=ps[:, :], func=AF.Copy))
            sq = self.tmp.next()
            S.op("act", [ps], [sq], lambda e: e.activation(out=sq[:, :], in_=ps[:, :], func=AF.Square))
            yield
            pss = ring4.next()
            S.op("pe", [sq, self.cst], [pss], lambda e: e.matmul(pss[:, :], lhsT=self.blk1, rhs=sq[:, :], start=True, stop=True))
            qst[c] = dict(qf=qf, sq=sq, pss=pss)

        def qB(c):
            qf, sq, pss = qst[c]["qf"], qst[c]["sq"], qst[c]["pss"]
            gi = 0 if c < 4 else 1
            S.op("act", [pss], [sq], lambda e: e.activation(out=sq[:, :], in_=pss[:, :], func=AF.Sqrt, scale=1.0 / HD, bias=EPS))
            S.op("dve", [sq], [sq], lambda e: e.reciprocal(out=sq[:, :], in_=sq[:, :]))
            yield
            S.op("dve", [qf, sq, self.qkn], [qf], lambda e: e.scalar_tensor_tensor(
                out=qf[:, :], in0=qf[:, :], scalar=self.qkn[:, gi:gi + 1], in1=sq[:, :], op0=ALU.mult, op1=ALU.mult))
            psr_ = ring4.next()
            S.op("pe", [qf, self.cst], [psr_], lambda e: e.matmul(psr_[:, :], lhsT=self.rrot, rhs=qf[:, :], start=True, stop=True))
            yield
            t1 = self.tmp.next()
            S.op("pool", [qf, rope], [t1], lambda e: e.tensor_tensor(out=t1[:, :], in0=qf[:, :], in1=cos, op=ALU.mult))
            S.op("dve", [psr_, rope], [sq], lambda e: e.tensor_tensor(out=sq[:, :], in0=psr_[:, :], in1=sin, op=ALU.mult))
            yield
            if c < 4:
                S.op("dve", [t1, sq], [self.QT[c]], lambda e: e.tensor_tensor(
                    out=self.qT4[:, :, c, :], in0=t1[:, :].rearrange("p (b q) -> p b q", b=4), in1=sq[:, :].rearrange("p (b q) -> p b q", b=4), op=ALU.add))
            else:
                S.op("dve", [t1, sq], [self.kr], lambda e: e.tensor_tensor(out=self.kr[:, :], in0=t1[:, :], in1=sq[:, :], op=ALU.add))
                S.op("dve", [t1, sq], [self.kT], lambda e: e.tensor_tensor(out=self.kT[:, 128:640], in0=t1[:, :], in1=sq[:, :], op=ALU.add))

        yield from qA(0)
        yield
        for c in range(5):
            if c + 1 < 5:
                yield from qA(c + 1)
                if c + 1 == 4:
                    self.wdone(wi2)
                yield
            yield from qB(c)
            yield
        psv = self.psr.next()

        def mmv(e):
            for b in range(4):
                for kc in range(8):
                    ins = e.matmul(psv[:, b * 128:(b + 1) * 128], lhsT=xnT[:, kc, b * 128:(b + 1) * 128], rhs=w3[:, kc, 128:256], start=(kc == 0), stop=(kc == 7))
            return ins
        S.op("pe", [slot3] + XNT, [psv], mmv)
        S.op("act", [psv], [self.vext], lambda e: e.activation(
            out=self.vext[:, 1:5, :, 0:64], in_=psv[:, :].rearrange("p (b g d) -> p b g d", b=4, g=2), func=AF.Copy))
        self.wdone(wi3)
        if last:
            S.op("act", [psv], [self.vf], lambda e: e.activation(out=self.vf[:, :], in_=psv[:, 384:512], func=AF.Copy))
        yield
        for b in range(4):
            psO = [self.psA, self.psB]
            for g in range(2):
                pts = []
                for kb in range(2):
                    pss = self.psr.next()
                    S.op("pe", [self.kT] + self.QT, [pss], lambda e, pss=pss, g=g, kb=kb, b=b: e.matmul(
                        pss[:, :], lhsT=self.kT[g * 64:(g + 1) * 64, (b + kb) * 128:(b + kb + 1) * 128],
                        rhs=self.qT4[g * 64:(g + 1) * 64, b, :, :].rearrange("p c q -> p (c q)"), start=True, stop=True))
                    ep = self.ep.next()
                    S.op("act", [pss], [ep], lambda e, ep=ep, pss=pss: e.activation(out=ep[:, :], in_=pss[:, :], func=AF.Exp, scale=HD ** -0.5))
                    if t == 0 and b == 0 and kb == 0:
                        mT, mk = self.maskp0, self.maskp0[:, :, :].rearrange("p h q -> p (h q)")
                    else:
                        mT, mk = self.maskp, self.maskp[:, kb, :, :].rearrange("p h q -> p (h q)")
                    S.op("pool", [ep, mT], [ep], lambda e, ep=ep, mk=mk: e.tensor_tensor(out=ep[:, :], in0=ep[:, :], in1=mk, op=ALU.mult))
                    pts.append(ep)
                    yield

                def pv(e, g=g, pts=pts, b=b):
                    for h in range(4):
                        for kb in range(2):
                            ins = e.matmul(psO[g][:, h * 65:(h + 1) * 65], lhsT=pts[kb][:, h * 128:(h + 1) * 128],
                                           rhs=self.vext[:, b + kb, g, :], start=(kb == 0), stop=(kb == 1))
                    return ins
                S.op("pe", pts + [self.vext], [psO[g]], pv)
                yield
            yield from self.g_attn_finish(b, 128, psO)
            yield
        S.op("pool", [self.kT], [self.kT], lambda e: e.tensor_copy(out=self.kT[:, 0:128], in_=self.kT[:, 512:640]))
        S.op("pool", [self.vext], [self.vext], lambda e: e.tensor_copy(out=self.vext[:, 0, :, 0:64], in_=self.vext[:, 4, :, 0:64]))
        S.dma(self.ONT, [self.SPO[t]], [lambda e: e.dma_start(out=self.SPO[t].t[:, :, :], in_=self.onT_t[:, :, :])], self.stsem.next())
        S.dma(XB, [self.SPX[t]], [lambda e: e.dma_start(out=self.SPX[t].t[:, :, :], in_=xt[:, :, :])], self.stsem.next())
        yield

    def exchange_send(self):
        S = self.S
        import os
        ident = self.ident
        ps = self.psr.next()
        S.op("pe", self.HC + [self.cst], [ps], lambda e: e.transpose(out=ps[0:4, 0:128], in_=self.hc[:, 0:4], identity=ident))
        st = self.tmp.next()
        S.op("dve", [ps], [st], lambda e: e.tensor_copy(out=st[0:4, 0:128], in_=ps[0:4, 0:128]))
        S.dma([st], [self.ccsrc], [lambda e: e.dma_start(out=self.ccsrc.t[:, :], in_=st[0:4, 0:128])], st)
        ncr = int(os.environ.get("KCORES", str(NCORES)))
        groups = [[2 * i, 2 * i + 1] for i in range(ncr // 2)]
        if ncr >= 2:
            S.collective([self.ccsrc], [self.ccdst], lambda e: e.collective_compute(
                "AllGather", ALU.bypass, replica_groups=groups, ins=[self.ccsrc.t], outs=[self.ccdst.t]))
        else:
            S.dma([self.ccsrc], [self.ccdst], [lambda e: e.dma_start(out=self.ccdst.t[0:4, :], in_=self.ccsrc.t[:, :])], T(None, "ccfake"))

    def exchange_recv(self):
        S = self.S
        g = self.tmp.next()
        S.dma([self.ccdst], [g], [lambda e: e.dma_start(out=g[0:4, 0:128], in_=self.ccdst.t[0:4, :])], g)
        ps2 = self.psr.next()
        S.op("pe", [g, self.cst], [ps2], lambda e: e.transpose(out=ps2[:, 0:4], in_=g[0:4, 0:128], identity=self.cst.t[0:4, 0, 0:4]))
        S.op("dve", [ps2, self.flagB], [self.hmid], lambda e: e.tensor_scalar(
            out=self.hmid[:, :], in0=ps2[:, 0:4], scalar1=self.flagB[:, 0:1], scalar2=None, op0=ALU.mult))

    def g_l2(self, t):
        S = self.S
        s = t % 2
        XB = self.XT[s]
        xt = self.xt_t[s]
        xaps = [xt.t[:, b, :] for b in range(4)]
        S.dma([self.SPX[t]], XB, [lambda e: e.dma_start(out=xt[:, :, :], in_=self.SPX[t].t[:, :, :])], xt)
        S.dma([self.SPH[t]], self.hg, [lambda e, c=c: e.dma_start(out=self.hg[c][:, :], in_=self.SPH[t].t[:, c, :]) for c in range(4)], self.hg[0])
        S.dma([self.SPA[t]], self.ag, [lambda e, c=c: e.dma_start(out=self.ag_ap[c], in_=self.SPA[t].t[:, c, :]) for c in range(4)], self.ag[0])
        S.dma([self.SPO[t]], self.ONT, [lambda e: e.dma_start(out=self.onT_t[:, :, :], in_=self.SPO[t].t[:, :, :])], self.onT_t)
        yield
        for c in range(4):
            S.op("dve", [self.ag[c], self.hg[c], self.hmid], [self.hg[c]], lambda e, c=c: e.scalar_tensor_tensor(
                out=self.hg[c][:, :], in0=self.ag_ap[c], scalar=self.hmid[:, c:c + 1], in1=self.hg[c][:, :], op0=ALU.mult, op1=ALU.add))
        self.lru_out_norm(512)
        yield
        yield from self.g_out_proj(XB, xaps, 128, 4)

    def g_out_proj(self, XB, xaps, P, nb):
        S = self.S
        for n in range(2):
            slot, w, widx = self.wget("w_out%d" % n)
            for b in range(nb):
                ps = self.psr.next()

                def mm(e, ps=ps, b=b, w=w):
                    for kc in range(8):
                        lhsT = self.lnT[:, kc, b * P:(b + 1) * P] if kc < 4 else self.onT_t[:, kc - 4, b * P:(b + 1) * P]
                        ins = e.matmul(ps[0:P, :], lhsT=lhsT, rhs=w[:, kc, :], start=(kc == 0), stop=(kc == 7))
                    return ins
                S.op("pe", [slot, self.lnT, self.ONT[b]], [ps], mm)
                xap = xaps[b][:, n * 512:(n + 1) * 512]
                S.op("dve", [ps, XB[b]], [XB[b]], lambda e, ps=ps, xap=xap: e.tensor_tensor(out=xap, in0=ps[0:P, :], in1=xap, op=ALU.add))
            self.wdone(widx)
            yield

    def _mm8(self, e, out, lhs_fn, rhs_fn):
        for kc in range(8):
            ins = e.matmul(out, lhsT=lhs_fn(kc), rhs=rhs_fn(kc), start=(kc == 0), stop=(kc == 7))
        return ins

    def prompt_state_out(self):
        S = self.S
        O = self.O
        ident = self.ident
        hf = self.small.next()
        S.op("dve", self.AC + [self.hmid], [hf], lambda e: e.tensor_tensor(out=hf[:, 0:4], in0=self.ac[:, 0:4], in1=self.hmid[:, 0:4], op=ALU.mult))
        S.op("dve", self.HC + [hf], [hf], lambda e: e.tensor_tensor(out=hf[:, 0:4], in0=hf[:, 0:4], in1=self.hc[:, 0:4], op=ALU.add))
        ps = self.psr.next()
        S.op("pe", [hf, self.cst], [ps], lambda e: e.transpose(out=ps[0:4, 0:128], in_=hf[:, 0:4], identity=ident))
        st = self.tmp.next()
        S.op("dve", [ps], [st], lambda e: e.tensor_copy(out=st[0:4, 0:128], in_=ps[0:4, 0:128]))
        S.dma([st], [O["p_h"]], [lambda e: e.dma_start(out=O["p_h"].t[:, :], in_=st[0:4, 0:128])], st)
        ps2 = self.psr.next()

        def trc(e):
            for c in range(4):
                ins = e.transpose(out=ps2[0:3, c * 128:(c + 1) * 128], in_=self.carry[c][:, 0:3], identity=ident)
            return ins
        S.op("pe", self.carry + [self.cst], [ps2], trc)
        st2 = self.tmp.next()
        S.op("dve", [ps2], [st2], lambda e: e.tensor_copy(out=st2[0:3, :], in_=ps2[0:3, :]))
        S.dma([st2], [O["p_conv"]], [lambda e: e.dma_start(out=O["p_conv"].t[:, :], in_=st2[0:3, :])], st2)
        ps3 = self.psr.next()
        S.op("pe", [self.kr, self.cst], [ps3], lambda e: e.transpose(out=ps3[:, 0:128], in_=self.kr[:, 384:512], identity=ident))
        st3 = self.tmp.next()
        S.op("dve", [ps3], [st3], lambda e: e.tensor_copy(out=st3[:, 0:128], in_=ps3[:, 0:128]))
        S.dma([st3], [O["p_k"]], [lambda e: e.dma_start(out=O["p_k"].t[:, :], in_=st3[:, 0:128])], st3)
        S.dma([self.vf], [O["p_v"]], [lambda e: e.dma_start(out=O["p_v"].t[:, :], in_=self.vf[:, :])], self.vf)

    def sample_stage(self):
        S = self.S
        I, O = self.I, self.O
        P = NSEQ * TS
        ident = self.ident
        stg = self.xt_t[0]
        SB = self.XT[0]
        ckv = stg.t[:, 0:2, :].rearrange("p a (s d) -> p (a s) d", d=128)
        cvv = stg.t[:, 2:4, :].rearrange("p a (s d) -> p (a s) d", d=128)
        S.dma([I["ck"]], SB[0:2], [lambda e: e.dma_start(out=ckv, in_=I["ck"].t.rearrange("s k d -> k s d"))], SB[0])
        S.dma([I["cv"]], SB[2:4], [lambda e: e.dma_start(out=cvv, in_=I["cv"].t.rearrange("s k d -> k s d"))], SB[2])
        for q4 in range(4):
            ps = self.psr.next()

            def trk(e, ps=ps, q4=q4):
                for i in range(4):
                    ins = e.transpose(out=ps[:, i * 128:(i + 1) * 128], in_=ckv[:, q4 * 4 + i, :], identity=ident)
                return ins
            S.op("pe", SB[0:2] + [self.cst], [ps], trk)
            S.op("act", [ps], [self.kTc], lambda e, ps=ps, q4=q4: e.activation(
                out=self.kTc[:, q4 * 4:(q4 + 1) * 4, :], in_=ps[:, :].rearrange("p (s k) -> p s k", s=4), func=AF.Copy))
        S.op("dve", SB[2:4], [self.vc], lambda e: e.tensor_copy(out=self.vc[:, :, :, 0:64], in_=cvv.rearrange("p s (g d) -> p s g d", g=2)))
        S.op("pool", [], [self.vc], lambda e: e.memset(self.vc[:, :, :, 64:65], 1.0))
        S.dma([I["ck"]], [O["s_k"]], [lambda e: e.dma_start(out=O["s_k"].t[:, 0:WIN - TS, :], in_=I["ck"].t[:, TS:WIN, :])], T(None, "ckc"))
        S.dma([I["cv"]], [O["s_v"]], [lambda e: e.dma_start(out=O["s_v"].t[:, 0:WIN - TS, :], in_=I["cv"].t[:, TS:WIN, :])], T(None, "cvc"))
        st1 = self.tmp.next()
        st2 = self.tmp.next()
        S.dma([I["st_conv"]], [st1], [lambda e: e.dma_start(out=st1[0:48, :], in_=I["st_conv"].t[:, :])], st1)
        S.dma([I["st_h"]], [st2], [lambda e: e.dma_start(out=st2[0:16, :], in_=I["st_h"].t[:, :])], st2)
        ps = self.psr.next()

        def trs(e):
            for c in range(4):
                ins = e.transpose(out=ps[:, c * 64:c * 64 + 48], in_=st1[0:48, c * 128:(c + 1) * 128], identity=self.cst.t[0:48, 0, 0:48])
            for c in range(4):
                ins = e.transpose(out=ps[:, 256 + c * 16:256 + (c + 1) * 16], in_=st2[0:16, c * 128:(c + 1) * 128], identity=self.cst.t[0:16, 0, 0:16])
            return ins
        S.op("pe", [st1, st2, self.cst], [ps], trs)
        S.op("dve", [ps], [self.stcT], lambda e: e.tensor_copy(
            out=self.stcT[:, :, :, :], in_=ps[:, 0:256].rearrange("p (c x) -> p c x", c=4)[:, :, 0:48].rearrange("p c (s j) -> p c s j", j=3)))
        S.op("dve", [ps], [self.h0T], lambda e: e.tensor_copy(out=self.h0T[:, :, :], in_=ps[:, 256:320].rearrange("p (c s) -> p c s", c=4)))

    def g_sample_ffn1(self):
        S = self.S
        I = self.I
        P = NSEQ * TS
        xt = self.xt_t[0]
        XB = [self.XT[0][0]]
        xaps = [xt.t[0:P, 0, :]]
        S.dma([I["xs"]], XB, [lambda e: e.dma_start(out=xt[0:P, 0, :], in_=I["xs"].t[:, :])], xt)
        yield from self.g_ffn(1, XB, xaps, P, 0, fine=True)

    def sample_part1(self, XB, xaps):
        S = self.S
        I, O = self.I, self.O
        P = NSEQ * TS
        ident = self.ident
        self.norm_T(XB, xaps, P, 1)
        slot0, w0, wi0 = self.wget("w_in0")
        slot1, w1, wi1 = self.wget("w_in1")
        cos = self.rope_s.t[:, 0, :]
        sin = self.rope_s.t[:, 1, :]
        xls = self.xl.next()
        xlv = xls.t[:, 0:448].rearrange("p (c s j) -> p c s j", c=4, s=NSEQ)
        hst = self.hst
        for c in range(4):
            ps = self.psr.next()
            S.op("pe", [slot0, self.XNT[0]], [ps], lambda e, ps=ps, c=c: self._mm8(e, ps[:, 0:P], lambda kc: w0[:, kc, c * 128:(c + 1) * 128], lambda kc: self.xnT_t[:, kc, 0:P]))
            S.op("dve", [self.stcT], [xls], lambda e, c=c: e.tensor_copy(out=xlv[:, c, :, 0:3], in_=self.stcT[:, c, :, :]))
            S.op("act", [ps], [xls], lambda e, ps=ps, c=c: e.activation(out=xlv[:, c, :, 3:7], in_=ps[:, 0:P].rearrange("p (s t) -> p s t", t=TS), func=AF.Copy))
            xc = self.tmp.next()
            xc3 = xc.t[:, 0:P].rearrange("p (s t) -> p s t", t=TS)
            self.conv4(c, xls, lambda j, c=c: xlv[:, c, :, j:j + TS], xc, xc3)
            self._xcT = xc
            a, u = self.lru_gates(c, xc[:, 0:P], P)
            a3 = a.t[:, 0:P].rearrange("p (s t) -> p s t", t=TS)
            u3 = u.t[:, 0:P].rearrange("p (s t) -> p s t", t=TS)
            tm = self.small.next()
            S.op("dve", [a, self.h0T], [tm], lambda e, a3=a3, tm=tm, c=c: e.tensor_tensor(out=tm[:, 0:NSEQ], in0=a3[:, :, 0], in1=self.h0T[:, c, :], op=ALU.mult))
            S.op("dve", [u, tm], [u], lambda e, u3=u3, tm=tm: e.tensor_tensor(out=u3[:, :, 0], in0=u3[:, :, 0], in1=tm[:, 0:NSEQ], op=ALU.add))
            S.op("dve", [a], [a], lambda e, a3=a3: e.memset(a3[:, :, 0], 0.0))
            h = self.tmp.next()
            S.op("dve", [a, u], [h], lambda e, a=a, u=u, h=h: e.tensor_tensor_scan(out=h[:, 0:P], data0=a[:, 0:P], data1=u[:, 0:P], initial=0.0, op0=ALU.mult, op1=ALU.add))
            S.op("pool", [h], [hst], lambda e, h=h, c=c: e.tensor_copy(
                out=hst[:, c * NSEQ:(c + 1) * NSEQ], in_=h.t[:, 0:P].rearrange("p (s t) -> p s t", t=TS)[:, :, TS - 1]))
            psg = self.psr.next()
            S.op("pe", [slot1, self.XNT[0]], [psg], lambda e, psg=psg, c=c: self._mm8(e, psg[:, 0:P], lambda kc: w1[:, kc, c * 128:(c + 1) * 128], lambda kc: self.xnT_t[:, kc, 0:P]))
            gg = self.tmp.next()
            S.op("act", [psg], [gg], lambda e, gg=gg, psg=psg: e.activation(out=gg[:, 0:P], in_=psg[:, 0:P], func=AF.Gelu_apprx_tanh))
            S.op("dve", [h, gg], [self.hg[c]], lambda e, h=h, gg=gg, c=c: e.tensor_tensor(out=self.hg[c][:, 0:P], in0=h[:, 0:P], in1=gg[:, 0:P], op=ALU.mult))
        self.lru_out_norm(P)
        self.wdone(wi0)
        self.wdone(wi1)
        cvst = self.tmp.next()
        S.op("pool", [xls], [cvst], lambda e: e.tensor_copy(out=cvst[:, 0:192].rearrange("p (c s j) -> p c s j", c=4, s=NSEQ), in_=xlv[:, :, :, 4:7]))
        ps = self.psr.next()
        ps2 = self.psr.next()

        def tro(e):
            for c in range(4):
                ins = e.transpose(out=ps[0:NSEQ, c * 128:(c + 1) * 128], in_=hst[:, c * NSEQ:(c + 1) * NSEQ], identity=ident)
            for c in range(4):
                ins = e.transpose(out=ps2[0:48, c * 128:(c + 1) * 128], in_=cvst[:, c * 48:(c + 1) * 48], identity=ident)
            return ins
        S.op("pe", [hst, cvst, self.cst], [ps, ps2], tro)
        o1 = self.tmp.next()
        o2 = self.tmp.next()
        S.op("dve", [ps], [o1], lambda e: e.tensor_copy(out=o1[0:NSEQ, :], in_=ps[0:NSEQ, :]))
        S.op("dve", [ps2], [o2], lambda e: e.tensor_copy(out=o2[0:48, :], in_=ps2[0:48, :]))
        S.dma([o1], [O["s_h"]], [lambda e: e.dma_start(out=O["s_h"].t[:, :], in_=o1[0:NSEQ, :])], o1)
        S.dma([o2], [O["s_conv"]], [lambda e: e.dma_start(out=O["s_conv"].t[:, :], in_=o2[0:48, :])], o2)
        slot2, w2, wi2 = self.wget("w_in2")
        for c in range(4):
            ps = self.psr.next()
            S.op("pe", [slot2, self.XNT[0]], [ps], lambda e, ps=ps, c=c: self._mm8(
                e, ps[:, 0:P], lambda kc: w2[:, kc, c * 128:(c + 1) * 128], lambda kc: self.xnT_t[:, kc, 0:P]))
            self.qk_chunk(ps, P, 0, self.rope_s, cos, sin, [
                (self.QT[c], self.qsA[:, :, c, :], lambda ap: ap.rearrange("p (s t) -> p s t", t=TS)),
                (self.QT[c], self.qsB[:, c, :], lambda ap: ap)])
        self.wdone(wi2)
        slot3, w3, wi3 = self.wget("w_in3")
        ps = self.psr.next()
        S.op("pe", [slot3, self.XNT[0]], [ps], lambda e, ps=ps: self._mm8(e, ps[:, 0:P], lambda kc: w3[:, kc, 0:128], lambda kc: self.xnT_t[:, kc, 0:P]))
        self.qk_chunk(ps, P, 1, self.rope_s, cos, sin, [(self.kT, self.kT[:, 0:P], lambda ap: ap)], out_f32=(self.kr, self.kr[:, 0:P]))
        psv = self.psr.next()
        S.op("pe", [slot3, self.XNT[0]], [psv], lambda e: self._mm8(e, psv[0:P, 0:128], lambda kc: self.xnT_t[:, kc, 0:P], lambda kc: w3[:, kc, 128:256]))
        S.op("act", [psv], [self.vext], lambda e: e.activation(out=self.vext[0:P, 0, :, 0:64], in_=psv[0:P, 0:128].rearrange("p (g d) -> p g d", g=2), func=AF.Copy))
        self.wdone(wi3)
        S.op("pool", [], [self.vext], lambda e: e.memset(self.vext[:, :, :, 64:65], 1.0))
        vnew = self.tmp.next()
        S.op("act", [psv], [vnew], lambda e: e.activation(out=vnew[0:P, 0:128], in_=psv[0:P, 0:128], func=AF.Copy))
        S.dma([vnew], [O["s_v"]], [lambda e: e.dma_start(out=O["s_v"].t[:, WIN - TS:WIN, :], in_=vnew[0:P, 0:128])], vnew)
        psk = self.psr.next()
        S.op("pe", [self.kr, self.cst], [psk], lambda e: e.transpose(out=psk[0:P, 0:128], in_=self.kr[:, 0:P], identity=ident))
        knew = self.tmp.next()
        S.op("dve", [psk], [knew], lambda e: e.tensor_copy(out=knew[0:P, 0:128], in_=psk[0:P, 0:128]))
        S.dma([knew], [O["s_k"]], [lambda e: e.dma_start(out=O["s_k"].t[:, WIN - TS:WIN, :], in_=knew[0:P, 0:128])], knew)
        self.ck(3)
        pscg = [self.psf.next(), self.psf.next()]
        psng = [self.psf.next(), self.psf.next()]
        for g in range(2):
            def sc(e, g=g):
                for s in range(NSEQ):
                    ins = e.matmul(pscg[g][:, s * 16:(s + 1) * 16], lhsT=self.kTc[g * 64:(g + 1) * 64, s, :],
                                   rhs=self.qsA[g * 64:(g + 1) * 64, s, :, :].rearrange("p c t -> p (c t)"), start=True, stop=True)
                return ins
            S.op("pe", [self.kTc] + self.QT, [pscg[g]], sc)
            S.op("pe", [self.kT] + self.QT, [psng[g]], lambda e, g=g: e.matmul(
                psng[g][0:P, 0:256], lhsT=self.kT[g * 64:(g + 1) * 64, 0:P],
                rhs=self.qsB[g * 64:(g + 1) * 64, :, :].rearrange("p c q -> p (c q)"), start=True, stop=True))
        self.ck(3.1)
        ec = self.ep.next()
        en = self.ep.next()
        for g in range(2):
            S.op("act", [pscg[g]], [ec], lambda e, g=g: e.activation(out=ec[:, g * 256:(g + 1) * 256], in_=pscg[g][:, 0:256], func=AF.Exp, scale=HD ** -0.5))
            S.op("act", [psng[g]], [en], lambda e, g=g: e.activation(out=en[0:P, g * 256:(g + 1) * 256], in_=psng[g][0:P, 0:256], func=AF.Exp, scale=HD ** -0.5))
        S.op("pool", [en, self.maskn], [en], lambda e: e.tensor_tensor(out=en[0:P, :], in0=en[0:P, :], in1=self.maskn[:, :, :].rearrange("p h q -> p (h q)"), op=ALU.mult))
        self.ck(3.2)
        aT = self.aT_t
        PAD = self.AT[0:16]
        pstride = aT.t[:, 0, 0:1].ap[0][0]
        S.op("pool", [], PAD, lambda e: e.memset(aT[:, 0:16, :], 0.0))
        for g in range(2):
            pad_out = bass.AP(aT.t, g * 256, [[pstride, 128], [516, NSEQ], [64, 4], [1, TS]])
            S.op("dve", [ec, self.maskc], PAD, lambda e, g=g, pad_out=pad_out: e.tensor_tensor(
                out=pad_out, in0=ec[:, g * 256:(g + 1) * 256].rearrange("p (s h t) -> p s h t", s=NSEQ, h=4),
                in1=self.maskc[:, :].unsqueeze(1).unsqueeze(1).broadcast_to([128, NSEQ, 4, TS]), op=ALU.mult))
        self.ck(3.3)
        psO = [self.psA, self.psB]
        for g in range(2):
            def pv(e, g=g):
                for h in range(4):
                    gh = g * 4 + h
                    for s in range(NSEQ):
                        ins = e.matmul(psO[g][0:P, h * 65:(h + 1) * 65], lhsT=aT[:, s, gh * 64:(gh + 1) * 64], rhs=self.vc[:, s, g, :],
                                       start=(s == 0), stop=False)
                    ins = e.matmul(psO[g][0:P, h * 65:(h + 1) * 65], lhsT=en[0:P, gh * 64:(gh + 1) * 64], rhs=self.vext[0:P, 0, g, :],
                                   start=False, stop=True)
                return ins
            S.op("pe", PAD + [self.vc, en, self.vext], [psO[g]], pv)
        self.ck(3.4)
        self.attn_finish(0, P, psO)
        self.ck(4)
        self.sp_xs = S.dram("sp_xs", [P, D], F32)
        self.sp_ons = S.dram("sp_ons", [128, 4, P], BF16)
        S.dma(XB, [self.sp_xs], [lambda e: e.dma_start(out=self.sp_xs.t[:, :], in_=xaps[0])], T(None, "spxs"))
        S.dma([self.ONT[0]], [self.sp_ons], [lambda e: e.dma_start(out=self.sp_ons.t[:, :, :], in_=self.onT_t[:, :, 0:P])], T(None, "spons"))

    def sample_part2(self):
        S = self.S
        O = self.O
        P = NSEQ * TS
        xt = self.xt_t[0]
        XB = [self.XT[0][0]]
        xaps = [xt.t[0:P, 0, :]]
        S.dma([self.sp_xs], XB, [lambda e: e.dma_start(out=xt[0:P, 0, :], in_=self.sp_xs.t[:, :])], xt)
        S.dma([self.sp_ons], [self.ONT[0]], [lambda e: e.dma_start(out=self.onT_t[:, :, 0:P], in_=self.sp_ons.t[:, :, :])], self.onT_t)
        self.out_proj(XB, xaps, P, 1)
        self.ffn(2, XB, xaps, P, 2)
        S.dma(XB, [O["ys"]], [lambda e: e.dma_start(out=O["ys"].t[:, :], in_=xt[0:P, 0, :])], T(None, "yss"))


def _static_consts():
    ident = np.eye(128, dtype=np.float32)
    blk = np.zeros((128, 128), np.float32)
    blk[:64, :64] = 1.0
    blk[64:, 64:] = 1.0
    rrot = np.zeros((128, 128), np.float32)
    for m in range(128):
        if m % 64 < 32:
            rrot[m + 32, m] = -1.0
        else:
            rrot[m - 32, m] = 1.0
    ones = np.ones((128, 128), np.float32)
    cst = np.concatenate([ident, blk, rrot, ones], axis=1)
    j = np.arange(128)[:, None]
    i = np.arange(128)[None, :]
    mask_p = np.stack([(j > i), (j <= i)], axis=1).astype(np.float32)
    mc = (np.arange(128)[:, None] >= (np.arange(TS)[None, :] + 1)).astype(np.float32)
    kk = np.arange(64)
    mn = ((kk[:, None] // TS == kk[None, :] // TS) & (kk[:, None] % TS <= kk[None, :] % TS)).astype(np.float32)
    mn = np.concatenate([mn, np.zeros((64, 64), np.float32)], axis=0)
    mask_s = np.concatenate([mc, mn], axis=1)
    return dict(cst=cst, mask_p=np.ascontiguousarray(mask_p), mask_s=np.ascontiguousarray(mask_s))


def _rope_tab(pos):
    half = HD // 2
    inv = (np.float32(THETA) ** (-np.arange(half, dtype=np.float32) / np.float32(half))).astype(np.float32)
    ang = np.asarray(pos).astype(np.float32)[None, :] * inv[:, None]
    c = np.tile(np.cos(ang).astype(np.float32), (4, 1))
    s = np.tile(np.sin(ang).astype(np.float32), (4, 1))
    return np.ascontiguousarray(np.stack([c, s], axis=1))


_BUILD_CACHE = {}
_HOOK = None


def kernel(x_prompt, x_sample, state_lru_h, state_conv, cache_k, cache_v,
           ffn1_norm, ffn1_w_gate, ffn1_w_up, ffn1_w_down,
           mix_norm, w_in, conv_w, conv_b, lru_wa, lru_ba, lru_wx, lru_bx, lru_lambda,
           q_norm, k_norm, attn_sinks, lru_out_norm, attn_out_norm, w_out,
           ffn2_norm, ffn2_w_gate, ffn2_w_up, ffn2_w_down):
    f = lambda a: np.ascontiguousarray(np.asarray(a, dtype=np.float32))
    x_prompt = f(x_prompt)
    B, SEQ, _ = x_prompt.shape
    assert B * 2 == NCORES
    HALF = SEQ // 2
    NT = HALF // 512
    assert NT % 2 == 0 and NT >= 2
    DB = x_sample.shape[0]
    assert DB == NSEQ * NCORES
    if NT not in _BUILD_CACHE:
        b0 = Builder(NT)
        b0.build()
        _BUILD_CACHE[NT] = Builder(NT, wseq=b0.wrec).build()
    nc = _BUILD_CACHE[NT]
    shared = dict(
        f1_g=f(ffn1_w_gate[0]), f1_u=f(ffn1_w_up[0]), f1_d=f(ffn1_w_down[0]),
        f2_g=f(ffn2_w_gate[0]), f2_u=f(ffn2_w_up[0]), f2_d=f(ffn2_w_down[0]),
        w_in=f(w_in[0]), w_out=f(w_out[0]),
        gains=f(np.stack([ffn1_norm[0], mix_norm[0], ffn2_norm[0]], axis=0)),
        chv=f(np.concatenate([conv_w[0], conv_b, lru_ba, lru_bx, lru_lambda, lru_out_norm, attn_out_norm], axis=0)),
        qkn=f(np.stack([np.tile(q_norm[0], 2), np.tile(k_norm[0], 2)], axis=0)),
        sinks=f(attn_sinks).reshape(NH), wa=f(lru_wa[0]), wx=f(lru_wx[0]),
    )
    shared.update(_static_consts())
    shared["rope_s"] = _rope_tab(PAST_LEN + (np.arange(NSEQ * TS) % TS))
    xs = f(x_sample)
    ropes = [_rope_tab(h * HALF + np.arange(HALF)) for h in range(2)]
    rope_h = [_rope_tab(np.maximum(h * HALF - 128 + np.arange(128), 0)) for h in range(2)]
    in_maps = []
    for c in range(NCORES):
        m = dict(shared)
        p, h = c // 2, c % 2
        m["xp"] = x_prompt[p, h * HALF:(h + 1) * HALF]
        m["xh"] = x_prompt[p, HALF - 128:HALF] if h == 1 else np.zeros((128, D), np.float32)
        m["flag"] = np.full((1,), float(h), np.float32)
        m["rope_p"] = ropes[h]
        m["rope_h"] = rope_h[h]
        sl = slice(c * NSEQ, (c + 1) * NSEQ)
        m["xs"] = xs[sl].reshape(NSEQ * TS, D)
        m["st_h"] = f(state_lru_h[0, sl])
        m["st_conv"] = f(state_conv[0, sl]).reshape(NSEQ * 3, LRU_W)
        m["ck"] = f(cache_k[0, sl]).reshape(NSEQ, WIN, 128)
        m["cv"] = f(cache_v[0, sl]).reshape(NSEQ, WIN, 128)
        in_maps.append(m)
    if _HOOK is not None:
        R = _HOOK(nc, in_maps)
    else:
        res = run_bass_kernel_spmd(nc, in_maps, core_ids=list(range(NCORES)))
        R = res.results
    yp = np.stack([np.concatenate([R[2 * b]["yp"], R[2 * b + 1]["yp"]], axis=0) for b in range(B)], axis=0)
    ys = np.concatenate([R[c]["ys"].reshape(NSEQ, TS, D) for c in range(NCORES)], axis=0)
    p_h = np.stack([R[2 * b + 1]["p_h"].reshape(LRU_W) for b in range(B)], axis=0)[None]
    p_conv = np.stack([R[2 * b + 1]["p_conv"] for b in range(B)], axis=0)[None]
    p_k = np.stack([R[2 * b + 1]["p_k"].reshape(WIN, 2, HD) for b in range(B)], axis=0)[None]
    p_v = np.stack([R[2 * b + 1]["p_v"].reshape(WIN, 2, HD) for b in range(B)], axis=0)[None]
    s_h = np.concatenate([R[c]["s_h"] for c in range(NCORES)], axis=0)[None]
    s_conv = np.concatenate([R[c]["s_conv"].reshape(NSEQ, 3, LRU_W) for c in range(NCORES)], axis=0)[None]
    s_k = np.concatenate([R[c]["s_k"].reshape(NSEQ, WIN, 2, HD) for c in range(NCORES)], axis=0)[None]
    s_v = np.concatenate([R[c]["s_v"].reshape(NSEQ, WIN, 2, HD) for c in range(NCORES)], axis=0)[None]
    return (yp, ys, p_h, p_conv, p_k, p_v, s_h, s_conv, s_k, s_v)
```

```python
import numpy as np
from contextlib import ExitStack
import concourse.bass as bass
import concourse.mybir as mybir
from concourse.bass_utils import run_bass_kernel_spmd

F32 = mybir.dt.float32
BF16 = mybir.dt.bfloat16
AF = mybir.ActivationFunctionType
ALU = mybir.AluOpType

D = 1024
DFF = 2816
NJ = DFF // 128
LRU_W = 512
HD = 64
NH = 8
WIN = 128
EPS = 1e-6
PAST_LEN = 16384
THETA = 10000.0
NCORES = 8
NSEQ = 16
TS = 4


class T:
    __slots__ = ("t", "name", "w", "r", "dsem", "dcnt")

    def __init__(self, t=None, name=""):
        self.t = t
        self.name = name
        self.w = None
        self.r = {}
        self.dsem = None
        self.dcnt = 0

    def __getitem__(self, k):
        return self.t[k]


class Rec:
    def __init__(self):
        self.calls = []

    def __getattr__(self, name):
        def f(*a, **k):
            self.calls.append((name, a, k))
            return self
        return f


def _record(fn):
    r = Rec()
    fn(r)
    assert r.calls
    return r.calls


class Sched:
    ENG = ("pe", "act", "dve", "pool", "sp")

    def __init__(self, nc, es):
        self.nc = nc
        self.es = es
        self.lists = {e: [] for e in self.ENG}
        self.sem = {}
        self.cnt = {e: 0 for e in self.ENG}
        self.seen = {e: {} for e in self.ENG}
        for e in ("pe", "act", "dve", "pool"):
            self.sem[e] = es.enter_context(nc.semaphore("s_" + e))
        import os
        self.same_eng_sync = os.environ.get("KSYNC", "1") == "1"
        self.nsem = 4
        self.sb_bytes = 0

    def sb(self, name, shape, dt):
        t = self.es.enter_context(self.nc.sbuf_tensor("sb_" + name, list(shape), dt))
        n = 1
        for s in shape[1:]:
            n *= s
        self.sb_bytes += n * (2 if dt == BF16 else 4)
        return T(t, name)

    def ps(self, name):
        t = self.es.enter_context(self.nc.psum_tensor("ps_" + name, [128, 512], F32))
        return T(t, name)

    def dram(self, name, shape, dt):
        t = self.nc.dram_tensor(name, list(shape), dt)
        return T(t.ap(), name)

    def _dsem(self, b):
        if b.dsem is None:
            b.dsem = self.es.enter_context(self.nc.semaphore("d%d" % self.nsem))
            self.nsem += 1
        return b.dsem

    def _deps(self, reads, writes):
        deps = {}

        def add(tk):
            if tk is None:
                return
            k = id(tk[0])
            if k not in deps or deps[k][1] < tk[1]:
                deps[k] = tk
        for b in reads:
            add(b.w)
        for b in writes:
            add(b.w)
            for tk in b.r.values():
                add(tk)
        return deps

    def _commit(self, reads, writes, tk):
        k = id(tk[0])
        ws = set(id(b) for b in writes)
        for b in reads:
            if id(b) in ws:
                continue
            b.r[k] = tk
        for b in writes:
            b.w = tk
            b.r = {}

    def op(self, eng, reads, writes, fn):
        deps = self._deps(reads, writes)
        waits = []
        own = id(self.sem[eng])
        for k, (s, v) in deps.items():
            if k == own and (eng == "pe" or not self.same_eng_sync):
                continue
            if self.seen[eng].get(k, 0) >= v:
                continue
            self.seen[eng][k] = v
            waits.append((s, v))
        self.cnt[eng] += 1
        tk = (self.sem[eng], self.cnt[eng])
        self.lists[eng].append((waits, _record(fn), tk))
        self._commit(reads, writes, tk)
        return tk

    def dma(self, reads, writes, fns, semb, q="sp"):
        sem = self._dsem(semb)
        deps = self._deps(reads, writes)
        if semb.dcnt > 0:
            k = id(sem)
            if k not in deps or deps[k][1] < semb.dcnt:
                deps[k] = (sem, semb.dcnt)
        waits = []
        for k, (s, v) in deps.items():
            if self.seen[q].get(k, 0) >= v:
                continue
            self.seen[q][k] = v
            waits.append((s, v))
        semb.dcnt += 16 * len(fns)
        tk = (sem, semb.dcnt)
        self.lists[q].append((waits, [_record(f) for f in fns], ("dma", sem)))
        self._commit(reads, writes, tk)
        return tk

    def collective(self, reads, writes, fn):
        eng = "pool"
        sem = self.es.enter_context(self.nc.semaphore("ccsem%d" % self.nsem))
        self.nsem += 1
        deps = self._deps(reads, writes)
        waits = []
        for k, (s, v) in deps.items():
            if self.seen[eng].get(k, 0) >= v:
                continue
            self.seen[eng][k] = v
            waits.append((s, v))
        tk = (sem, 1)
        self.lists[eng].append((waits, _record(fn), ("cc", sem)))
        self._commit(reads, writes, tk)
        return tk

    def final_wait(self, q, bufs):
        deps = self._deps(bufs, [])
        self.lists[q].append((list(deps.values()), None, None))

    def emit(self):
        nc = self.nc
        lists = self.lists

        def run(e, items):
            for waits, fn, tk in items:
                for s, v in waits:
                    e.wait_ge(s, v)
                if fn is None:
                    continue
                if tk[0] == "cc":
                    for (nm, a, k) in fn:
                        ins = getattr(e, nm)(*a, **k)
                    ins.then_inc(tk[1])
                elif tk[0] == "dma":
                    for calls in fn:
                        for (nm, a, k) in calls:
                            ins = getattr(e, nm)(*a, **k)
                        ins.then_inc(tk[1], 16)
                else:
                    for (nm, a, k) in fn:
                        ins = getattr(e, nm)(*a, **k)
                    ins.then_inc(tk[0], 1)

        with nc.Block() as block:
            @block.tensor
            def _(e):
                run(e, lists["pe"])

            @block.scalar
            def _(e):
                run(e, lists["act"])

            @block.vector
            def _(e):
                run(e, lists["dve"])

            @block.gpsimd
            def _(e):
                run(e, lists["pool"])

            @block.sync
            def _(e):
                run(e, lists["sp"])


class Ring:
    def __init__(self, items):
        self.items = items
        self.i = 0

    def next(self):
        x = self.items[self.i % len(self.items)]
        self.i += 1
        return x


class Builder:
    def __init__(self, NT, wseq=None):
        self.NT = NT
        self.wseq_in = wseq
        self.nc = bass.Bass("TRN2", target_bir_lowering=False)

    def din(self, name, shape, dt=F32):
        return T(self.nc.dram_tensor(name, list(shape), dt, kind="ExternalInput").ap(), name)

    def dout(self, name, shape, dt=F32):
        t = T(self.nc.dram_tensor(name, list(shape), dt, kind="ExternalOutput").ap(), name)
        self.outs.append(t)
        return t

    def build(self):
        nc = self.nc
        NT = self.NT
        NTOK = NT * 512
        self.outs = []
        with ExitStack() as es:
            S = self.S = Sched(nc, es)
            I = self.I = {}
            I["xp"] = self.din("xp", [NTOK, D])
            I["xh"] = self.din("xh", [128, D])
            I["flag"] = self.din("flag", [1])
            I["rope_h"] = self.din("rope_h", [128, 2, 128])
            I["xs"] = self.din("xs", [NSEQ * TS, D])
            I["st_h"] = self.din("st_h", [NSEQ, LRU_W])
            I["st_conv"] = self.din("st_conv", [NSEQ * 3, LRU_W])
            I["ck"] = self.din("ck", [NSEQ, WIN, 128])
            I["cv"] = self.din("cv", [NSEQ, WIN, 128])
            for f in (1, 2):
                I["f%d_g" % f] = self.din("f%d_g" % f, [D, DFF])
                I["f%d_u" % f] = self.din("f%d_u" % f, [D, DFF])
                I["f%d_d" % f] = self.din("f%d_d" % f, [DFF, D])
            I["w_in"] = self.din("w_in", [D, 1792])
            I["w_out"] = self.din("w_out", [D, D])
            I["gains"] = self.din("gains", [3, D])
            I["chv"] = self.din("chv", [10, LRU_W])
            I["qkn"] = self.din("qkn", [2, 128])
            I["sinks"] = self.din("sinks", [NH])
            I["wa"] = self.din("wa", [8, 64, 64])
            I["wx"] = self.din("wx", [8, 64, 64])
            I["cst"] = self.din("cst", [128, 4 * 128])
            I["mask_p"] = self.din("mask_p", [128, 2, 128])
            I["mask_s"] = self.din("mask_s", [128, TS + 64])
            I["rope_p"] = self.din("rope_p", [128, 2, NTOK])
            I["rope_s"] = self.din("rope_s", [128, 2, 64])
            O = self.O = {}
            O["yp"] = self.dout("yp", [NTOK, D])
            O["ys"] = self.dout("ys", [NSEQ * TS, D])
            O["p_h"] = self.dout("p_h", [4, 128])
            O["p_conv"] = self.dout("p_conv", [3, LRU_W])
            O["p_k"] = self.dout("p_k", [WIN, 128])
            O["p_v"] = self.dout("p_v", [WIN, 128])
            O["s_h"] = self.dout("s_h", [NSEQ, LRU_W])
            O["s_conv"] = self.dout("s_conv", [NSEQ * 3, LRU_W])
            O["s_k"] = self.dout("s_k", [NSEQ, WIN, 128])
            O["s_v"] = self.dout("s_v", [NSEQ, WIN, 128])
            self.dbg_out = {}
            self.alloc()
            import os
            self.stop = float(os.environ.get("KSTOP", "99"))
            try:
                self.prologue()
                self.ck(0)
                self.prompt_all()
            except StopIteration:
                pass
            S.final_wait("sp", self.outs)
            S.emit()
        return nc

    def ck(self, n):
        if self.stop <= n:
            raise StopIteration

    def alloc(self):
        S = self.S
        self.cst = S.sb("cst", [128, 4, 128], F32)
        self.identb = S.sb("identb", [128, 128], BF16)
        self.maskp = S.sb("maskp", [128, 2, 4, 128], BF16)
        self.maskc = S.sb("maskc", [128, TS], BF16)
        self.maskn = S.sb("maskn", [64, 8, 64], BF16)
        self.gT = S.sb("gT", [128, 3, 8], F32)
        self.chv = S.sb("chv", [128, 4, 16], F32)
        self.gaT = S.sb("gaT", [128, 4], F32)
        self.qkn = S.sb("qkn", [128, 2], F32)
        self.esink = S.sb("esink", [128, NH], F32)
        self.wab = S.sb("wab", [128, 4, 2, 128], BF16)
        self.rope = [S.sb("rope%d" % i, [128, 2, 512], F32) for i in range(1)]
        self.rope_s = S.sb("rope_s", [128, 2, 64], F32)
        self.NWB = 4
        self.cstage = Ring([S.sb("cstage%d" % i, [128, 2048], F32) for i in range(2)])
        self.ceng = Ring(["act"])
        self.spill_sem = Ring([T(None, "spill%d" % i) for i in range(4)])
        self.wbuf = [S.sb("wbuf%d" % i, [128, 4096], BF16) for i in range(self.NWB)]
        self.xt_t = [S.sb("xt%d" % i, [128, 4, D], F32) for i in range(2)]
        self.XT = [[T(self.xt_t[i].t, "xt%d_%d" % (i, b)) for b in range(4)] for i in range(2)]
        self.xnb = Ring([S.sb("xnb%d" % i, [128, D], BF16) for i in range(3)])
        self.junk = S.sb("junk", [128, D], BF16)
        self.xnT_t = S.sb("xnT", [128, 8, 512], BF16)
        self.XNT = [T(self.xnT_t.t, "xnT%d" % b) for b in range(4)]
        self.aT_t = S.sb("aT", [128, NJ, 512], BF16)
        self.AT = [T(self.aT_t.t, "aT%d" % j) for j in range(NJ)]
        self.sg = Ring([S.sb("sg%d" % i, [128, 512], F32) for i in range(2)])
        self.small = Ring([S.sb("small%d" % i, [128, 16], F32) for i in range(24)])
        self.small_f = Ring([S.sb("smallf%d" % i, [128, 16], F32) for i in range(12)])
        self.zerob = S.sb("zerob", [128, 512], BF16)
        self.flagB = S.sb("flagB", [128, 1], F32)
        self.maskp0 = S.sb("maskp0", [128, 4, 128], BF16)
        self.hmid = S.sb("hmid", [128, 4], F32)
        self.ac = S.sb("ac", [128, 4], F32)
        self.AC = [T(self.ac.t, "ac%d" % c) for c in range(4)]
        self.tmp = Ring([S.sb("tmp%d" % i, [128, 512], F32) for i in range(10)])
        self.tmpb = Ring([S.sb("tmpb%d" % i, [128, 512], BF16) for i in range(2)])
        self.xl = Ring([S.sb("xl%d" % i, [128, 515], F32) for i in range(2)])
        self.carry = [S.sb("carry%d" % c, [128, 3], F32) for c in range(4)]
        self.hc = S.sb("hc", [128, 4], F32)
        self.HC = [T(self.hc.t, "hc%d" % c) for c in range(4)]
        self.hg = [S.sb("hg%d" % c, [128, 512], F32) for c in range(4)]
        self.qT4 = S.sb("qT4", [128, 4, 4, 128], BF16)
        self.qsA = S.sb("qsA", [128, NSEQ, 4, TS], BF16)
        self.qsB = S.sb("qsB", [128, 4, NSEQ * TS], BF16)
        self.QT = [T(self.qT4.t, "qT%d" % c) for c in range(4)]
        self.kT = S.sb("kT", [128, 640], BF16)
        self.kr = S.sb("kr", [128, 512], F32)
        self.vext = S.sb("vext", [128, 5, 2, 65], BF16)
        self.vf = S.sb("vf", [128, 128], F32)
        self.ep = Ring([S.sb("ep%d" % i, [128, 512], BF16) for i in range(4)])
        self.o = Ring([S.sb("o%d" % i, [128, 512], F32) for i in range(1)])
        self.on = Ring([S.sb("on%d" % i, [128, 512], BF16) for i in range(1)])
        self.onT_t = S.sb("onT", [128, 4, 512], BF16)
        self.ONT = [T(self.onT_t.t, "onT%d" % b) for b in range(4)]
        self.lnT = S.sb("lnT", [128, 4, 512], BF16)
        self.kTc = S.sb("kTc", [128, NSEQ, 128], BF16)
        self.vc = S.sb("vc", [128, NSEQ, 2, 65], BF16)
        self.stcT = S.sb("stcT", [128, 4, NSEQ, 3], F32)
        self.h0T = S.sb("h0T", [128, 4, NSEQ], F32)
        self.hst = S.sb("hst", [128, 64], F32)
        self.psf = Ring([S.ps("psf%d" % i) for i in range(4)])
        self.psr = Ring([S.ps("psr%d" % i) for i in range(2)])
        self.psA = S.ps("psA")
        self.psB = S.ps("psB")
        self.cxF = dict(xnT=self.xnT_t.t, XNT=self.XNT, small=self.small_f, ps=self.psf)
        self.cxM = None
        self.wseq = []
        self.wloaded = 0
        self.wslot = {}
        print('SBUF bytes/partition', S.sb_bytes)


    def prologue(self):
        S = self.S
        I = self.I
        nc = self.nc
        cst = self.cst

        def ld(dst, src_ap, dst_ap=None, q="sp"):
            S.dma([], [dst], [lambda e: e.dma_start(out=dst_ap if dst_ap is not None else dst.t[:], in_=src_ap)], dst, q=q)

        ld(cst, I["cst"].t.rearrange("p (a b) -> p a b", a=4))
        mpf = self.tmp.next()
        msf = self.tmp.next()
        self.maskp_f = T(mpf.t, "maskp_f")
        self.masks_f = T(msf.t, "masks_f")
        mpv = mpf.t[:, 0:256].rearrange("p (a b) -> p a b", a=2)
        ld(mpf, I["mask_p"].t[:, :, :], dst_ap=mpv)
        ld(msf, I["mask_s"].t[:, :], dst_ap=msf.t[:, 0:TS + 64])
        ld(self.rope_s, I["rope_s"].t[:, :, :])
        self.ident = cst.t[:, 0, :]
        self.blk1 = cst.t[:, 1, :]
        self.rrot = cst.t[:, 2, :]
        self.ones = cst.t[:, 3, :]
        S.op("dve", [cst], [self.identb], lambda e: e.tensor_copy(out=self.identb[:, :], in_=self.ident))

        def mk(e):
            for i in range(2):
                for h in range(4):
                    ins = e.tensor_copy(out=self.maskp[:, i, h, :], in_=mpv[:, i, :])
            return ins
        S.op("dve", [mpf], [self.maskp], mk)
        S.op("dve", [msf], [self.maskc], lambda e: e.tensor_copy(out=self.maskc[:, :], in_=msf[:, 0:TS]))
        S.op("dve", [msf], [self.maskn], lambda e: e.tensor_copy(
            out=self.maskn[:, :, :], in_=msf[0:64, TS:TS + 64].unsqueeze(1).broadcast_to([64, 8, 64])))
        sk = self.small.next()
        S.dma([], [sk], [lambda e: e.dma_start(out=sk[:, 0:NH], in_=I["sinks"].t.partition_broadcast(128))], sk)
        S.op("act", [sk], [self.esink], lambda e: e.activation(out=self.esink[:, :], in_=sk[:, 0:NH], func=AF.Exp))
        stg = self.tmp.next()
        g_st = self.tmp.next()
        g_st2 = self.tmp.next()
        S.dma([], [g_st], [lambda e: e.dma_start(out=g_st[0:3, :], in_=I["gains"].t[:, 0:512])], g_st)
        S.dma([], [g_st2], [lambda e: e.dma_start(out=g_st2[0:3, :], in_=I["gains"].t[:, 512:1024])], g_st2)
        ps = self.psr.next()

        def trg(e):
            for kc in range(8):
                src = (g_st if kc < 4 else g_st2)
                ins = e.transpose(out=ps[:, kc * 4:kc * 4 + 3], in_=src[0:3, (kc % 4) * 128:(kc % 4 + 1) * 128], identity=cst.t[0:3, 0, 0:3])
            return ins
        S.op("pe", [g_st, g_st2, cst], [ps], trg)
        S.op("dve", [ps], [self.gT], lambda e: e.tensor_copy(
            out=self.gT[:, :, :], in_=ps[:, 0:32].rearrange("p (k w) -> p w k", w=4)[:, 0:3, :]))
        S.dma([], [stg], [lambda e: e.dma_start(out=stg[0:10, :], in_=I["chv"].t[:, :])], stg)
        ps2 = self.psr.next()

        def trc(e):
            for c in range(4):
                ins = e.transpose(out=ps2[:, c * 16:c * 16 + 10], in_=stg[0:10, c * 128:(c + 1) * 128], identity=cst.t[0:10, 0, 0:10])
            return ins
        S.op("pe", [stg, cst], [ps2], trc)
        chv = self.chv
        S.op("dve", [ps2], [chv], lambda e: e.tensor_copy(out=chv[:, :, 0:10], in_=ps2[:, 0:64].rearrange("p (c i) -> p c i", i=16)[:, :, 0:10]))
        S.op("act", [chv], [chv], lambda e: e.activation(out=chv[:, :, 11], in_=chv[:, :, 7], func=AF.Exp, scale=-1.0))
        S.op("act", [chv], [chv], lambda e: e.activation(out=chv[:, :, 10], in_=chv[:, :, 11], func=AF.Ln, bias=1.0))
        S.op("dve", [chv], [chv], lambda e: e.tensor_scalar(out=chv[:, :, 10], in0=chv[:, :, 10], scalar1=-8.0, scalar2=None, op0=ALU.mult))
        S.op("dve", [chv], [self.gaT], lambda e: e.tensor_copy(out=self.gaT[:, :], in_=chv[:, :, 9]))
        st3 = self.tmp.next()
        S.dma([], [st3], [lambda e: e.dma_start(out=st3[0:2, 0:128], in_=I["qkn"].t[:, :])], st3)
        ps3 = self.psr.next()
        S.op("pe", [st3, cst], [ps3], lambda e: e.transpose(out=ps3[:, 0:2], in_=st3[0:2, 0:128], identity=cst.t[0:2, 0, 0:2]))
        S.op("dve", [ps3], [self.qkn], lambda e: e.tensor_copy(out=self.qkn[:, :], in_=ps3[:, 0:2]))
        wst = self.tmp.next()
        for which, nm in ((0, "wa"), (1, "wx")):
            wst = self.tmp.next()
            S.op("pool", [], [wst], lambda e, wst=wst: e.memset(wst[:, :], 0.0))
            fns = []
            for c in range(4):
                for half in range(2):
                    n = 2 * c + half
                    fns.append(lambda e, c=c, half=half, n=n, nm=nm, wst=wst: e.dma_start(
                        out=wst[half * 64:(half + 1) * 64, c * 128 + half * 64:c * 128 + half * 64 + 64], in_=I[nm].t[n, :, :]))
            S.dma([], [wst], fns, wst)
            S.op("dve", [wst], [self.wab], lambda e, wst=wst, which=which: e.tensor_copy(
                out=self.wab[:, :, which, :], in_=wst[:, :].rearrange("p (c m) -> p c m", c=4)))
        self.pieces = {}

        def piece(name, shape, halves):
            d = S.dram("bf_" + name, shape, BF16)
            self.pieces[name] = (d, shape, halves)

        self.dgroups = [(0, 8), (8, 16), (16, 22)]
        for f in (1, 2):
            for pi in range(NJ // 2):
                j0 = pi * 2
                halves = []
                for gu, nm in ((0, "g"), (1, "u")):
                    src = I["f%d_%s" % (f, nm)].t[:, j0 * 128:(j0 + 2) * 128].rearrange("(kc p) m -> p kc m", p=128)
                    halves.append((src, gu * 2048, 2048, (8, 256)))
                piece("f%d_gu%d" % (f, pi), [128, 2, 8, 256], halves)
            for n in range(2):
                for gi, (j0, j1) in enumerate(self.dgroups):
                    jn = j1 - j0
                    hs = []
                    ja = j0
                    for cnt in (jn // 2, jn - jn // 2):
                        src = I["f%d_d" % f].t[ja * 128:(ja + cnt) * 128, n * 512:(n + 1) * 512].rearrange("(j p) n -> p j n", p=128)
                        hs.append((src, (ja - j0) * 512, cnt * 512, (cnt, 512)))
                        ja += cnt
                    piece("f%d_d%d_%d" % (f, n, gi), [128, jn, 512], hs)
        for pi, (c0, c1) in enumerate(((0, 512), (512, 1024), (1024, 1536), (1536, 1792))):
            w = c1 - c0
            hs = []
            for kc0 in (0, 4):
                src = I["w_in"].t[kc0 * 128:(kc0 + 4) * 128, c0:c1].rearrange("(kc p) n -> p kc n", p=128)
                hs.append((src, kc0 * w, 4 * w, (4, w)))
            piece("w_in%d" % pi, [128, 8, w], hs)
        for n in range(2):
            hs = []
            for kc0 in (0, 4):
                src = I["w_out"].t[kc0 * 128:(kc0 + 4) * 128, n * 512:(n + 1) * 512].rearrange("(kc p) n -> p kc n", p=128)
                hs.append((src, kc0 * 512, 4 * 512, (4, 512)))
            piece("w_out%d" % n, [128, 8, 512], hs)
        seq = []
        for f in (1, 2):
            part = ["f%d_gu%d" % (f, pi) for pi in range(NJ // 2)]
            part += ["f%d_d%d_%d" % (f, n, gi) for n in range(2) for gi in range(3)]
            if f == 1:
                part += ["w_in0", "w_in1", "w_in2", "w_in3", "w_out0", "w_out1"]
            seq += part
        self.tile_seq = seq
        self.record = self.wseq_in is None
        self.wseq = [] if self.record else list(self.wseq_in)
        self.converted = set()
        self.cpos = 0
        self.CONV_AHEAD = 6
        self.wrec = []
        self.wpos = 0
        self.wheld = set()
        self.wfree = list(range(self.NWB))
        self.wslot_of = {}
        self.wissued = 0

    def wget(self, name):
        if self.record:
            self.wrec.append(name)
            if self.wpos >= len(self.wseq):
                self.wseq.append(name)
        assert self.wseq[self.wpos] == name, (self.wseq[self.wpos], name)
        idx = self.wpos
        self.wpos += 1
        self.wheld.add(idx)
        self._wprefetch()
        assert self.wissued > idx
        slot = self.wbuf[self.wslot_of[idx]]
        d, shape, _ = self.pieces[name]
        n = 1
        for s in shape[1:]:
            n *= s
        v = slot.t[:, 0:n]
        if len(shape) == 4:
            v = v.rearrange("p (a b c) -> p a b c", a=shape[1], b=shape[2])
        else:
            v = v.rearrange("p (a b) -> p a b", a=shape[1])
        return slot, v, idx

    def wdone(self, idx):
        self.wheld.discard(idx)
        self.wfree.append(self.wslot_of.pop(idx))
        self._wprefetch()

    def _convert_upto(self, last):
        S = self.S
        while self.cpos <= min(last, len(self.wseq) - 1):
            nm = self.wseq[self.cpos]
            self.cpos += 1
            if nm in self.converted:
                continue
            self.converted.add(nm)
            d, shape, halves = self.pieces[nm]
            dflat = d.t.rearrange("p a b c -> p (a b c)") if len(shape) == 4 else d.t.rearrange("p a b -> p (a b)")
            fns = []
            for (src, off, cnt, (a_, b_)) in halves:
                if nm == "w_in2":
                    for k4 in range(4):
                        for two in range(2):
                            dv = dflat[:, off + k4 * 512:off + (k4 + 1) * 512].rearrange("p (c two d) -> p c two d", c=4, two=2)[:, :, two, :]
                            sv = src[:, k4, :].rearrange("p (two c d) -> p two c d", two=2, c=4)[:, two, :, :]
                            fns.append(lambda e, dv=dv, sv=sv: e.dma_start(out=dv, in_=sv))
                else:
                    dv = dflat[:, off:off + cnt].rearrange("p (a b) -> p a b", a=a_)
                    fns.append(lambda e, dv=dv, src=src: e.dma_start(out=dv, in_=src))
            S.dma([], [d], fns, self.spill_sem.next(), q="pool")

    def _convert_one_early(self):
        p = self.cpos
        while p < len(self.wseq) and self.wseq[p] in self.converted:
            p += 1
        if p < len(self.wseq):
            save = self.cpos
            self.cpos = p
            self._convert_upto(p)
            self.cpos = save

    def _wprefetch(self):
        S = self.S
        while self.wissued < len(self.wseq) and self.wfree:
            i = self.wissued
            nm = self.wseq[i]
            d, shape, halves = self.pieces[nm]
            si = self.wfree.pop(0)
            self.wslot_of[i] = si
            slot = self.wbuf[si]
            n = 1
            for s_ in shape[1:]:
                n *= s_
            dflat = d.t.rearrange("p a b c -> p (a b c)") if len(shape) == 4 else d.t.rearrange("p a b -> p (a b)")
            self._convert_upto(i + self.CONV_AHEAD)
            if i % 4 == 0 and not self.record:
                self._convert_one_early()
            S.dma([d], [slot], [lambda e, slot=slot, dflat=dflat, n=n: e.dma_start(out=slot[:, 0:n], in_=dflat)], slot)
            self.wissued += 1

    def rstd_tm(self, x_T, x_ap, P, n, small=None):
        S = self.S
        small = small or self.small
        ss = small.next()
        junk = self.junk
        S.op("act", [x_T], [ss], lambda e: e.activation(out=junk[0:P, 0:n], in_=x_ap, func=AF.Square, accum_out=ss[0:P, 0:1]))
        sr = small.next()
        S.op("act", [ss], [sr], lambda e: e.activation(out=sr[0:P, 0:1], in_=ss[0:P, 0:1], func=AF.Sqrt, scale=1.0 / n, bias=EPS))
        rs = small.next()
        S.op("dve", [sr], [rs], lambda e: e.reciprocal(out=rs[0:P, 0:1], in_=sr[0:P, 0:1]))
        return rs

    def norm_T(self, XB, xaps, P, which, cx=None):
        S = self.S
        cx = cx or self.cxF
        xnT_t = cx["xnT"]
        for b, (xT_, xap) in enumerate(zip(XB, xaps)):
            rs = self.rstd_tm(xT_, xap, P, D, cx["small"])
            xnb = self.xnb.next()
            S.op("act", [xT_, rs], [xnb], lambda e, xnb=xnb, xap=xap, rs=rs: e.activation(
                out=xnb[0:P, :], in_=xap, func=AF.Copy, scale=rs[0:P, 0:1]))
            ps = cx["ps"].next()
            psb = ps.t[:, :].bitcast(BF16)

            def tr(e, xnb=xnb, psb=psb):
                for k in range(8):
                    ins = e.transpose(out=psb[:, k * 128:k * 128 + P], in_=xnb[0:P, k * 128:(k + 1) * 128], identity=self.identb[0:P, 0:P])
                return ins
            S.op("pe", [xnb, self.identb], [ps], tr)
            gt = self.gT.t[:, which, :]
            S.op("dve", [ps, self.gT], [cx["XNT"][b]], lambda e, psb=psb, b=b, gt=gt: e.tensor_tensor(
                out=xnT_t[:, :, b * P:(b + 1) * P],
                in0=psb.rearrange("p (k t) -> p k t", k=8)[:, :, 0:P],
                in1=gt.unsqueeze(2).broadcast_to([128, 8, P]), op=ALU.mult))

    def ffn(self, f, XB, xaps, P, which):
        for _ in self.g_ffn(f, XB, xaps, P, which):
            pass

    def g_ffn(self, f, XB, xaps, P, which, fine=False):
        S = self.S
        cx = self.cxF
        xnT_t = cx["xnT"]
        nb = len(XB)
        NTK = nb * P
        self.norm_T(XB, xaps, P, which, cx)
        yield
        for pi in range(NJ // 2):
            slot, w, widx = self.wget("f%d_gu%d" % (f, pi))
            banks = [(self.psf.next(), self.psf.next()) for jj in range(2)]
            sgs = [None, None]
            for gu in range(2):
                for jj in range(2):
                    ps = banks[jj][gu]

                    def mm(e, w=w, jj=jj, ps=ps, gu=gu):
                        for kc in range(8):
                            ins = e.matmul(ps[:, 0:NTK], lhsT=w[:, gu, kc, jj * 128:(jj + 1) * 128], rhs=xnT_t[:, kc, 0:NTK],
                                           start=(kc == 0), stop=(kc == 7))
                        return ins
                    S.op("pe", [slot] + cx["XNT"][0:nb], [ps], mm)
                    if fine:
                        yield
                for jj in range(2):
                    j = pi * 2 + jj
                    psG, psU = banks[jj]
                    if gu == 0:
                        sgs[jj] = self.sg.next()
                        if fine:
                            S.op("act", [psG], [sgs[jj]], lambda e, sg=sgs[jj], psG=psG: e.activation(out=sg[:, 0:NTK], in_=psG[:, 0:NTK], func=AF.Tanh, scale=0.5))
                            S.op("dve", [sgs[jj], psG], [sgs[jj]], lambda e, sg=sgs[jj], psG=psG: e.scalar_tensor_tensor(
                                out=sg[:, 0:NTK], in0=sg[:, 0:NTK], scalar=1.0, in1=psG[:, 0:NTK], op0=ALU.add, op1=ALU.mult))
                        else:
                            S.op("act", [psG], [sgs[jj]], lambda e, sg=sgs[jj], psG=psG: e.activation(out=sg[:, 0:NTK], in_=psG[:, 0:NTK], func=AF.Silu))
                    elif fine:
                        S.op("dve", [sgs[jj], psU], [self.AT[j]], lambda e, sg=sgs[jj], psU=psU, j=j: e.scalar_tensor_tensor(
                            out=self.aT_t[:, j, 0:NTK], in0=sg[:, 0:NTK], scalar=0.5, in1=psU[:, 0:NTK], op0=ALU.mult, op1=ALU.mult))
                    else:
                        S.op("dve", [sgs[jj], psU], [self.AT[j]], lambda e, sg=sgs[jj], psU=psU, j=j: e.tensor_tensor(
                            out=self.aT_t[:, j, 0:NTK], in0=sg[:, 0:NTK], in1=psU[:, 0:NTK], op=ALU.mult))
            self.wdone(widx)
            if not fine:
                yield
        for n in range(2):
            pss = [self.psf.next() for b in range(nb)]
            for gi, (j0, j1) in enumerate(self.dgroups):
                slot, w, widx = self.wget("f%d_d%d_%d" % (f, n, gi))
                for b in range(nb):
                    def mm(e, w=w, b=b, ps=pss[b], j0=j0, j1=j1):
                        for j in range(j0, j1):
                            ins = e.matmul(ps[0:P, :], lhsT=self.aT_t[:, j, b * P:(b + 1) * P], rhs=w[:, j - j0, :],
                                           start=(j == 0), stop=(j == NJ - 1))
                        return ins
                    S.op("pe", [slot] + self.AT[j0:j1], [pss[b]], mm)
                    if fine:
                        yield
                self.wdone(widx)
                if not fine:
                    yield
            for b in range(nb):
                xap = xaps[b][:, n * 512:(n + 1) * 512]
                S.op("dve", [pss[b], XB[b]], [XB[b]], lambda e, ps=pss[b], xap=xap: e.scalar_tensor_tensor(
                    out=xap, in0=ps[0:P, :], scalar=0.5, in1=xap, op0=ALU.mult, op1=ALU.add))
            yield

    def qk_chunk(self, ps, N, gain_idx, rope_T, cos, sin, outs, out_f32=None):
        S = self.S
        qf = self.tmp.next()
        S.op("act", [ps], [qf], lambda e: e.activation(out=qf[:, 0:N], in_=ps[:, 0:N], func=AF.Copy))
        sq = self.tmp.next()
        S.op("act", [ps], [sq], lambda e: e.activation(out=sq[:, 0:N], in_=ps[:, 0:N], func=AF.Square))
        pss = self.psr.next()
        S.op("pe", [sq, self.cst], [pss], lambda e: e.matmul(pss[:, 0:N], lhsT=self.blk1, rhs=sq[:, 0:N], start=True, stop=True))
        sr = self.tmp.next()
        S.op("act", [pss], [sr], lambda e: e.activation(out=sr[:, 0:N], in_=pss[:, 0:N], func=AF.Sqrt, scale=1.0 / HD, bias=EPS))
        rs = self.tmp.next()
        S.op("dve", [sr], [rs], lambda e: e.reciprocal(out=rs[:, 0:N], in_=sr[:, 0:N]))
        qn = self.tmp.next()
        S.op("dve", [qf, rs, self.qkn], [qn], lambda e: e.scalar_tensor_tensor(
            out=qn[:, 0:N], in0=qf[:, 0:N], scalar=self.qkn[:, gain_idx:gain_idx + 1], in1=rs[:, 0:N], op0=ALU.mult, op1=ALU.mult))
        psr_ = self.psr.next()
        S.op("pe", [qn, self.cst], [psr_], lambda e: e.matmul(psr_[:, 0:N], lhsT=self.rrot, rhs=qn[:, 0:N], start=True, stop=True))
        t1 = self.tmp.next()
        S.op("pool", [qn, rope_T], [t1], lambda e: e.tensor_tensor(out=t1[:, 0:N], in0=qn[:, 0:N], in1=cos, op=ALU.mult))
        t2 = self.tmp.next()
        S.op("dve", [psr_, rope_T], [t2], lambda e: e.tensor_tensor(out=t2[:, 0:N], in0=psr_[:, 0:N], in1=sin, op=ALU.mult))
        if out_f32 is not None:
            fT, fap = out_f32
            S.op("dve", [t1, t2], [fT], lambda e: e.tensor_tensor(out=fap, in0=t1[:, 0:N], in1=t2[:, 0:N], op=ALU.add))
        for (oT, oap, view) in outs:
            S.op("dve", [t1, t2], [oT], lambda e, oap=oap, view=view: e.tensor_tensor(out=oap, in0=view(t1[:, 0:N]), in1=view(t2[:, 0:N]), op=ALU.add))

    def lru_gates(self, c, xc_ap, N, shape3=None):
        S = self.S
        chv = self.chv
        xcb = self.tmpb.next()
        xc_T = self._xcT
        S.op("pool", [xc_T], [xcb], lambda e: e.tensor_copy(out=xcb[:, 0:N], in_=xc_ap))
        psa = self.psr.next()
        psx = self.psr.next()
        S.op("pe", [xcb, self.wab], [psa], lambda e: e.matmul(psa[:, 0:N], lhsT=self.wab[:, c, 0, :], rhs=xcb[:, 0:N], start=True, stop=True))
        S.op("pe", [xcb, self.wab], [psx], lambda e: e.matmul(psx[:, 0:N], lhsT=self.wab[:, c, 1, :], rhs=xcb[:, 0:N], start=True, stop=True))
        r = self.tmp.next()
        S.op("act", [psa, chv], [r], lambda e: e.activation(out=r[:, 0:N], in_=psa[:, 0:N], func=AF.Sigmoid, bias=chv[:, c, 5:6]))
        ig = self.tmp.next()
        S.op("act", [psx, chv], [ig], lambda e: e.activation(out=ig[:, 0:N], in_=psx[:, 0:N], func=AF.Sigmoid, bias=chv[:, c, 6:7]))
        a = self.tmp.next()
        S.op("act", [r, chv], [a], lambda e: e.activation(out=a[:, 0:N], in_=r[:, 0:N], func=AF.Exp, scale=chv[:, c, 10:11]))
        a2 = self.tmp.next()
        S.op("act", [a], [a2], lambda e: e.activation(out=a2[:, 0:N], in_=a[:, 0:N], func=AF.Square))
        m = self.tmp.next()
        S.op("act", [a2], [m], lambda e: e.activation(out=m[:, 0:N], in_=a2[:, 0:N], func=AF.Sqrt, scale=-1.0, bias=1.0))
        t = self.tmp.next()
        S.op("dve", [ig, xc_T], [t], lambda e: e.tensor_tensor(out=t[:, 0:N], in0=ig[:, 0:N], in1=xc_ap, op=ALU.mult))
        u = self.tmp.next()
        S.op("dve", [t, m], [u], lambda e: e.tensor_tensor(out=u[:, 0:N], in0=t[:, 0:N], in1=m[:, 0:N], op=ALU.mult))
        return a, u

    def conv4(self, c, xl_T, xl_view, out_T, out_ap):
        S = self.S
        chv = self.chv
        S.op("dve", [xl_T, chv], [out_T], lambda e: e.tensor_scalar(
            out=out_ap, in0=xl_view(0), scalar1=chv[:, c, 0:1], scalar2=chv[:, c, 4:5], op0=ALU.mult, op1=ALU.add))
        for j in range(1, 4):
            S.op("dve", [xl_T, chv, out_T], [out_T], lambda e, j=j: e.scalar_tensor_tensor(
                out=out_ap, in0=xl_view(j), scalar=chv[:, c, j:j + 1], in1=out_ap, op0=ALU.mult, op1=ALU.add))

    def lru_out_norm(self, N):
        S = self.S
        psn = self.psB
        for c in range(4):
            sq = self.tmp.next()
            S.op("act", [self.hg[c]], [sq], lambda e, sq=sq, c=c: e.activation(out=sq[:, 0:N], in_=self.hg[c][:, 0:N], func=AF.Square))
            S.op("pe", [sq, self.cst], [psn], lambda e, sq=sq, c=c: e.matmul(psn[:, 0:N], lhsT=self.ones, rhs=sq[:, 0:N], start=(c == 0), stop=(c == 3)))
        sr = self.tmp.next()
        S.op("act", [psn], [sr], lambda e: e.activation(out=sr[:, 0:N], in_=psn[:, 0:N], func=AF.Sqrt, scale=1.0 / LRU_W, bias=EPS))
        rs = self.tmp.next()
        S.op("dve", [sr], [rs], lambda e: e.reciprocal(out=rs[:, 0:N], in_=sr[:, 0:N]))
        for c in range(4):
            S.op("dve", [self.hg[c], rs, self.chv], [self.lnT], lambda e, c=c: e.scalar_tensor_tensor(
                out=self.lnT[:, c, 0:N], in0=self.hg[c][:, 0:N], scalar=self.chv[:, c, 8:9], in1=rs[:, 0:N], op0=ALU.mult, op1=ALU.mult))

    def attn_finish(self, b, P, psO):
        return self.run(self.g_attn_finish(b, P, psO))

    def g_attn_finish(self, b, P, psO):
        S = self.S
        o = self.o.next()
        den = self.small.next()
        rden = self.small.next()
        for g in range(2):
            pv = psO[g][0:P, 0:260].rearrange("p (h d) -> p h d", h=4)
            S.op("dve", [psO[g], self.esink], [den], lambda e, g=g, pv=pv: e.tensor_tensor(
                out=den[0:P, g * 4:(g + 1) * 4], in0=pv[:, :, 64], in1=self.esink[0:P, g * 4:(g + 1) * 4], op=ALU.add))
        S.op("dve", [den], [rden], lambda e: e.reciprocal(out=rden[0:P, 0:8], in_=den[0:P, 0:8]))
        for g in range(2):
            pv = psO[g][0:P, 0:260].rearrange("p (h d) -> p h d", h=4)
            S.op("dve", [psO[g], rden], [o], lambda e, g=g, pv=pv: e.tensor_tensor(
                out=o[0:P, g * 256:(g + 1) * 256].rearrange("p (h d) -> p h d", h=4), in0=pv[:, :, 0:64],
                in1=rden[0:P, g * 4:(g + 1) * 4].unsqueeze(2).broadcast_to([P, 4, 64]), op=ALU.mult))
        yield
        rs = self.rstd_tm(o, o[0:P, :], P, 512)
        on = self.on.next()
        S.op("act", [o, rs], [on], lambda e: e.activation(out=on[0:P, :], in_=o[0:P, :], func=AF.Copy, scale=rs[0:P, 0:1]))
        ps = self.psr.next()
        psb = ps.t[:, :].bitcast(BF16)

        def tr(e):
            for k in range(4):
                ins = e.transpose(out=psb[:, k * 128:k * 128 + P], in_=on[0:P, k * 128:(k + 1) * 128], identity=self.identb[0:P, 0:P])
            return ins
        S.op("pe", [on, self.identb], [ps], tr)
        yield
        S.op("dve", [ps, self.gaT], [self.ONT[b]], lambda e: e.tensor_tensor(
            out=self.onT_t[:, :, b * P:(b + 1) * P], in0=psb[:, 0:512].rearrange("p (k t) -> p k t", k=4)[:, :, 0:P],
            in1=self.gaT[:, :].unsqueeze(2).broadcast_to([128, 4, P]), op=ALU.mult))

    def out_proj(self, XB, xaps, P, nb):
        for _ in self.g_out_proj(XB, xaps, P, nb):
            pass

    def handoff_back(self):
        c0, c1 = self.cstage.items
        for c, als in ((c0, self.ag), (c1, self.cxM["XNT"])):
            for al in als:
                for tk in [al.w] + list(al.r.values()):
                    if tk is None:
                        continue
                    k = id(tk[0])
                    if k not in c.r or c.r[k][1] < tk[1]:
                        c.r[k] = tk

    def handoff(self):
        c0, c1 = self.cstage.items

        def inherit(name, src):
            t_ = T(src.t, name)
            t_.w = src.w
            t_.r = dict(src.r)
            return t_
        self.ag = [inherit("ag%d" % c, c0) for c in range(4)]
        self.ag_ap = [c0.t[:, c * 512:(c + 1) * 512] for c in range(4)]
        xnTm = c1.t[:, :].bitcast(BF16).rearrange("p (k t) -> p k t", k=8)
        self.cxM = dict(xnT=xnTm, XNT=[inherit("xnTm%d" % b, c1) for b in range(4)], small=self.small, ps=self.psr)

    def run(self, g):
        try:
            while True:
                next(g)
        except StopIteration as e:
            return e.value

    def interleave(self, ga, gb, ra=1, rb=1):
        da = db = False
        na = nb = 0
        while not (da and db):
            pick_a = (not da) and (db or na * rb <= nb * ra)
            if pick_a:
                try:
                    next(ga)
                    na += 1
                except StopIteration:
                    da = True
            else:
                try:
                    next(gb)
                    nb += 1
                except StopIteration:
                    db = True
        self.unit_counts = (na, nb)

    def prompt_all(self):
        S = self.S
        NT = self.NT
        sx = S.dram("sp_x1", [NT, 128, 4, D], F32)
        sh = S.dram("sp_hg", [NT, 128, 4, 512], F32)
        sa = S.dram("sp_ag", [NT, 128, 4, 512], F32)
        so = S.dram("sp_on", [NT, 128, 4, 512], BF16)
        self.SPX = [T(sx.t[t], "spx%d" % t) for t in range(NT)]
        self.SPH = [T(sh.t[t], "sph%d" % t) for t in range(NT)]
        self.SPA = [T(sa.t[t], "spa%d" % t) for t in range(NT)]
        self.SPO = [T(so.t[t], "spo%d" % t) for t in range(NT)]
        self.ccsrc = S.dram("cc_src", [4, 128], F32)
        self.ccdst = S.dram("cc_dst", [8, 128], F32)
        self.stsem = Ring([T(None, "stsem%d" % i) for i in range(4)])
        import os
        self.r1 = tuple(int(x) for x in os.environ.get("KR1", "4,3").split(","))
        self.r2 = tuple(int(x) for x in os.environ.get("KR2", "1,1").split(","))
        self.ysems = [T(None, "ysem0"), T(None, "ysem1")]
        self.handoff()
        self.sample_stage()
        self.run(self.g_halo())
        self.ck(6)
        self.load_x(0)
        self.run(self.g_ffn1(0))
        for t in range(NT):
            if t + 1 < NT:
                self.load_x(t + 1)
                self.interleave(self.g_mix1(t), self.g_ffn1(t + 1), self.r1[0], self.r1[1])
            else:
                self.run(self.g_mix1(t))
        self.ck(7)
        self.exchange_send()
        self.sample_part2()
        self.exchange_recv()
        self.ck(8)
        self.run(self.g_l2(0))
        for t in range(NT):
            if t + 1 < NT:
                self.interleave(self.g_ffn2(t), self.g_l2(t + 1), self.r2[0], self.r2[1])
            else:
                self.run(self.g_ffn2(t))
        self.prompt_state_out()

    def load_x(self, t):
        S = self.S
        I = self.I
        s = t % 2
        xt = self.xt_t[s]
        S.dma([I["xp"]], self.XT[s], [lambda e: e.dma_start(out=xt[:, :, :], in_=I["xp"].t[t * 512:(t + 1) * 512, :].rearrange("(b p) d -> p b d", p=128))], xt)

    def g_ffn1(self, t):
        s = t % 2
        xaps = [self.xt_t[s].t[:, b, :] for b in range(4)]
        yield from self.g_ffn(1, self.XT[s], xaps, 128, 0, fine=True)

    def g_ffn2(self, t):
        S = self.S
        O = self.O
        s = t % 2
        xt = self.xt_t[s]
        xaps = [xt.t[:, b, :] for b in range(4)]
        yield from self.g_ffn(2, self.XT[s], xaps, 128, 2)
        S.dma(self.XT[s], [O["yp"]], [lambda e: e.dma_start(out=O["yp"].t[t * 512:(t + 1) * 512, :].rearrange("(b p) d -> p b d", p=128), in_=xt[:, :, :])], self.ysems[s])

    def g_halo(self):
        S = self.S
        I = self.I
        cx = self.cxF
        xnT = cx["xnT"]
        XNT = cx["XNT"]
        xt = self.xt_t[1]
        XB = [self.XT[1][0]]
        xaps = [xt.t[:, 0, :]]
        S.dma([I["xh"]], XB, [lambda e: e.dma_start(out=xt[:, 0, :], in_=I["xh"].t[:, :])], xt)
        PS_ = NSEQ * TS
        XBs = [self.XT[1][1]]
        S.op("pool", [], XBs, lambda e: e.memset(xt[64:128, 1, :], 0.0))
        S.dma([I["xs"]], XBs, [lambda e: e.dma_start(out=xt[0:PS_, 1, :], in_=I["xs"].t[:, :])], T(None, "xsld"))
        rope = self.rope[0]
        S.dma([I["rope_h"]], [rope], [lambda e: e.dma_start(out=rope[:, :, 0:128], in_=I["rope_h"].t[:, :, :])], rope)
        S.dma([I["flag"]], [self.flagB], [lambda e: e.dma_start(out=self.flagB[:, :], in_=I["flag"].t.partition_broadcast(128))], self.flagB)
        S.op("dve", [self.maskp, self.flagB], [self.maskp0], lambda e: e.tensor_scalar(
            out=self.maskp0[:, :, :], in0=self.maskp[:, 0, :, :], scalar1=self.flagB[:, 0:1], scalar2=None, op0=ALU.mult))
        S.op("pool", [], [self.zerob], lambda e: e.memset(self.zerob[:, :], 0.0))
        self.run(self.g_ffn(1, XB + XBs, xaps + [xt.t[:, 1, :]], 128, 0))
        self.sample_part1(XBs, [xt.t[0:PS_, 1, :]])
        self.norm_T(XB, xaps, 128, 1, cx)
        slot0, w0, wi0 = self.wget("w_in0")
        for c in range(4):
            ps = self.psr.next()
            S.op("pe", [slot0, XNT[0]], [ps], lambda e, ps=ps, c=c: self._mm8(e, ps[:, 0:128], lambda kc: w0[:, kc, c * 128:(c + 1) * 128], lambda kc: xnT[:, kc, 0:128]))
            S.op("act", [ps], [self.carry[c]], lambda e, ps=ps, c=c: e.activation(out=self.carry[c][:, :], in_=ps[:, 125:128], func=AF.Copy))
        self.wdone(wi0)
        slot3, w3, wi3 = self.wget("w_in3")
        ps = self.psr.next()
        S.op("pe", [slot3, XNT[0]], [ps], lambda e, ps=ps: self._mm8(e, ps[:, 0:128], lambda kc: w3[:, kc, 0:128], lambda kc: xnT[:, kc, 0:128]))
        self.qk_chunk(ps, 128, 1, rope, rope.t[:, 0, 0:128], rope.t[:, 1, 0:128], [(self.kT, self.kT[:, 0:128], lambda ap: ap)])
        psv = self.psr.next()
        S.op("pe", [slot3, XNT[0]], [psv], lambda e: self._mm8(e, psv[:, 0:128], lambda kc: xnT[:, kc, 0:128], lambda kc: w3[:, kc, 128:256]))
        S.op("act", [psv], [self.vext], lambda e: e.activation(out=self.vext[:, 0, :, 0:64], in_=psv[:, 0:128].rearrange("p (g d) -> p g d", g=2), func=AF.Copy))
        self.wdone(wi3)
        S.op("pool", [], [self.vext], lambda e: e.memset(self.vext[:, :, :, 64:65], 1.0))
        yield

    def g_mix1(self, t):
        S = self.S
        I = self.I
        NT = self.NT
        cx = self.cxM
        xnT = cx["xnT"]
        XNT = cx["XNT"]
        s = t % 2
        XB = self.XT[s]
        xt = self.xt_t[s]
        xaps = [xt.t[:, b, :] for b in range(4)]
        last = (t == NT - 1)
        rope = self.rope[0]
        S.dma([I["rope_p"]], [rope], [lambda e: e.dma_start(out=rope[:, :, :], in_=I["rope_p"].t[:, :, t * 512:(t + 1) * 512])], rope)
        cos = rope.t[:, 0, :]
        sin = rope.t[:, 1, :]
        self.norm_T(XB, xaps, 128, 1, cx)
        yield
        slot0, w0, wi0 = self.wget("w_in0")
        slot1, w1, wi1 = self.wget("w_in1")
        chv = self.chv
        ring4 = Ring([self.psr.items[0], self.psr.items[1], self.psA, self.psB])
        st = {}

        def stageA(c):
            ps = ring4.next()
            S.op("pe", [slot0] + XNT, [ps], lambda e: self._mm8(e, ps[:, :], lambda kc: w0[:, kc, c * 128:(c + 1) * 128], lambda kc: xnT[:, kc, :]))
            xl = self.xl.next()
            S.op("pool", [self.carry[c]], [xl], lambda e: e.tensor_copy(out=xl[:, 0:3], in_=self.carry[c][:, :]))
            S.op("act", [ps], [xl], lambda e: e.activation(out=xl[:, 3:515], in_=ps[:, :], func=AF.Copy))
            S.op("pool", [xl], [self.carry[c]], lambda e: e.tensor_copy(out=self.carry[c][:, :], in_=xl[:, 512:515]))
            yield
            xc = self.tmp.next()
            self.conv4(c, xl, lambda j: xl[:, j:j + 512], xc, xc[:, :])
            yield
            xcb = self.tmpb.next()
            S.op("act", [xc], [xcb], lambda e: e.activation(out=xcb[:, :], in_=xc[:, :], func=AF.Copy))
            psa = ring4.next()
            psx = ring4.next()
            S.op("pe", [xcb, self.wab], [psa], lambda e: e.matmul(psa[:, :], lhsT=self.wab[:, c, 0, :], rhs=xcb[:, :], start=True, stop=True))
            S.op("pe", [xcb, self.wab], [psx], lambda e: e.matmul(psx[:, :], lhsT=self.wab[:, c, 1, :], rhs=xcb[:, :], start=True, stop=True))
            yield
            r = self.tmp.next()
            S.op("act", [psa, chv], [r], lambda e: e.activation(out=r[:, :], in_=psa[:, :], func=AF.Sigmoid, bias=chv[:, c, 5:6]))
            ig = self.tmp.next()
            S.op("act", [psx, chv], [ig], lambda e: e.activation(out=ig[:, :], in_=psx[:, :], func=AF.Sigmoid, bias=chv[:, c, 6:7]))
            st[c] = dict(xc=xc, r=r, ig=ig)

        def stageB(c):
            xc, r, ig = st[c]["xc"], st[c]["r"], st[c]["ig"]
            S.op("act", [r, chv], [r], lambda e: e.activation(out=r[:, :], in_=r[:, :], func=AF.Exp, scale=chv[:, c, 10:11]))
            S.op("dve", [ig, xc], [ig], lambda e: e.tensor_tensor(out=ig[:, :], in0=ig[:, :], in1=xc[:, :], op=ALU.mult))
            yield
            S.op("act", [r], [xc], lambda e: e.activation(out=xc[:, :], in_=r[:, :], func=AF.Square))
            S.op("act", [xc], [xc], lambda e: e.activation(out=xc[:, :], in_=xc[:, :], func=AF.Sqrt, scale=-1.0, bias=1.0))
            S.op("dve", [ig, xc], [ig], lambda e: e.tensor_tensor(out=ig[:, :], in0=ig[:, :], in1=xc[:, :], op=ALU.mult))
            a, u = r, ig
            yield
            h = self.tmp.next()
            init = 0.0 if t == 0 else self.hc[:, c:c + 1]
            S.op("dve", [a, u] + ([] if t == 0 else [self.HC[c]]), [h], lambda e: e.tensor_tensor_scan(
                out=h[:, :], data0=a[:, :], data1=u[:, :], initial=init, op0=ALU.mult, op1=ALU.add))
            S.op("pool", [h], [self.HC[c]], lambda e: e.tensor_copy(out=self.hc[:, c:c + 1], in_=h[:, 511:512]))
            yield
            A = self.tmp.next()
            inita = 1.0 if t == 0 else self.ac[:, c:c + 1]
            S.op("dve", [a, self.zerob] + ([] if t == 0 else [self.AC[c]]), [A], lambda e: e.tensor_tensor_scan(
                out=A[:, :], data0=a[:, :], data1=self.zerob[:, :], initial=inita, op0=ALU.mult, op1=ALU.add))
            S.op("pool", [A], [self.AC[c]], lambda e: e.tensor_copy(out=self.ac[:, c:c + 1], in_=A[:, 511:512]))
            st[c].update(h=h, A=A)

        def stageC(c):
            h, A = st[c]["h"], st[c]["A"]
            psg = ring4.next()
            S.op("pe", [slot1] + XNT, [psg], lambda e: self._mm8(e, psg[:, :], lambda kc: w1[:, kc, c * 128:(c + 1) * 128], lambda kc: xnT[:, kc, :]))
            yield
            gg = self.tmp.next()
            S.op("act", [psg], [gg], lambda e: e.activation(out=gg[:, :], in_=psg[:, :], func=AF.Gelu_apprx_tanh))
            yield
            S.op("dve", [h, gg], [self.hg[c]], lambda e: e.tensor_tensor(out=self.hg[c][:, :], in0=h[:, :], in1=gg[:, :], op=ALU.mult))
            S.op("pool", [A, gg], [self.ag[c]], lambda e: e.tensor_tensor(out=self.ag_ap[c], in0=A[:, :], in1=gg[:, :], op=ALU.mult))

        yield from stageA(0)
        yield
        for c in range(4):
            if c + 1 < 4:
                yield from stageA(c + 1)
                yield
            yield from stageB(c)
            yield
            yield from stageC(c)
            yield
        self.wdone(wi0)
        self.wdone(wi1)
        S.dma(self.hg, [self.SPH[t]], [lambda e, c=c: e.dma_start(out=self.SPH[t].t[:, c, :], in_=self.hg[c][:, :]) for c in range(4)], self.stsem.next())
        S.dma(self.ag, [self.SPA[t]], [lambda e, c=c: e.dma_start(out=self.SPA[t].t[:, c, :], in_=self.ag_ap[c]) for c in range(4)], self.stsem.next())
        slot2, w2, wi2 = self.wget("w_in2")
        slot3 = w3 = wi3 = None
        qst = {}

        def qA(c):
            nonlocal slot3, w3, wi3
            if c < 4:
                sl, lhs = slot2, (lambda kc: w2[:, kc, c * 128:(c + 1) * 128])
            else:
                slot3, w3, wi3 = self.wget("w_in3")
                w3_ = w3
                sl, lhs = slot3, (lambda kc: w3_[:, kc, 0:128])
            ps = ring4.next()
            S.op("pe", [sl] + XNT, [ps], lambda e: self._mm8(e, ps[:, :], lhs, lambda kc: xnT[:, kc, :]))
            qf = self.tmp.next()
            S.op("act", [ps], [qf], lambda e: e.activation(out=qf[:, :], in_=ps[:, :], func=AF.Copy))
            sq = self.tmp.next()
            S.op("act", [ps], [sq], lambda e: e.activation(out=sq[:, :], in_=ps[:, :], func=AF.Square))
            yield
            pss = ring4.next()
            S.op("pe", [sq, self.cst], [pss], lambda e: e.matmul(pss[:, :], lhsT=self.blk1, rhs=sq[:, :], start=True, stop=True))
            qst[c] = dict(qf=qf, sq=sq, pss=pss)

        def qB(c):
            qf, sq, pss = qst[c]["qf"], qst[c]["sq"], qst[c]["pss"]
            gi = 0 if c < 4 else 1
            S.op("act", [pss], [sq], lambda e: e.activation(out=sq[:, :], in_=pss[:, :], func=AF.Sqrt, scale=1.0 / HD, bias=EPS))
            S.op("dve", [sq], [sq], lambda e: e.reciprocal(out=sq[:, :], in_=sq[:, :]))
            yield
            S.op("dve", [qf, sq, self.qkn], [qf], lambda e: e.scalar_tensor_tensor(
                out=qf[:, :], in0=qf[:, :], scalar=self.qkn[:, gi:gi + 1], in1=sq[:, :], op0=ALU.mult, op1=ALU.mult))
            psr_ = ring4.next()
            S.op("pe", [qf, self.cst], [psr_], lambda e: e.matmul(psr_[:, :], lhsT=self.rrot, rhs=qf[:, :], start=True, stop=True))
            yield
            t1 = self.tmp.next()
            S.op("dve", [qf, rope], [t1], lambda e: e.tensor_tensor(out=t1[:, :], in0=qf[:, :], in1=cos, op=ALU.mult))
            S.op("dve", [psr_, rope], [sq], lambda e: e.tensor_tensor(out=sq[:, :], in0=psr_[:, :], in1=sin, op=ALU.mult))
            yield
            if c < 4:
                S.op("dve", [t1, sq], [self.QT[c]], lambda e: e.tensor_tensor(
                    out=self.qT4[:, :, c, :], in0=t1[:, :].rearrange("p (b q) -> p b q", b=4), in1=sq[:, :].rearrange("p (b q) -> p b q", b=4), op=ALU.add))
            else:
                S.op("dve", [t1, sq], [self.kr], lambda e: e.tensor_tensor(out=self.kr[:, :], in0=t1[:, :], in1=sq[:, :], op=ALU.add))
                S.op("dve", [t1, sq], [self.kT], lambda e: e.tensor_tensor(out=self.kT[:, 128:640], in0=t1[:, :], in1=sq[:, :], op=ALU.add))

        yield from qA(0)
        yield
        for c in range(5):
            if c + 1 < 5:
                yield from qA(c + 1)
                if c + 1 == 4:
                    self.wdone(wi2)
                yield
            yield from qB(c)
            yield
        psv = self.psr.next()

        def mmv(e):
            for b in range(4):
                for kc in range(8):
                    ins = e.matmul(psv[:, b * 128:(b + 1) * 128], lhsT=xnT[:, kc, b * 128:(b + 1) * 128], rhs=w3[:, kc, 128:256], start=(kc == 0), stop=(kc == 7))
            return ins
        S.op("pe", [slot3] + XNT, [psv], mmv)
        S.op("act", [psv], [self.vext], lambda e: e.activation(
            out=self.vext[:, 1:5, :, 0:64], in_=psv[:, :].rearrange("p (b g d) -> p b g d", b=4, g=2), func=AF.Copy))
        self.wdone(wi3)
        if last:
            S.op("act", [psv], [self.vf], lambda e: e.activation(out=self.vf[:, :], in_=psv[:, 384:512], func=AF.Copy))
        yield
        for b in range(4):
            psO = [self.psA, self.psB]
            for g in range(2):
                pts = []
                for kb in range(2):
                    pss = self.psr.next()
                    S.op("pe", [self.kT] + self.QT, [pss], lambda e, pss=pss, g=g, kb=kb, b=b: e.matmul(
                        pss[:, :], lhsT=self.kT[g * 64:(g + 1) * 64, (b + kb) * 128:(b + kb + 1) * 128],
                        rhs=self.qT4[g * 64:(g + 1) * 64, b, :, :].rearrange("p c q -> p (c q)"), start=True, stop=True))
                    ep = self.ep.next()
                    S.op("act", [pss], [ep], lambda e, ep=ep, pss=pss: e.activation(out=ep[:, :], in_=pss[:, :], func=AF.Exp, scale=HD ** -0.5))
                    if t == 0 and b == 0 and kb == 0:
                        mT, mk = self.maskp0, self.maskp0[:, :, :].rearrange("p h q -> p (h q)")
                    else:
                        mT, mk = self.maskp, self.maskp[:, kb, :, :].rearrange("p h q -> p (h q)")
                    S.op("dve", [ep, mT], [ep], lambda e, ep=ep, mk=mk: e.tensor_tensor(out=ep[:, :], in0=ep[:, :], in1=mk, op=ALU.mult))
                    pts.append(ep)
                    yield

                def pv(e, g=g, pts=pts, b=b):
                    for h in range(4):
                        for kb in range(2):
                            ins = e.matmul(psO[g][:, h * 65:(h + 1) * 65], lhsT=pts[kb][:, h * 128:(h + 1) * 128],
                                           rhs=self.vext[:, b + kb, g, :], start=(kb == 0), stop=(kb == 1))
                    return ins
                S.op("pe", pts + [self.vext], [psO[g]], pv)
                yield
            yield from self.g_attn_finish(b, 128, psO)
            yield
        S.op("pool", [self.kT], [self.kT], lambda e: e.tensor_copy(out=self.kT[:, 0:128], in_=self.kT[:, 512:640]))
        S.op("pool", [self.vext], [self.vext], lambda e: e.tensor_copy(out=self.vext[:, 0, :, 0:64], in_=self.vext[:, 4, :, 0:64]))
        S.dma(self.ONT, [self.SPO[t]], [lambda e: e.dma_start(out=self.SPO[t].t[:, :, :], in_=self.onT_t[:, :, :])], self.stsem.next())
        S.dma(XB, [self.SPX[t]], [lambda e: e.dma_start(out=self.SPX[t].t[:, :, :], in_=xt[:, :, :])], self.stsem.next())
        yield

    def exchange_send(self):
        S = self.S
        import os
        ident = self.ident
        ps = self.psr.next()
        S.op("pe", self.HC + [self.cst], [ps], lambda e: e.transpose(out=ps[0:4, 0:128], in_=self.hc[:, 0:4], identity=ident))
        st = self.tmp.next()
        S.op("dve", [ps], [st], lambda e: e.tensor_copy(out=st[0:4, 0:128], in_=ps[0:4, 0:128]))
        S.dma([st], [self.ccsrc], [lambda e: e.dma_start(out=self.ccsrc.t[:, :], in_=st[0:4, 0:128])], st)
        ncr = int(os.environ.get("KCORES", str(NCORES)))
        groups = [[2 * i, 2 * i + 1] for i in range(ncr // 2)]
        if ncr >= 2:
            S.collective([self.ccsrc], [self.ccdst], lambda e: e.collective_compute(
                "AllGather", ALU.bypass, replica_groups=groups, ins=[self.ccsrc.t], outs=[self.ccdst.t]))
        else:
            S.dma([self.ccsrc], [self.ccdst], [lambda e: e.dma_start(out=self.ccdst.t[0:4, :], in_=self.ccsrc.t[:, :])], T(None, "ccfake"))

    def exchange_recv(self):
        S = self.S
        g = self.tmp.next()
        S.dma([self.ccdst], [g], [lambda e: e.dma_start(out=g[0:4, 0:128], in_=self.ccdst.t[0:4, :])], g)
        ps2 = self.psr.next()
        S.op("pe", [g, self.cst], [ps2], lambda e: e.transpose(out=ps2[:, 0:4], in_=g[0:4, 0:128], identity=self.cst.t[0:4, 0, 0:4]))
        S.op("dve", [ps2, self.flagB], [self.hmid], lambda e: e.tensor_scalar(
            out=self.hmid[:, :], in0=ps2[:, 0:4], scalar1=self.flagB[:, 0:1], scalar2=None, op0=ALU.mult))

    def g_l2(self, t):
        S = self.S
        s = t % 2
        XB = self.XT[s]
        xt = self.xt_t[s]
        xaps = [xt.t[:, b, :] for b in range(4)]
        S.dma([self.SPX[t]], XB, [lambda e: e.dma_start(out=xt[:, :, :], in_=self.SPX[t].t[:, :, :])], xt)
        S.dma([self.SPH[t]], self.hg, [lambda e, c=c: e.dma_start(out=self.hg[c][:, :], in_=self.SPH[t].t[:, c, :]) for c in range(4)], self.hg[0])
        S.dma([self.SPA[t]], self.ag, [lambda e, c=c: e.dma_start(out=self.ag_ap[c], in_=self.SPA[t].t[:, c, :]) for c in range(4)], self.ag[0])
        S.dma([self.SPO[t]], self.ONT, [lambda e: e.dma_start(out=self.onT_t[:, :, :], in_=self.SPO[t].t[:, :, :])], self.onT_t)
        yield
        for c in range(4):
            S.op("dve", [self.ag[c], self.hg[c], self.hmid], [self.hg[c]], lambda e, c=c: e.scalar_tensor_tensor(
                out=self.hg[c][:, :], in0=self.ag_ap[c], scalar=self.hmid[:, c:c + 1], in1=self.hg[c][:, :], op0=ALU.mult, op1=ALU.add))
        self.lru_out_norm(512)
        yield
        yield from self.g_out_proj(XB, xaps, 128, 4)

    def g_out_proj(self, XB, xaps, P, nb):
        S = self.S
        for n in range(2):
            slot, w, widx = self.wget("w_out%d" % n)
            for b in range(nb):
                ps = self.psr.next()

                def mm(e, ps=ps, b=b, w=w):
                    for kc in range(8):
                        lhsT = self.lnT[:, kc, b * P:(b + 1) * P] if kc < 4 else self.onT_t[:, kc - 4, b * P:(b + 1) * P]
                        ins = e.matmul(ps[0:P, :], lhsT=lhsT, rhs=w[:, kc, :], start=(kc == 0), stop=(kc == 7))
                    return ins
                S.op("pe", [slot, self.lnT, self.ONT[b]], [ps], mm)
                xap = xaps[b][:, n * 512:(n + 1) * 512]
                S.op("dve", [ps, XB[b]], [XB[b]], lambda e, ps=ps, xap=xap: e.tensor_tensor(out=xap, in0=ps[0:P, :], in1=xap, op=ALU.add))
            self.wdone(widx)
            yield

    def _mm8(self, e, out, lhs_fn, rhs_fn):
        for kc in range(8):
            ins = e.matmul(out, lhsT=lhs_fn(kc), rhs=rhs_fn(kc), start=(kc == 0), stop=(kc == 7))
        return ins

    def prompt_state_out(self):
        S = self.S
        O = self.O
        ident = self.ident
        hf = self.small.next()
        S.op("dve", self.AC + [self.hmid], [hf], lambda e: e.tensor_tensor(out=hf[:, 0:4], in0=self.ac[:, 0:4], in1=self.hmid[:, 0:4], op=ALU.mult))
        S.op("dve", self.HC + [hf], [hf], lambda e: e.tensor_tensor(out=hf[:, 0:4], in0=hf[:, 0:4], in1=self.hc[:, 0:4], op=ALU.add))
        ps = self.psr.next()
        S.op("pe", [hf, self.cst], [ps], lambda e: e.transpose(out=ps[0:4, 0:128], in_=hf[:, 0:4], identity=ident))
        st = self.tmp.next()
        S.op("dve", [ps], [st], lambda e: e.tensor_copy(out=st[0:4, 0:128], in_=ps[0:4, 0:128]))
        S.dma([st], [O["p_h"]], [lambda e: e.dma_start(out=O["p_h"].t[:, :], in_=st[0:4, 0:128])], st)
        ps2 = self.psr.next()

        def trc(e):
            for c in range(4):
                ins = e.transpose(out=ps2[0:3, c * 128:(c + 1) * 128], in_=self.carry[c][:, 0:3], identity=ident)
            return ins
        S.op("pe", self.carry + [self.cst], [ps2], trc)
        st2 = self.tmp.next()
        S.op("dve", [ps2], [st2], lambda e: e.tensor_copy(out=st2[0:3, :], in_=ps2[0:3, :]))
        S.dma([st2], [O["p_conv"]], [lambda e: e.dma_start(out=O["p_conv"].t[:, :], in_=st2[0:3, :])], st2)
        ps3 = self.psr.next()
        S.op("pe", [self.kr, self.cst], [ps3], lambda e: e.transpose(out=ps3[:, 0:128], in_=self.kr[:, 384:512], identity=ident))
        st3 = self.tmp.next()
        S.op("dve", [ps3], [st3], lambda e: e.tensor_copy(out=st3[:, 0:128], in_=ps3[:, 0:128]))
        S.dma([st3], [O["p_k"]], [lambda e: e.dma_start(out=O["p_k"].t[:, :], in_=st3[:, 0:128])], st3)
        S.dma([self.vf], [O["p_v"]], [lambda e: e.dma_start(out=O["p_v"].t[:, :], in_=self.vf[:, :])], self.vf)

    def sample_stage(self):
        S = self.S
        I, O = self.I, self.O
        P = NSEQ * TS
        ident = self.ident
        stg = self.xt_t[0]
        SB = self.XT[0]
        ckv = stg.t[:, 0:2, :].rearrange("p a (s d) -> p (a s) d", d=128)
        cvv = stg.t[:, 2:4, :].rearrange("p a (s d) -> p (a s) d", d=128)
        S.dma([I["ck"]], SB[0:2], [lambda e: e.dma_start(out=ckv, in_=I["ck"].t.rearrange("s k d -> k s d"))], SB[0])
        S.dma([I["cv"]], SB[2:4], [lambda e: e.dma_start(out=cvv, in_=I["cv"].t.rearrange("s k d -> k s d"))], SB[2])
        for q4 in range(4):
            ps = self.psr.next()

            def trk(e, ps=ps, q4=q4):
                for i in range(4):
                    ins = e.transpose(out=ps[:, i * 128:(i + 1) * 128], in_=ckv[:, q4 * 4 + i, :], identity=ident)
                return ins
            S.op("pe", SB[0:2] + [self.cst], [ps], trk)
            S.op("act", [ps], [self.kTc], lambda e, ps=ps, q4=q4: e.activation(
                out=self.kTc[:, q4 * 4:(q4 + 1) * 4, :], in_=ps[:, :].rearrange("p (s k) -> p s k", s=4), func=AF.Copy))
        S.op("dve", SB[2:4], [self.vc], lambda e: e.tensor_copy(out=self.vc[:, :, :, 0:64], in_=cvv.rearrange("p s (g d) -> p s g d", g=2)))
        S.op("pool", [], [self.vc], lambda e: e.memset(self.vc[:, :, :, 64:65], 1.0))
        S.dma([I["ck"]], [O["s_k"]], [lambda e: e.dma_start(out=O["s_k"].t[:, 0:WIN - TS, :], in_=I["ck"].t[:, TS:WIN, :])], T(None, "ckc"))
        S.dma([I["cv"]], [O["s_v"]], [lambda e: e.dma_start(out=O["s_v"].t[:, 0:WIN - TS, :], in_=I["cv"].t[:, TS:WIN, :])], T(None, "cvc"))
        st1 = self.tmp.next()
        st2 = self.tmp.next()
        S.dma([I["st_conv"]], [st1], [lambda e: e.dma_start(out=st1[0:48, :], in_=I["st_conv"].t[:, :])], st1)
        S.dma([I["st_h"]], [st2], [lambda e: e.dma_start(out=st2[0:16, :], in_=I["st_h"].t[:, :])], st2)
        ps = self.psr.next()

        def trs(e):
            for c in range(4):
                ins = e.transpose(out=ps[:, c * 64:c * 64 + 48], in_=st1[0:48, c * 128:(c + 1) * 128], identity=self.cst.t[0:48, 0, 0:48])
            for c in range(4):
                ins = e.transpose(out=ps[:, 256 + c * 16:256 + (c + 1) * 16], in_=st2[0:16, c * 128:(c + 1) * 128], identity=self.cst.t[0:16, 0, 0:16])
            return ins
        S.op("pe", [st1, st2, self.cst], [ps], trs)
        S.op("dve", [ps], [self.stcT], lambda e: e.tensor_copy(
            out=self.stcT[:, :, :, :], in_=ps[:, 0:256].rearrange("p (c x) -> p c x", c=4)[:, :, 0:48].rearrange("p c (s j) -> p c s j", j=3)))
        S.op("dve", [ps], [self.h0T], lambda e: e.tensor_copy(out=self.h0T[:, :, :], in_=ps[:, 256:320].rearrange("p (c s) -> p c s", c=4)))

    def g_sample_ffn1(self):
        S = self.S
        I = self.I
        P = NSEQ * TS
        xt = self.xt_t[0]
        XB = [self.XT[0][0]]
        xaps = [xt.t[0:P, 0, :]]
        S.dma([I["xs"]], XB, [lambda e: e.dma_start(out=xt[0:P, 0, :], in_=I["xs"].t[:, :])], xt)
        yield from self.g_ffn(1, XB, xaps, P, 0, fine=True)

    def sample_part1(self, XB, xaps):
        S = self.S
        I, O = self.I, self.O
        P = NSEQ * TS
        ident = self.ident
        self.norm_T(XB, xaps, P, 1)
        slot0, w0, wi0 = self.wget("w_in0")
        slot1, w1, wi1 = self.wget("w_in1")
        cos = self.rope_s.t[:, 0, :]
        sin = self.rope_s.t[:, 1, :]
        xls = self.xl.next()
        xlv = xls.t[:, 0:448].rearrange("p (c s j) -> p c s j", c=4, s=NSEQ)
        hst = self.hst
        for c in range(4):
            ps = self.psr.next()
            S.op("pe", [slot0, self.XNT[0]], [ps], lambda e, ps=ps, c=c: self._mm8(e, ps[:, 0:P], lambda kc: w0[:, kc, c * 128:(c + 1) * 128], lambda kc: self.xnT_t[:, kc, 0:P]))
            S.op("dve", [self.stcT], [xls], lambda e, c=c: e.tensor_copy(out=xlv[:, c, :, 0:3], in_=self.stcT[:, c, :, :]))
            S.op("act", [ps], [xls], lambda e, ps=ps, c=c: e.activation(out=xlv[:, c, :, 3:7], in_=ps[:, 0:P].rearrange("p (s t) -> p s t", t=TS), func=AF.Copy))
            xc = self.tmp.next()
            xc3 = xc.t[:, 0:P].rearrange("p (s t) -> p s t", t=TS)
            self.conv4(c, xls, lambda j, c=c: xlv[:, c, :, j:j + TS], xc, xc3)
            self._xcT = xc
            a, u = self.lru_gates(c, xc[:, 0:P], P)
            a3 = a.t[:, 0:P].rearrange("p (s t) -> p s t", t=TS)
            u3 = u.t[:, 0:P].rearrange("p (s t) -> p s t", t=TS)
            tm = self.small.next()
            S.op("dve", [a, self.h0T], [tm], lambda e, a3=a3, tm=tm, c=c: e.tensor_tensor(out=tm[:, 0:NSEQ], in0=a3[:, :, 0], in1=self.h0T[:, c, :], op=ALU.mult))
            S.op("dve", [u, tm], [u], lambda e, u3=u3, tm=tm: e.tensor_tensor(out=u3[:, :, 0], in0=u3[:, :, 0], in1=tm[:, 0:NSEQ], op=ALU.add))
            S.op("dve", [a], [a], lambda e, a3=a3: e.memset(a3[:, :, 0], 0.0))
            h = self.tmp.next()
            S.op("dve", [a, u], [h], lambda e, a=a, u=u, h=h: e.tensor_tensor_scan(out=h[:, 0:P], data0=a[:, 0:P], data1=u[:, 0:P], initial=0.0, op0=ALU.mult, op1=ALU.add))
            S.op("pool", [h], [hst], lambda e, h=h, c=c: e.tensor_copy(
                out=hst[:, c * NSEQ:(c + 1) * NSEQ], in_=h.t[:, 0:P].rearrange("p (s t) -> p s t", t=TS)[:, :, TS - 1]))
            psg = self.psr.next()
            S.op("pe", [slot1, self.XNT[0]], [psg], lambda e, psg=psg, c=c: self._mm8(e, psg[:, 0:P], lambda kc: w1[:, kc, c * 128:(c + 1) * 128], lambda kc: self.xnT_t[:, kc, 0:P]))
            gg = self.tmp.next()
            S.op("act", [psg], [gg], lambda e, gg=gg, psg=psg: e.activation(out=gg[:, 0:P], in_=psg[:, 0:P], func=AF.Gelu_apprx_tanh))
            S.op("dve", [h, gg], [self.hg[c]], lambda e, h=h, gg=gg, c=c: e.tensor_tensor(out=self.hg[c][:, 0:P], in0=h[:, 0:P], in1=gg[:, 0:P], op=ALU.mult))
        self.lru_out_norm(P)
        self.wdone(wi0)
        self.wdone(wi1)
        cvst = self.tmp.next()
        S.op("pool", [xls], [cvst], lambda e: e.tensor_copy(out=cvst[:, 0:192].rearrange("p (c s j) -> p c s j", c=4, s=NSEQ), in_=xlv[:, :, :, 4:7]))
        ps = self.psr.next()
        ps2 = self.psr.next()

        def tro(e):
            for c in range(4):
                ins = e.transpose(out=ps[0:NSEQ, c * 128:(c + 1) * 128], in_=hst[:, c * NSEQ:(c + 1) * NSEQ], identity=ident)
            for c in range(4):
                ins = e.transpose(out=ps2[0:48, c * 128:(c + 1) * 128], in_=cvst[:, c * 48:(c + 1) * 48], identity=ident)
            return ins
        S.op("pe", [hst, cvst, self.cst], [ps, ps2], tro)
        o1 = self.tmp.next()
        o2 = self.tmp.next()
        S.op("dve", [ps], [o1], lambda e: e.tensor_copy(out=o1[0:NSEQ, :], in_=ps[0:NSEQ, :]))
        S.op("dve", [ps2], [o2], lambda e: e.tensor_copy(out=o2[0:48, :], in_=ps2[0:48, :]))
        S.dma([o1], [O["s_h"]], [lambda e: e.dma_start(out=O["s_h"].t[:, :], in_=o1[0:NSEQ, :])], o1)
        S.dma([o2], [O["s_conv"]], [lambda e: e.dma_start(out=O["s_conv"].t[:, :], in_=o2[0:48, :])], o2)
        slot2, w2, wi2 = self.wget("w_in2")
        for c in range(4):
            ps = self.psr.next()
            S.op("pe", [slot2, self.XNT[0]], [ps], lambda e, ps=ps, c=c: self._mm8(
                e, ps[:, 0:P], lambda kc: w2[:, kc, c * 128:(c + 1) * 128], lambda kc: self.xnT_t[:, kc, 0:P]))
            self.qk_chunk(ps, P, 0, self.rope_s, cos, sin, [
                (self.QT[c], self.qsA[:, :, c, :], lambda ap: ap.rearrange("p (s t) -> p s t", t=TS)),
                (self.QT[c], self.qsB[:, c, :], lambda ap: ap)])
        self.wdone(wi2)
        slot3, w3, wi3 = self.wget("w_in3")
        ps = self.psr.next()
        S.op("pe", [slot3, self.XNT[0]], [ps], lambda e, ps=ps: self._mm8(e, ps[:, 0:P], lambda kc: w3[:, kc, 0:128], lambda kc: self.xnT_t[:, kc, 0:P]))
        self.qk_chunk(ps, P, 1, self.rope_s, cos, sin, [(self.kT, self.kT[:, 0:P], lambda ap: ap)], out_f32=(self.kr, self.kr[:, 0:P]))
        psv = self.psr.next()
        S.op("pe", [slot3, self.XNT[0]], [psv], lambda e: self._mm8(e, psv[0:P, 0:128], lambda kc: self.xnT_t[:, kc, 0:P], lambda kc: w3[:, kc, 128:256]))
        S.op("act", [psv], [self.vext], lambda e: e.activation(out=self.vext[0:P, 0, :, 0:64], in_=psv[0:P, 0:128].rearrange("p (g d) -> p g d", g=2), func=AF.Copy))
        self.wdone(wi3)
        S.op("pool", [], [self.vext], lambda e: e.memset(self.vext[:, :, :, 64:65], 1.0))
        vnew = self.tmp.next()
        S.op("act", [psv], [vnew], lambda e: e.activation(out=vnew[0:P, 0:128], in_=psv[0:P, 0:128], func=AF.Copy))
        S.dma([vnew], [O["s_v"]], [lambda e: e.dma_start(out=O["s_v"].t[:, WIN - TS:WIN, :], in_=vnew[0:P, 0:128])], vnew)
        psk = self.psr.next()
        S.op("pe", [self.kr, self.cst], [psk], lambda e: e.transpose(out=psk[0:P, 0:128], in_=self.kr[:, 0:P], identity=ident))
        knew = self.tmp.next()
        S.op("dve", [psk], [knew], lambda e: e.tensor_copy(out=knew[0:P, 0:128], in_=psk[0:P, 0:128]))
        S.dma([knew], [O["s_k"]], [lambda e: e.dma_start(out=O["s_k"].t[:, WIN - TS:WIN, :], in_=knew[0:P, 0:128])], knew)
        self.ck(3)
        pscg = [self.psf.next(), self.psf.next()]
        psng = [self.psf.next(), self.psf.next()]
        for g in range(2):
            def sc(e, g=g):
                for s in range(NSEQ):
                    ins = e.matmul(pscg[g][:, s * 16:(s + 1) * 16], lhsT=self.kTc[g * 64:(g + 1) * 64, s, :],
                                   rhs=self.qsA[g * 64:(g + 1) * 64, s, :, :].rearrange("p c t -> p (c t)"), start=True, stop=True)
                return ins
            S.op("pe", [self.kTc] + self.QT, [pscg[g]], sc)
            S.op("pe", [self.kT] + self.QT, [psng[g]], lambda e, g=g: e.matmul(
                psng[g][0:P, 0:256], lhsT=self.kT[g * 64:(g + 1) * 64, 0:P],
                rhs=self.qsB[g * 64:(g + 1) * 64, :, :].rearrange("p c q -> p (c q)"), start=True, stop=True))
        self.ck(3.1)
        ec = self.ep.next()
        en = self.ep.next()
        for g in range(2):
            S.op("act", [pscg[g]], [ec], lambda e, g=g: e.activation(out=ec[:, g * 256:(g + 1) * 256], in_=pscg[g][:, 0:256], func=AF.Exp, scale=HD ** -0.5))
            S.op("act", [psng[g]], [en], lambda e, g=g: e.activation(out=en[0:P, g * 256:(g + 1) * 256], in_=psng[g][0:P, 0:256], func=AF.Exp, scale=HD ** -0.5))
        S.op("pool", [en, self.maskn], [en], lambda e: e.tensor_tensor(out=en[0:P, :], in0=en[0:P, :], in1=self.maskn[:, :, :].rearrange("p h q -> p (h q)"), op=ALU.mult))
        self.ck(3.2)
        aT = self.aT_t
        PAD = self.AT[0:16]
        pstride = aT.t[:, 0, 0:1].ap[0][0]
        S.op("pool", [], PAD, lambda e: e.memset(aT[:, 0:16, :], 0.0))
        for g in range(2):
            pad_out = bass.AP(aT.t, g * 256, [[pstride, 128], [516, NSEQ], [64, 4], [1, TS]])
            S.op("dve", [ec, self.maskc], PAD, lambda e, g=g, pad_out=pad_out: e.tensor_tensor(
                out=pad_out, in0=ec[:, g * 256:(g + 1) * 256].rearrange("p (s h t) -> p s h t", s=NSEQ, h=4),
                in1=self.maskc[:, :].unsqueeze(1).unsqueeze(1).broadcast_to([128, NSEQ, 4, TS]), op=ALU.mult))
        self.ck(3.3)
        psO = [self.psA, self.psB]
        for g in range(2):
            def pv(e, g=g):
                for h in range(4):
                    gh = g * 4 + h
                    for s in range(NSEQ):
                        ins = e.matmul(psO[g][0:P, h * 65:(h + 1) * 65], lhsT=aT[:, s, gh * 64:(gh + 1) * 64], rhs=self.vc[:, s, g, :],
                                       start=(s == 0), stop=False)
                    ins = e.matmul(psO[g][0:P, h * 65:(h + 1) * 65], lhsT=en[0:P, gh * 64:(gh + 1) * 64], rhs=self.vext[0:P, 0, g, :],
                                   start=False, stop=True)
                return ins
            S.op("pe", PAD + [self.vc, en, self.vext], [psO[g]], pv)
        self.ck(3.4)
        self.attn_finish(0, P, psO)
        self.ck(4)
        self.sp_xs = S.dram("sp_xs", [P, D], F32)
        self.sp_ons = S.dram("sp_ons", [128, 4, P], BF16)
        S.dma(XB, [self.sp_xs], [lambda e: e.dma_start(out=self.sp_xs.t[:, :], in_=xaps[0])], T(None, "spxs"))
        S.dma([self.ONT[0]], [self.sp_ons], [lambda e: e.dma_start(out=self.sp_ons.t[:, :, :], in_=self.onT_t[:, :, 0:P])], T(None, "spons"))

    def sample_part2(self):
        S = self.S
        O = self.O
        P = NSEQ * TS
        xt = self.xt_t[0]
        XB = [self.XT[0][0]]
        xaps = [xt.t[0:P, 0, :]]
        S.dma([self.sp_xs], XB, [lambda e: e.dma_start(out=xt[0:P, 0, :], in_=self.sp_xs.t[:, :])], xt)
        S.dma([self.sp_ons], [self.ONT[0]], [lambda e: e.dma_start(out=self.onT_t[:, :, 0:P], in_=self.sp_ons.t[:, :, :])], self.onT_t)
        self.out_proj(XB, xaps, P, 1)
        self.ffn(2, XB, xaps, P, 2)
        S.dma(XB, [O["ys"]], [lambda e: e.dma_start(out=O["ys"].t[:, :], in_=xt[0:P, 0, :])], T(None, "yss"))


def _static_consts():
    ident = np.eye(128, dtype=np.float32)
    blk = np.zeros((128, 128), np.float32)
    blk[:64, :64] = 1.0
    blk[64:, 64:] = 1.0
    rrot = np.zeros((128, 128), np.float32)
    for m in range(128):
        if m % 64 < 32:
            rrot[m + 32, m] = -1.0
        else:
            rrot[m - 32, m] = 1.0
    ones = np.ones((128, 128), np.float32)
    cst = np.concatenate([ident, blk, rrot, ones], axis=1)
    j = np.arange(128)[:, None]
    i = np.arange(128)[None, :]
    mask_p = np.stack([(j > i), (j <= i)], axis=1).astype(np.float32)
    mc = (np.arange(128)[:, None] >= (np.arange(TS)[None, :] + 1)).astype(np.float32)
    kk = np.arange(64)
    mn = ((kk[:, None] // TS == kk[None, :] // TS) & (kk[:, None] % TS <= kk[None, :] % TS)).astype(np.float32)
    mn = np.concatenate([mn, np.zeros((64, 64), np.float32)], axis=0)
    mask_s = np.concatenate([mc, mn], axis=1)
    return dict(cst=cst, mask_p=np.ascontiguousarray(mask_p), mask_s=np.ascontiguousarray(mask_s))


def _rope_tab(pos):
    half = HD // 2
    inv = (np.float32(THETA) ** (-np.arange(half, dtype=np.float32) / np.float32(half))).astype(np.float32)
    ang = np.asarray(pos).astype(np.float32)[None, :] * inv[:, None]
    c = np.tile(np.cos(ang).astype(np.float32), (4, 1))
    s = np.tile(np.sin(ang).astype(np.float32), (4, 1))
    return np.ascontiguousarray(np.stack([c, s], axis=1))


_BUILD_CACHE = {}
_HOOK = None


def kernel(x_prompt, x_sample, state_lru_h, state_conv, cache_k, cache_v,
           ffn1_norm, ffn1_w_gate, ffn1_w_up, ffn1_w_down,
           mix_norm, w_in, conv_w, conv_b, lru_wa, lru_ba, lru_wx, lru_bx, lru_lambda,
           q_norm, k_norm, attn_sinks, lru_out_norm, attn_out_norm, w_out,
           ffn2_norm, ffn2_w_gate, ffn2_w_up, ffn2_w_down):
    f = lambda a: np.ascontiguousarray(np.asarray(a, dtype=np.float32))
    x_prompt = f(x_prompt)
    B, SEQ, _ = x_prompt.shape
    assert B * 2 == NCORES
    HALF = SEQ // 2
    NT = HALF // 512
    assert NT % 2 == 0 and NT >= 2
    DB = x_sample.shape[0]
    assert DB == NSEQ * NCORES
    if NT not in _BUILD_CACHE:
        b0 = Builder(NT)
        b0.build()
        _BUILD_CACHE[NT] = Builder(NT, wseq=b0.wrec).build()
    nc = _BUILD_CACHE[NT]
    shared = dict(
        f1_g=f(ffn1_w_gate[0]), f1_u=f(ffn1_w_up[0]), f1_d=f(ffn1_w_down[0]),
        f2_g=f(ffn2_w_gate[0]), f2_u=f(ffn2_w_up[0]), f2_d=f(ffn2_w_down[0]),
        w_in=f(w_in[0]), w_out=f(w_out[0]),
        gains=f(np.stack([ffn1_norm[0], mix_norm[0], ffn2_norm[0]], axis=0)),
        chv=f(np.concatenate([conv_w[0], conv_b, lru_ba, lru_bx, lru_lambda, lru_out_norm, attn_out_norm], axis=0)),
        qkn=f(np.stack([np.tile(q_norm[0], 2), np.tile(k_norm[0], 2)], axis=0)),
        sinks=f(attn_sinks).reshape(NH), wa=f(lru_wa[0]), wx=f(lru_wx[0]),
    )
    shared.update(_static_consts())
    shared["rope_s"] = _rope_tab(PAST_LEN + (np.arange(NSEQ * TS) % TS))
    xs = f(x_sample)
    ropes = [_rope_tab(h * HALF + np.arange(HALF)) for h in range(2)]
    rope_h = [_rope_tab(np.maximum(h * HALF - 128 + np.arange(128), 0)) for h in range(2)]
    in_maps = []
    for c in range(NCORES):
        m = dict(shared)
        p, h = c // 2, c % 2
        m["xp"] = x_prompt[p, h * HALF:(h + 1) * HALF]
        m["xh"] = x_prompt[p, HALF - 128:HALF] if h == 1 else np.zeros((128, D), np.float32)
        m["flag"] = np.full((1,), float(h), np.float32)
        m["rope_p"] = ropes[h]
        m["rope_h"] = rope_h[h]
        sl = slice(c * NSEQ, (c + 1) * NSEQ)
        m["xs"] = xs[sl].reshape(NSEQ * TS, D)
        m["st_h"] = f(state_lru_h[0, sl])
        m["st_conv"] = f(state_conv[0, sl]).reshape(NSEQ * 3, LRU_W)
        m["ck"] = f(cache_k[0, sl]).reshape(NSEQ, WIN, 128)
        m["cv"] = f(cache_v[0, sl]).reshape(NSEQ, WIN, 128)
        in_maps.append(m)
    if _HOOK is not None:
        R = _HOOK(nc, in_maps)
    else:
        res = run_bass_kernel_spmd(nc, in_maps, core_ids=list(range(NCORES)))
        R = res.results
    yp = np.stack([np.concatenate([R[2 * b]["yp"], R[2 * b + 1]["yp"]], axis=0) for b in range(B)], axis=0)
    ys = np.concatenate([R[c]["ys"].reshape(NSEQ, TS, D) for c in range(NCORES)], axis=0)
    p_h = np.stack([R[2 * b + 1]["p_h"].reshape(LRU_W) for b in range(B)], axis=0)[None]
    p_conv = np.stack([R[2 * b + 1]["p_conv"] for b in range(B)], axis=0)[None]
    p_k = np.stack([R[2 * b + 1]["p_k"].reshape(WIN, 2, HD) for b in range(B)], axis=0)[None]
    p_v = np.stack([R[2 * b + 1]["p_v"].reshape(WIN, 2, HD) for b in range(B)], axis=0)[None]
    s_h = np.concatenate([R[c]["s_h"] for c in range(NCORES)], axis=0)[None]
    s_conv = np.concatenate([R[c]["s_conv"].reshape(NSEQ, 3, LRU_W) for c in range(NCORES)], axis=0)[None]
    s_k = np.concatenate([R[c]["s_k"].reshape(NSEQ, WIN, 2, HD) for c in range(NCORES)], axis=0)[None]
    s_v = np.concatenate([R[c]["s_v"].reshape(NSEQ, WIN, 2, HD) for c in range(NCORES)], axis=0)[None]
    return (yp, ys, p_h, p_conv, p_k, p_v, s_h, s_conv, s_k, s_v)
```
